# Optimizing a Trainium2 kernel written in Bass

```python
import math
import jax, jax.numpy as jnp
from jax import lax
import numpy as np

D_MODEL = 1024
BATCH = 2
SEQ = 8192
DEPTH = 2
DEC_BATCH = 128
DEC_SEQ = 8
PAST_LEN = 2048
PAGE_SIZE = 128

N_MIXERS = 4
GROUP_W = D_MODEL // N_MIXERS
A_HEADS = 4
A_HALF = GROUP_W // (2 * A_HEADS)
A_VDIM = 2 * A_HALF
ROPE_THETA = 10000.0
Q_BLOCK = 128
NEG_BIG = -1e30
B_GROUPS = 4
B_GDIM = GROUP_W // B_GROUPS
SG_CHUNK = 128
CONV_W = 31
D_HEADS = 4
D_KDIM = GROUP_W // D_HEADS
D_VDIM = GROUP_W // D_HEADS
HGRN_CHUNK = 16
F_FLOOR = 1e-30
ALPHA = (2.0 * DEPTH) ** 0.25
BETA = (8.0 * DEPTH) ** -0.25
EPS = 1e-5
IN_SIZES = (GROUP_W, GROUP_W, GROUP_W, GROUP_W,
            GROUP_W, GROUP_W, GROUP_W,
            2 * GROUP_W, GROUP_W,
            GROUP_W, GROUP_W, GROUP_W, GROUP_W)
D_IN = sum(IN_SIZES)
SPLITS = [int(s) for s in np.cumsum(IN_SIZES)[:-1]]

kernel_name = 'hymba_diffattn_gmlp_conformer_hgrn2_step'

F32 = jnp.float32


def _layer_norm(x, g, b):
    xf = x.astype(F32)
    xc = xf - xf.mean(-1, keepdims=True)
    var = (xc * xc).mean(-1, keepdims=True)
    return xc * lax.rsqrt(var + EPS) * g.astype(F32) + b.astype(F32)


def _rms_norm(x, g):
    xf = x.astype(F32)
    return xf * lax.rsqrt((xf * xf).mean(-1, keepdims=True) + EPS) * g.astype(F32)


def _rope(x, pos):
    d = x.shape[-1]
    half = d // 2
    inv = ROPE_THETA ** (-jnp.arange(half, dtype=F32) * 2.0 / d)
    ang = pos.astype(F32)[:, None] * inv[None, :]
    cos = jnp.cos(ang)[:, None, :]
    sin = jnp.sin(ang)[:, None, :]
    xf = x.astype(F32)
    x1, x2 = xf[..., :half], xf[..., half:]
    return jnp.concatenate([x1 * cos - x2 * sin, x2 * cos + x1 * sin], -1)


def _diff_attend(q1, q2, k1, k2, v, qpos, kpos, lam):
    scale = q1.shape[-1] ** -0.5
    visible = kpos[None, :] <= qpos[:, None]

    def probs(q, k):
        s = jnp.einsum('bqhd,bkhd->bhqk', q.astype(F32), k.astype(F32)) * scale
        return jax.nn.softmax(jnp.where(visible, s, NEG_BIG), axis=-1)

    p = probs(q1, k1) - lam * probs(q2, k2)
    return jnp.einsum('bhqk,bkhd->bqhd', p, v.astype(F32))


def _diff_attn_blocks(q1, q2, k1, k2, v, lam):
    B, T, H, _ = q1.shape
    nb = T // Q_BLOCK
    kpos = jnp.arange(T)

    def to_blocks(a):
        return a.reshape(B, nb, Q_BLOCK, H, a.shape[-1]).swapaxes(0, 1)

    def one(args):
        qa, qb, qp = args
        return _diff_attend(qa, qb, k1, k2, v, qp, kpos, lam)

    out = lax.map(one, (to_blocks(q1), to_blocks(q2), kpos.reshape(nb, Q_BLOCK)))
    return out.swapaxes(0, 1).reshape(B, T, H, v.shape[-1])


def _spatial_gate(u, v, ws, bs, g, b):
    Bn, T, W = v.shape
    vn = _layer_norm(v, g, b)
    Tp = -(-T // SG_CHUNK) * SG_CHUNK
    vc = jnp.pad(vn, ((0, 0), (0, Tp - T), (0, 0))).reshape(Bn, Tp // SG_CHUNK, SG_CHUNK, B_GROUPS, B_GDIM)
    wm = jnp.tril(ws.astype(F32))
    mixed = jnp.einsum('gts,bcsgd->bctgd', wm, vc) + bs.astype(F32).T[None, None, :, :, None]
    mixed = mixed.reshape(Bn, Tp, W)[:, :T]
    return u.astype(F32) * mixed, vn


def _conformer_conv(a, buf, cw, cb, ng, nb, wpw):
    glu = a[..., :GROUP_W] * jax.nn.sigmoid(a[..., GROUP_W:])
    hc = jnp.concatenate([buf.astype(glu.dtype), glu], 1)
    y = lax.conv_general_dilated(hc, cw[:, None, :].astype(hc.dtype), window_strides=(1,), padding='VALID',
                                 dimension_numbers=('NWC', 'WIO', 'NWC'), feature_group_count=GROUP_W)
    y = jax.nn.silu(_layer_norm(y + cb, ng, nb))
    return jnp.einsum('btc,cd->btd', y, wpw), hc[:, -(CONV_W - 1):]


def _hgrn2(q, log_f, k, v, S0):
    Bn, T, H, _ = q.shape
    dv = v.shape[-1]
    C = HGRN_CHUNK
    Tp = -(-T // C) * C

    def chunks(a):
        a = jnp.pad(a, ((0, 0), (0, Tp - T), (0, 0), (0, 0)))
        return a.reshape(Bn, Tp // C, C, H, a.shape[-1]).transpose(1, 0, 3, 2, 4)

    causal = jnp.tril(jnp.ones((C, C), bool))[:, :, None]

    def step(S, inp):
        qc, lfc, kc, vc = inp
        cum = jnp.cumsum(lfc, axis=2)
        o_inter = jnp.einsum('bhtk,bhkv->bhtv', qc * jnp.exp(cum), S)
        diff = jnp.where(causal, cum[:, :, :, None, :] - cum[:, :, None, :, :], 0.0)
        decay = jnp.where(causal, jnp.exp(diff), 0.0)
        att = jnp.einsum('bhtk,bhtsk,bhsk->bhts', qc, decay, kc)
        o = o_inter + jnp.einsum('bhts,bhsv->bhtv', att, vc)
        last = cum[:, :, -1:, :]
        S_new = jnp.exp(last[:, :, 0, :])[..., None] * S + jnp.einsum('bhsk,bhsv->bhkv', kc * jnp.exp(last - cum), vc)
        return S_new, o

    S_fin, o = lax.scan(step, S0, (chunks(q), chunks(log_f), chunks(k), chunks(v)))
    o = o.transpose(1, 0, 3, 2, 4).reshape(Bn, Tp, H, dv)[:, :T]
    return o, S_fin


def _mixer_layer(l, x, c, pos, kv_past, buf, S0, lb, wts):
    (w_ada, b_ada, w_in, lam_qk, attn_norm_g, sg_norm_g, sg_norm_b, w_s, b_s, conv_w, conv_b,
     conv_norm_g, conv_norm_b, w_pw, hgrn_norm_g, w_out, ln_g, ln_b) = wts
    dt = x.dtype
    Bn, T, _ = x.shape
    mod = jax.nn.silu(c) @ w_ada[l] + b_ada[l]
    shift, scale, gate = jnp.split(mod, 3, axis=-1)
    h = x * (1 + scale[:, None, :]) + shift[:, None, :]
    z = h @ w_in[l]
    aq, ak, av, ag, bu, bv, bg, cin, cg, dq, df, di, dg = jnp.split(z, SPLITS, axis=-1)

    aq = aq.reshape(Bn, T, A_HEADS, 2 * A_HALF)
    ak = ak.reshape(Bn, T, A_HEADS, 2 * A_HALF)
    q1 = _rope(aq[..., :A_HALF], pos)
    q2 = _rope(aq[..., A_HALF:], pos)
    k_rows = jnp.concatenate([_rope(ak[..., :A_HALF], pos), _rope(ak[..., A_HALF:], pos)], -1).astype(dt)
    v_rows = av.reshape(Bn, T, A_HEADS, A_VDIM)
    lp = lam_qk[l].astype(F32)
    lam_init = 0.8 - 0.6 * math.exp(-0.3 * l)
    lam = jnp.exp(jnp.sum(lp[0] * lp[1])) - jnp.exp(jnp.sum(lp[2] * lp[3])) + lam_init
    if kv_past is None:
        o_a = _diff_attn_blocks(q1, q2, k_rows[..., :A_HALF], k_rows[..., A_HALF:], v_rows, lam)
    else:
        k_all = jnp.concatenate([kv_past[0].astype(dt), k_rows], 1)
        v_all = jnp.concatenate([kv_past[1].astype(dt), v_rows], 1)
        kpos = jnp.arange(k_all.shape[1])
        o_a = _diff_attend(q1, q2, k_all[..., :A_HALF], k_all[..., A_HALF:], v_all, pos, kpos, lam)
    o_a = (_rms_norm(o_a, attn_norm_g[l]) * (1.0 - lam_init)).reshape(Bn, T, GROUP_W)
    o_a = o_a * jax.nn.silu(ag.astype(F32))

    o_b, v_state = _spatial_gate(jax.nn.gelu(bu, approximate=False), jax.nn.gelu(bv, approximate=False),
                                 w_s[l], b_s[l], sg_norm_g[l], sg_norm_b[l])
    o_b = o_b * jax.nn.silu(bg.astype(F32))

    o_c, new_buf = _conformer_conv(cin, buf, conv_w[l], conv_b[l], conv_norm_g[l], conv_norm_b[l], w_pw[l])
    o_c = o_c * jax.nn.silu(cg.astype(F32))

    dff = df.astype(F32)
    f = lb + (1.0 - lb) * jax.nn.sigmoid(dff)
    log_f = jnp.log(jnp.maximum(f, F_FLOOR)).reshape(Bn, T, D_HEADS, D_KDIM)
    qd = jax.nn.silu(dq.astype(F32)).reshape(Bn, T, D_HEADS, D_KDIM)
    kd = (1.0 - f).reshape(Bn, T, D_HEADS, D_KDIM)
    vd = di.astype(F32).reshape(Bn, T, D_HEADS, D_VDIM)
    o_d, S_new = _hgrn2(qd, log_f, kd, vd, S0.astype(F32))
    o_d = _rms_norm(o_d, hgrn_norm_g[l]).reshape(Bn, T, GROUP_W) * jax.nn.silu(dg.astype(F32))

    mixed = jnp.concatenate([o_a, o_b, o_c, o_d], -1).astype(dt) @ w_out[l]
    y = _layer_norm(ALPHA * x + gate[:, None, :] * mixed, ln_g[l], ln_b[l]).astype(dt)
    return y, k_rows, v_rows.astype(dt), v_state.astype(dt), new_buf.astype(dt), S_new.astype(dt)


def setup_inputs(seed: int = 0) -> dict:
    key = jax.random.key(seed)
    ks = jax.random.split(key, 28)
    n_pages = PAST_LEN // PAGE_SIZE
    n_used = DEC_BATCH * n_pages
    n_pool = n_used + max(1, n_used // 4)

    def nrm(k, shape, s):
        return s * jax.random.normal(k, shape, F32)

    page_table = jax.random.permutation(ks[0], n_pool)[:n_used].reshape(DEC_BATCH, n_pages).astype(jnp.int32)
    return {
        'x_prompt': nrm(ks[1], (BATCH, SEQ, D_MODEL), 1.0),
        'x_sample': nrm(ks[2], (DEC_BATCH, DEC_SEQ, D_MODEL), 1.0),
        'cache_k': nrm(ks[3], (DEPTH, n_pool, PAGE_SIZE, A_HEADS, 2 * A_HALF), 1.0),
        'cache_v': nrm(ks[4], (DEPTH, n_pool, PAGE_SIZE, A_HEADS, A_VDIM), 1.0),
        'state_conv': nrm(ks[5], (DEPTH, DEC_BATCH, CONV_W - 1, GROUP_W), 0.5),
        'state_hgrn': nrm(ks[6], (DEPTH, DEC_BATCH, D_HEADS, D_KDIM, D_VDIM), 0.3),
        'page_table': page_table,
        'c_prompt': nrm(ks[7], (BATCH, D_MODEL), 1.0),
        'c_sample': nrm(ks[8], (DEC_BATCH, D_MODEL), 1.0),
        'w_ada': nrm(ks[9], (DEPTH, D_MODEL, 3 * D_MODEL), 0.5 * D_MODEL ** -0.5),
        'b_ada': nrm(ks[10], (DEPTH, 3 * D_MODEL), 0.01),
        'w_in': nrm(ks[11], (DEPTH, D_MODEL, D_IN), D_MODEL ** -0.5),
        'lam_qk': nrm(ks[12], (DEPTH, 4, A_HALF), 0.1),
        'attn_norm_g': 1.0 + nrm(ks[13], (DEPTH, A_VDIM), 0.02),
        'sg_norm_g': 1.0 + nrm(ks[14], (DEPTH, GROUP_W), 0.02),
        'sg_norm_b': nrm(ks[15], (DEPTH, GROUP_W), 0.01),
        'w_s': nrm(ks[16], (DEPTH, B_GROUPS, SG_CHUNK, SG_CHUNK), SG_CHUNK ** -0.5),
        'b_s': 1.0 + nrm(ks[17], (DEPTH, B_GROUPS, SG_CHUNK), 0.02),
        'conv_w': nrm(ks[18], (DEPTH, CONV_W, GROUP_W), CONV_W ** -0.5),
        'conv_b': nrm(ks[19], (DEPTH, GROUP_W), 0.01),
        'conv_norm_g': 1.0 + nrm(ks[20], (DEPTH, GROUP_W), 0.02),
        'conv_norm_b': nrm(ks[21], (DEPTH, GROUP_W), 0.01),
        'w_pw': nrm(ks[22], (DEPTH, GROUP_W, GROUP_W), GROUP_W ** -0.5),
        'lower_bounds': nrm(ks[23], (DEPTH, GROUP_W), 0.5),
        'hgrn_norm_g': 1.0 + nrm(ks[24], (DEPTH, D_VDIM), 0.02),
        'w_out': nrm(ks[25], (DEPTH, D_MODEL, D_MODEL), BETA * D_MODEL ** -0.5),
        'ln_g': 1.0 + nrm(ks[26], (DEPTH, D_MODEL), 0.02),
        'ln_b': nrm(ks[27], (DEPTH, D_MODEL), 0.01),
    }


def reference(x_prompt, x_sample, cache_k, cache_v, state_conv, state_hgrn, page_table, c_prompt, c_sample,
              w_ada, b_ada, w_in, lam_qk, attn_norm_g, sg_norm_g, sg_norm_b, w_s, b_s, conv_w, conv_b,
              conv_norm_g, conv_norm_b, w_pw, lower_bounds, hgrn_norm_g, w_out, ln_g, ln_b):
    wts = (w_ada, b_ada, w_in, lam_qk, attn_norm_g, sg_norm_g, sg_norm_b, w_s, b_s, conv_w, conv_b,
           conv_norm_g, conv_norm_b, w_pw, hgrn_norm_g, w_out, ln_g, ln_b)
    lb_soft = jax.nn.softmax(lower_bounds.astype(F32), axis=0)
    lbs = jnp.cumsum(lb_soft, axis=0) - lb_soft[0:1]

    Bp, T = x_prompt.shape[:2]
    Bs, Ts = x_sample.shape[:2]
    past_len = page_table.shape[1] * cache_k.shape[2]
    pos_p = jnp.arange(T)
    pos_s = past_len + jnp.arange(Ts)
    buf0 = jnp.zeros((Bp, CONV_W - 1, GROUP_W), x_prompt.dtype)
    S0 = jnp.zeros((Bp, D_HEADS, D_KDIM, D_VDIM), F32)

    xp, xs = x_prompt, x_sample
    kp_l, vp_l, ks_l, vs_l, ch_l, cbp_l, cbs_l, sp_l, ss_l = [], [], [], [], [], [], [], [], []
    for l in range(DEPTH):
        xp, kp, vp, _, cbp, Sp = _mixer_layer(l, xp, c_prompt, pos_p, None, buf0, S0, lbs[l], wts)
        k_past = cache_k[l][page_table].reshape(Bs, past_len, A_HEADS, 2 * A_HALF)
        v_past = cache_v[l][page_table].reshape(Bs, past_len, A_HEADS, A_VDIM)
        xs, kss, vss, chs, cbs, Ss = _mixer_layer(l, xs, c_sample, pos_s, (k_past, v_past),
                                                 state_conv[l], state_hgrn[l], lbs[l], wts)
        kp_l.append(kp); vp_l.append(vp); cbp_l.append(cbp); sp_l.append(Sp)
        ks_l.append(kss); vs_l.append(vss); ch_l.append(chs); cbs_l.append(cbs); ss_l.append(Ss)

    return (xp, xs, jnp.stack(kp_l), jnp.stack(vp_l), jnp.stack(ks_l), jnp.stack(vs_l), jnp.stack(ch_l),
            jnp.stack(cbp_l), jnp.stack(cbs_l), jnp.stack(sp_l), jnp.stack(ss_l))
```

```python
import numpy as np
import concourse.bass as bass
import concourse.mybir as mybir
from concourse.bass_utils import run_bass_kernel_spmd

F32 = mybir.dt.float32
BF16 = mybir.dt.bfloat16
I32 = mybir.dt.int32
AF = mybir.ActivationFunctionType
ALU = mybir.AluOpType
AX = mybir.AxisListType


PSUM_NAMES = {"zA", "zB", "tpb", "sc0", "sc1", "ot0", "ot1", "misc"}


class Res:
    __slots__ = ("name", "w", "r")

    def __init__(self, name):
        self.name = name
        self.w = None
        self.r = []


class Op:
    __slots__ = ("eng", "fn", "deps", "idx", "dma", "lane", "lane_cnt", "marked", "val", "waits", "lane_wait")

    def __init__(self, eng, fn, dma):
        self.eng = eng
        self.fn = fn
        self.dma = dma
        self.deps = []
        self.marked = False
        self.val = 0
        self.waits = []
        self.lane = None
        self.lane_cnt = 0
        self.lane_wait = 0


class Prog:
    ENGS = ("pe", "act", "dve", "pool", "sp")
    NLANES = {"sp": 8, "pool": 6, "act": 2}

    def __init__(self, nc):
        self.nc = nc
        self.ops = {e: [] for e in self.ENGS}
        self.lane_rr = {e: 0 for e in self.NLANES}
        self.lane_count = {}
        self.nres = 0

    def res(self, name=None):
        self.nres += 1
        return Res(name or f"r{self.nres}")

    def add(self, eng, fn, reads=(), writes=(), dma=False):
        op = Op(eng, fn, dma)
        deps = []
        xr = [r for r in reads if r.name in PSUM_NAMES]
        if xr:
            reads = [r for r in reads if r.name not in PSUM_NAMES]
            writes = list(writes) + [r for r in xr if r not in writes]
        for r in reads:
            if r.w is not None:
                deps.append(r.w)
        for w in writes:
            if w.w is not None:
                deps.append(w.w)
            deps.extend(w.r)
        for r in reads:
            r.r.append(op)
        for w in writes:
            w.w = op
            w.r = []
        seen = set()
        for d in deps:
            if d is op or id(d) in seen:
                continue
            seen.add(id(d))
            if (not d.dma) and (not dma) and d.eng == eng and eng == "pe":
                continue
            op.deps.append(d)
        op.idx = len(self.ops[eng])
        if dma:
            n = self.NLANES[eng]
            k = self.lane_rr[eng]
            self.lane_rr[eng] = (k + 1) % n
            op.lane = (eng, k)
            c = self.lane_count.get(op.lane, 0)
            op.lane_wait = c
            op.lane_cnt = c + 1
            self.lane_count[op.lane] = c + 1
        self.ops[eng].append(op)
        return op

    def pe(self, fn, reads=(), writes=()):
        return self.add("pe", fn, reads, writes)

    def act(self, fn, reads=(), writes=()):
        return self.add("act", fn, reads, writes)

    def dve(self, fn, reads=(), writes=()):
        return self.add("dve", fn, reads, writes)

    def pool(self, fn, reads=(), writes=()):
        return self.add("pool", fn, reads, writes)

    def dma(self, fn, reads=(), writes=(), q="sp"):
        return self.add(q, fn, reads, writes, dma=True)

    def finalize_and_emit(self):
        nc = self.nc
        for e in self.ENGS:
            waited_idx = {}
            waited_lane = {}
            for op in self.ops[e]:
                need_idx = {}
                need_lane = {}
                for d in op.deps:
                    if d.dma:
                        if need_lane.get(d.lane, 0) < d.lane_cnt:
                            need_lane[d.lane] = d.lane_cnt
                    else:
                        if need_idx.get(d.eng, -1) < d.idx:
                            need_idx[d.eng] = d.idx
                if op.dma and op.lane_wait > 0:
                    if need_lane.get(op.lane, 0) < op.lane_wait:
                        need_lane[op.lane] = op.lane_wait
                op.waits = []
                for te, ix in need_idx.items():
                    if waited_idx.get(te, -1) >= ix:
                        continue
                    if te == e and not op.dma and ix < op.idx and False:
                        continue
                    waited_idx[te] = ix
                    tgt = self.ops[te][ix]
                    tgt.marked = True
                    op.waits.append(("op", tgt))
                for ln, cnt in need_lane.items():
                    if waited_lane.get(ln, 0) >= cnt:
                        continue
                    waited_lane[ln] = cnt
                    op.waits.append(("lane", ln, cnt))
        for e in self.ENGS:
            v = 0
            for op in self.ops[e]:
                if op.dma:
                    continue
                if op.marked:
                    v += 1
                    op.val = v
        self.stats = {e: len(self.ops[e]) for e in self.ENGS}
        sems = {}
        import contextlib
        with contextlib.ExitStack() as st:
            for e in self.ENGS:
                sems[e] = st.enter_context(nc.semaphore(f"sem_{e}"))
            lanes = {}
            for q, n in self.NLANES.items():
                for k in range(n):
                    lanes[(q, k)] = st.enter_context(nc.semaphore(f"lane_{q}{k}"))
            block = st.enter_context(nc.Block())

            def emit(e, eng):
                for op in self.ops[e]:
                    for w in op.waits:
                        if w[0] == "op":
                            eng.wait_ge(sems[w[1].eng], w[1].val)
                        else:
                            eng.wait_ge(lanes[w[1]], 16 * w[2])
                    ins = op.fn(eng)
                    if op.dma:
                        ins.then_inc(lanes[op.lane], 16)
                    elif op.marked:
                        ins.then_inc(sems[e], 1)
                if e in self.NLANES:
                    for k in range(self.NLANES[e]):
                        c = self.lane_count.get((e, k), 0)
                        if c:
                            eng.wait_ge(lanes[(e, k)], 16 * c)

            @block.tensor
            def _(eng):
                emit("pe", eng)

            @block.scalar
            def _(eng):
                emit("act", eng)

            @block.vector
            def _(eng):
                emit("dve", eng)

            @block.gpsimd
            def _(eng):
                emit("pool", eng)

            @block.sync
            def _(eng):
                emit("sp", eng)


D = 1024
DIN = 3584
NT = 64
NT_RUN = 64
NPOOL_ROWS = 2560 * 128
POOL_ROWS_RUN = NPOOL_ROWS
NCORES_RUN = 8
ALPHA = (2.0 * 2) ** 0.25
EPS = 1e-5
SCALE = 32 ** -0.5
import math


def V(t, off, dims, p0=0, npart=128):
    ps = t[:].ap[0][0]
    return bass.AP(t, p0 * ps + off, [[ps, npart]] + [list(d) for d in dims])


def DV(t, off, dims):
    return bass.AP(t, off, [list(d) for d in dims])


class K:
    pass


class _Stop(Exception):
    pass


STOP_AT = None
DBG_MISC = False
ACT_PSUM = True
DBG_CAT = False
DBG_RES = None
DBG_SER = False
DBG_SER_LIST = ["kTt", "gate"]


def ck(tag):
    if STOP_AT is not None and tag == STOP_AT:
        raise _Stop()


def build_program():
    nc = bass.Bass("TRN2", target_bir_lowering=False)
    p = Prog(nc)
    k = K()
    R = {}

    def res(n):
        if n not in R:
            R[n] = p.res(n)
        return R[n]

    def din(name, shape, dt=F32):
        return nc.dram_tensor(name, list(shape), dt, kind="ExternalInput")

    def dout(name, shape, dt=F32):
        return nc.dram_tensor(name, list(shape), dt, kind="ExternalOutput")

    xp = din("xp", [8192, D]); xs = din("xs", [128, D])
    cp = din("cp", [128, D]); cs = din("cs", [128, D])
    pos_p = din("pos_p", [128, NT]); pos_s = din("pos_s", [128, 1]); invf = din("invf", [128, 16])
    ptab = din("ptab", [128, 256], I32); iota_p = din("iota_p", [128, 1])
    cache_kl = [din("cache_k0", [POOL_ROWS_RUN, 256]), din("cache_k1", [POOL_ROWS_RUN, 256])]; cache_vl = [din("cache_v0", [POOL_ROWS_RUN, 256]), din("cache_v1", [POOL_ROWS_RUN, 256])]
    st_conv = din("st_conv", [2, 16, 30, 256]); st_hgrn = din("st_hgrn", [2, 16, 4, 64, 64])
    w_ada = din("w_ada", [2, D, 3 * D]); b_ada = din("b_ada", [2, 3 * D]); w_in = din("w_in", [2, D, DIN])
    lam_qk = din("lam_qk", [2, 128]); attn_g = din("attn_g", [2, 64])
    sg_g = din("sg_g", [2, 256]); sg_b = din("sg_b", [2, 256]); w_s = din("w_s", [2, 4, 128, 128]); b_s = din("b_s", [2, 4, 128])
    conv_w = din("conv_w", [2, 31, 256]); conv_b = din("conv_b", [2, 256]); cn_g = din("cn_g", [2, 256]); cn_b = din("cn_b", [2, 256])
    w_pw = din("w_pw", [2, 256, 256]); lowb = din("lowb", [2, 256]); hg_g = din("hg_g", [2, 64])
    w_out = din("w_out", [2, D, D]); ln_g = din("ln_g", [2, D]); ln_b = din("ln_b", [2, D])
    c_ident = din("c_ident", [128, 128]); c_tril = din("c_tril", [128, 128]); c_ms = din("c_ms", [128, 128])
    c_mcum = din("c_mcum", [2, 128, 128]); c_mlast = din("c_mlast", [2, 128, 128]); c_matt = din("c_matt", [2, 128, 128])
    c_cm = din("c_cm", [2, 128, 8 * 128]); c_seg = din("c_seg", [2, 128, 16 * 64]); c_rowm = din("c_rowm", [128, 4])

    y_p = dout("y_p", [8192, D]); y_s = dout("y_s", [128, D])
    nk_p = dout("nk_p", [2, 8192, 256]); nv_p = dout("nv_p", [2, 8192, 256])
    nk_s = dout("nk_s", [2, 128, 256]); nv_s = dout("nv_s", [2, 128, 256]); nch_s = dout("nch_s", [2, 128, 256])
    ncv_p = dout("ncv_p", [2, 30, 256]); ncv_s = dout("ncv_s", [2, 16, 30, 256])
    nh_p = dout("nh_p", [2, 4, 64, 64]); nh_s = dout("nh_s", [2, 16, 4, 64, 64])
    dbg_cat = dout("dbg_cat", [2, 8192, D], BF16) if DBG_CAT else None
    y1s = dout("y1s_scratch", [128, D])
    NTILES = NT_RUN
    y1 = dout("y1_scratch", [8192, D])
    ktD = dout("ktD", [128, 2 * 8192], BF16)
    vD = dout("vD", [8192, 260], BF16)

    def sb(name, free, dt=F32):
        return nc.alloc_sbuf_tensor(name, [128, free], dt)

    def ps(name, free, dt=F32):
        return nc.alloc_psum_tensor(name, [128, free], dt)

    zA = ps("zA", 512); zB = ps("zB", 512); tpb = ps("tpb", 1024, BF16)
    sc = [ps("sc0", 512), ps("sc1", 512)]
    ot = [ps("ot0", 512), ps("ot1", 512)]
    misc = ps("misc", 512)

    win = sb("win", 8 * DIN, BF16)
    wout = sb("wout", 8 * D, BF16)
    wpw = sb("wpw", 2 * 256, BF16)
    stg = [sb("stg0", 512)]
    ktB = [sb(f"ktB{j}", 256, BF16) for j in range(4)]
    vB = [sb(f"vB{j}", 260, BF16) for j in range(4)]
    identf = sb("identf", 128); identb = sb("identb", 128, BF16)
    trilb = sb("trilb", 128, BF16); msb = sb("msb", 128, BF16)
    mcum = sb("mcum", 2 * 128); mlast = sb("mlast", 2 * 128); mattb = sb("mattb", 2 * 512, BF16)
    cmb = sb("cmb", 2 * 1024, BF16); segb = sb("segb", 2 * 1024, BF16); rowm = sb("rowm", 4)
    mod = sb("mod", 3 * D)
    cosp = sb("cosp", NT * 16); sinp = sb("sinp", NT * 16); coss = sb("coss", 16); sins = sb("sins", 16)
    prm = sb("prm", 10 * 256)
    lnp = sb("lnp", 2 * D)
    lamt = sb("lamt", 8)
    wmT = sb("wmT", 2 * 4 * 128, BF16)
    bsE = sb("bsE", 2 * 256)
    cwT = sb("cwT", 2 * 31)
    idx = sb("idx", 256, I32); idxf = sb("idxf", 256)
    xt = sb("xt", D); hb = sb("hb", D, BF16); hT = sb("hT", D, BF16); t1 = sb("t1", D)
    qk = sb("qk", 512); qkr = sb("qkr", 512); qb = sb("qb", 256, BF16); kb = sb("kb", 256, BF16)
    qblk = sb("qblk", 2 * 4 * 128, BF16)
    gate = sb("gate", 4 * 256)
    vnb = sb("vnb", 256, BF16)
    hcT = sb("hcT", 2 * 640, BF16)
    cyb = sb("cyb", 256, BF16); cyT = sb("cyT", 256, BF16)
    dq = sb("dq", 256); df = sb("df", 256); lf = sb("lf", 256); kd = sb("kd", 256); vdb = sb("vdb", 256, BF16)
    cl = sb("cl", 512); e1 = sb("e1", 256); e2 = sb("e2", 256); e3 = sb("e3", 256)
    qdup = sb("qdup", 512, BF16); kt_ = sb("kt_", 256, BF16); kpdup = sb("kpdup", 512, BF16); eldup = sb("eldup", 512)
    q2T = sb("q2T", 512, BF16); khT = sb("khT", 512, BF16); a2T = sb("a2T", 512)
    qexp = sb("qexp", 1024, BF16)
    vexp = sb("vexp", 1024, BF16)
    attm = sb("attm", 512, BF16)
    sst = sb("sst", 8 * 64, BF16)
    sstf = sb("sstf", 8 * 64)
    Sf = sb("Sf", 4 * 64)
    od = sb("od", 256); sq = sb("sq", 256); ss4 = sb("ss4", 8)
    cat = sb("cat", D, BF16); catT = sb("catT", D, BF16)
    pt_ = sb("pt_", 512, BF16)
    oa = sb("oa", 2 * 260); rl = sb("rl", 8); oat = sb("oat", 256)
    st2 = sb("st2", 8)
    kpg = sb("kpg", 256); vpg = sb("vpg", 256); kpb = sb("kpb", 256, BF16); kpT = sb("kpT", 256, BF16); vpa = sb("vpa", 260, BF16)


    def rs(names):
        return [res(n) for n in names]

    def ACT(fn, r, w): p.act(fn, rs(r), rs(w))
    def DVE(fn, r, w): p.dve(fn, rs(r), rs(w))
    def POOL(fn, r, w): p.pool(fn, rs(r), rs(w))
    def PE(fn, r, w): p.pe(fn, rs(r), rs(w))
    def DMA(fn, r, w, q="sp"): p.dma(fn, rs(r), rs(w), q=q)

    def bcrow(dt_, off, n):
        return DV(dt_, off, [[0, 128], [1, n]])

    def load_const(dst, nfree, src_ap, name, cast_to=None, tmp=None):
        if cast_to is None:
            DMA(lambda e: e.dma_start(out=dst[:, 0:nfree], in_=src_ap), [], [name])
        else:
            DMA(lambda e: e.dma_start(out=tmp[:, 0:nfree], in_=src_ap), [], ["t1"])
            DVE(lambda e: e.tensor_copy(out=dst[:, 0:nfree], in_=tmp[:, 0:nfree]), ["t1"], [name])

    stgb = [sb("stgb0", 512, BF16), sb("stgb1", 512, BF16)]
    kTt = sb("kTt", 256, BF16); vat = sb("vat", 260, BF16)
    TWO_PI = 2.0 * math.pi
    load_const(identf, 128, c_ident[:, :], "identf")
    DVE(lambda e: e.tensor_copy(out=identb[:], in_=identf[:]), ["identf"], ["identb"])
    load_const(trilb, 128, c_tril[:, :], "trilb", BF16, t1)
    load_const(msb, 128, c_ms[:, :], "msb", BF16, t1)
    load_const(mcum, 256, c_mcum.ap().rearrange("k p t -> p k t"), "mcum")
    load_const(mlast, 256, c_mlast.ap().rearrange("k p t -> p k t"), "mlast")
    for kk in range(2):
        DMA(lambda e, kk=kk: e.dma_start(out=t1[:, 0:128], in_=c_matt[kk, :, :]), [], ["t1"])
        DVE(lambda e, kk=kk: e.tensor_copy(out=V(mattb, kk * 512, [[128, 4], [1, 128]]), in_=V(t1, 0, [[0, 4], [1, 128]])), ["t1"], ["mattb"])
        DMA(lambda e, kk=kk: e.dma_start(out=t1[:, 0:1024], in_=c_cm[kk, :, :]), [], ["t1"])
        DVE(lambda e, kk=kk: e.tensor_copy(out=cmb[:, kk * 1024:(kk + 1) * 1024], in_=t1[:, 0:1024]), ["t1"], ["cmb"])
        DMA(lambda e, kk=kk: e.dma_start(out=t1[:, 0:1024], in_=c_seg[kk, :, :]), [], ["t1"])
        DVE(lambda e, kk=kk: e.tensor_copy(out=segb[:, kk * 1024:(kk + 1) * 1024], in_=t1[:, 0:1024]), ["t1"], ["segb"])
    load_const(rowm, 4, c_rowm[:, :], "rowm")
    DMA(lambda e: e.dma_start(out=t1[:, 0:NT], in_=pos_p[:, :]), [], ["t1"])
    DMA(lambda e: e.dma_start(out=t1[:, 64:65], in_=pos_s[:, :]), [], ["t1"])
    DMA(lambda e: e.dma_start(out=t1[:, 128:144], in_=invf[:, :]), [], ["t1"])
    DVE(lambda e: e.tensor_tensor(out=V(xt, 0, [[16, NT], [1, 16]]), in0=V(t1, 0, [[1, NT], [0, 16]]), in1=V(t1, 128, [[0, NT], [1, 16]]), op=ALU.mult), ["t1"], ["xt"])
    DVE(lambda e: e.tensor_tensor(out=V(qk, 0, [[1, 16]]), in0=V(t1, 64, [[0, 16]]), in1=V(t1, 128, [[1, 16]]), op=ALU.mult), ["t1"], ["qk"])

    C1 = 6.28125
    C2 = TWO_PI - C1

    def sincos(dsin, dcos, sname, cname, src_ap, srcname, n):
        r = mod[:, 0:n]; ki = mod[:, 1024:1024 + n].bitcast(I32); kf = mod[:, 2048:2048 + n]; m = mod[:, 1024:1024 + n]
        DVE(lambda e: e.tensor_scalar(out=kf, in0=src_ap, scalar1=1.0 / TWO_PI, scalar2=None, op0=ALU.mult), [srcname], ["mod"])
        DVE(lambda e: e.tensor_copy(out=ki, in_=kf), ["mod"], ["mod"])
        DVE(lambda e: e.tensor_copy(out=kf, in_=ki), ["mod"], ["mod"])
        DVE(lambda e: e.scalar_tensor_tensor(out=r, in0=kf, scalar=-C1, in1=src_ap, op0=ALU.mult, op1=ALU.add), ["mod", srcname], ["mod"])
        DVE(lambda e: e.scalar_tensor_tensor(out=r, in0=kf, scalar=-C2, in1=r, op0=ALU.mult, op1=ALU.add), ["mod"], ["mod"])
        for which in range(2):
            if which == 1:
                DVE(lambda e: e.tensor_scalar_add(out=r, in0=r, scalar1=0.5 * math.pi), ["mod"], ["mod"])
            DVE(lambda e: e.tensor_scalar(out=m, in0=r, scalar1=math.pi, scalar2=None, op0=ALU.is_gt), ["mod"], ["mod"])
            DVE(lambda e: e.scalar_tensor_tensor(out=r, in0=m, scalar=-TWO_PI, in1=r, op0=ALU.mult, op1=ALU.add), ["mod"], ["mod"])
            DVE(lambda e: e.tensor_scalar(out=m, in0=r, scalar1=-math.pi, scalar2=None, op0=ALU.is_lt), ["mod"], ["mod"])
            DVE(lambda e: e.scalar_tensor_tensor(out=r, in0=m, scalar=TWO_PI, in1=r, op0=ALU.mult, op1=ALU.add), ["mod"], ["mod"])
            dst, dn = (dsin, sname) if which == 0 else (dcos, cname)
            ACT(lambda e, dst=dst: e.activation(out=dst[:, 0:n], in_=r, func=AF.Sin), ["mod"], [dn])
    sincos(sinp, cosp, "sinp", "cosp", xt[:, 0:NT * 16], "xt", NT * 16)
    sincos(sins, coss, "sins", "coss", qk[:, 0:16], "qk", 16)
    DMA(lambda e: e.dma_start(out=idx[:, :], in_=ptab[:, :]), [], ["idx"])
    DMA(lambda e: e.dma_start(out=st2[:, 0:1], in_=iota_p[:, :]), [], ["st2"])
    DVE(lambda e: e.tensor_copy(out=idxf[:, :], in_=idx[:, :]), ["idx"], ["idxf"])
    DVE(lambda e: e.tensor_scalar(out=idxf[:, :], in0=idxf[:, :], scalar1=128.0, scalar2=st2[:, 0:1], op0=ALU.mult, op1=ALU.add), ["idxf", "st2"], ["idxf"])
    DVE(lambda e: e.tensor_copy(out=idx[:, :], in_=idxf[:, :]), ["idxf"], ["idx"])
    for tv, nm in ((vB[0], "vB0"), (vB[1], "vB1"), (vpa, "vpa"), (vat, "vat")):
        DVE(lambda e, tv=tv: e.memset(tv[:, :], 1.0), [], [nm])

    k.__dict__.update(locals())
    try:
        ck('consts')
        for l in range(2):
            emit_layer(k, l)
    except _Stop:
        pass
    p.finalize_and_emit()
    return k


def emit_layer(k, l):
    g = k.__dict__
    p = k.p; nc = k.nc
    ACT, DVE, POOL, PE, DMA, res = k.ACT, k.DVE, k.POOL, k.PE, k.DMA, k.res
    (zA, zB, tpb, sc, ot, misc, win, wout, wpw, stg, stgb, identf, identb, trilb, msb, mcum, mlast, mattb, cmb, segb, rowm, mod,
     cosp, sinp, coss, sins, prm, lnp, lamt, wmT, bsE, cwT, idx, xt, hb, hT, t1, qk, qkr, qb, kb, qblk, gate, vnb,
     hcT, cyb, cyT, dq, df, lf, kd, vdb, cl, e1, e2, e3, qdup, kt_, kpdup, eldup, q2T, khT, a2T, qexp, vexp, attm,
     sst, sstf, Sf, od, sq, ss4, cat, catT, pt_, oa, rl, oat, st2, kpg, vpg, kpb, kpT, vpa, kTt, vat, ktB, vB) = [g[n] for n in (
        "zA zB tpb sc ot misc win wout wpw stg stgb identf identb trilb msb mcum mlast mattb cmb segb rowm mod "
        "cosp sinp coss sins prm lnp lamt wmT bsE cwT idx xt hb hT t1 qk qkr qb kb qblk gate vnb "
        "hcT cyb cyT dq df lf kd vdb cl e1 e2 e3 qdup kt_ kpdup eldup q2T khT a2T qexp vexp attm "
        "sst sstf Sf od sq ss4 cat catT pt_ oa rl oat st2 kpg vpg kpb kpT vpa kTt vat ktB vB").split()]
    lam_init = 0.8 - 0.6 * math.exp(-0.3 * l)
    zz = [zA, zB]; zn = ["zA", "zB"]

    cnt = [0]

    def load_w(dst, dst_off, dram, r0, c0, ncols, dname):
        i = 0; cnt[0] += 1
        DMA(lambda e: e.dma_start(out=stg[i][:, 0:ncols], in_=dram[l, r0:r0 + 128, c0:c0 + ncols]), [], [f"stg{i}"])
        POOL(lambda e: e.tensor_copy(out=dst[:, dst_off:dst_off + ncols], in_=stg[i][:, 0:ncols]), [f"stg{i}"], [dname])

    for kc in range(8):
        for n in range(7):
            load_w(win, kc * DIN + n * 512, k.w_in, kc * 128, n * 512, 512, "win")
        for n in range(2):
            load_w(wout, kc * D + n * 512, k.w_out, kc * 128, n * 512, 512, "wout")
    for ch in range(2):
        load_w(wpw, ch * 256, k.w_pw, ch * 128, 0, 256, "wpw")

    def bc(dram, n, slot, rep=1):
        if rep == 1:
            src = DV(dram, l * n, [[0, 128], [1, n]])
        else:
            src = DV(dram, l * n, [[0, 128], [0, rep], [1, n]])
        DMA(lambda e: e.dma_start(out=prm[:, slot * 256:(slot + 1) * 256] if rep == 1 else V(prm, slot * 256, [[n, rep], [1, n]]), in_=src), [], ["prm"])
    bc(k.sg_g, 256, 0); bc(k.sg_b, 256, 1); bc(k.conv_b, 256, 2); bc(k.cn_g, 256, 3); bc(k.cn_b, 256, 4)
    bc(k.attn_g, 64, 7, 4); bc(k.hg_g, 64, 8, 4)
    DVE(lambda e: e.tensor_scalar_mul(out=prm[:, 7 * 256:8 * 256], in0=prm[:, 7 * 256:8 * 256], scalar1=1.0 - lam_init), ["prm"], ["prm"])
    if l == 0:
        DVE(lambda e: e.memset(prm[:, 5 * 256:6 * 256], 0.0), [], ["prm"])
    else:
        DMA(lambda e: e.dma_start(out=prm[:, 5 * 256:6 * 256], in_=DV(k.lowb, 256, [[0, 128], [1, 256]])), [], ["prm"])
        DMA(lambda e: e.dma_start(out=prm[:, 9 * 256:10 * 256], in_=DV(k.lowb, 0, [[0, 128], [1, 256]])), [], ["prm"])
        DVE(lambda e: e.tensor_tensor(out=prm[:, 5 * 256:6 * 256], in0=prm[:, 5 * 256:6 * 256], in1=prm[:, 9 * 256:10 * 256], op=ALU.subtract), ["prm"], ["prm"])
        ACT(lambda e: e.activation(out=prm[:, 5 * 256:6 * 256], in_=prm[:, 5 * 256:6 * 256], func=AF.Sigmoid), ["prm"], ["prm"])
    DVE(lambda e: e.tensor_scalar(out=prm[:, 6 * 256:7 * 256], in0=prm[:, 5 * 256:6 * 256], scalar1=-1.0, scalar2=1.0, op0=ALU.mult, op1=ALU.add), ["prm"], ["prm"])
    DMA(lambda e: e.dma_start(out=lnp[:, 0:D], in_=DV(k.ln_g, l * D, [[0, 128], [1, D]])), [], ["lnp"])
    DMA(lambda e: e.dma_start(out=lnp[:, D:2 * D], in_=DV(k.ln_b, l * D, [[0, 128], [1, D]])), [], ["lnp"])
    DMA(lambda e: e.dma_start(out=e1[:, 0:128], in_=DV(k.lam_qk, l * 128, [[0, 128], [1, 128]])), [], ["e1"])
    DVE(lambda e: e.tensor_tensor(out=V(e2, 0, [[32, 2], [1, 32]]), in0=V(e1, 0, [[64, 2], [1, 32]]), in1=V(e1, 32, [[64, 2], [1, 32]]), op=ALU.mult), ["e1"], ["e2"])
    DVE(lambda e: e.reduce_sum(out=lamt[:, 1:3], in_=V(e2, 0, [[32, 2], [1, 32]]), axis=AX.X), ["e2"], ["lamt"])
    ACT(lambda e: e.activation(out=lamt[:, 1:3], in_=lamt[:, 1:3], func=AF.Exp), ["lamt"], ["lamt"])
    DVE(lambda e: e.tensor_tensor(out=lamt[:, 0:1], in0=lamt[:, 2:3], in1=lamt[:, 1:2], op=ALU.subtract), ["lamt"], ["lamt"])
    DVE(lambda e: e.tensor_scalar_add(out=lamt[:, 0:1], in0=lamt[:, 0:1], scalar1=-lam_init), ["lamt"], ["lamt"])
    for kind in range(2):
        for gi in range(4):
            if kind == 0:
                DMA(lambda e, gi=gi: e.dma_start(out=t1[:, 0:128], in_=k.w_s[l, gi, :, :]), [], ["t1"])
                PE(lambda e: e.transpose(out=sc[0][:, 0:128], in_=t1[:, 0:128], identity=identf[:, :]), ["t1", "identf"], ["sc0"])
                DVE(lambda e, gi=gi: e.tensor_tensor(out=wmT[:, gi * 128:(gi + 1) * 128], in0=sc[0][:, 0:128], in1=trilb[:, :], op=ALU.mult), ["sc0", "trilb"], ["wmT"])
            else:
                DVE(lambda e: e.memset(t1[:, 0:128], 0.0), [], ["t1"])
                for s_ in range(16):
                    DMA(lambda e, gi=gi, s_=s_: e.dma_start(out=t1[8 * s_:8 * s_ + 8, 8 * s_:8 * s_ + 8], in_=k.w_s[l, gi, 0:8, 0:8]), [], ["t1"])
                PE(lambda e: e.transpose(out=sc[0][:, 0:128], in_=t1[:, 0:128], identity=identf[:, :]), ["t1", "identf"], ["sc0"])
                DVE(lambda e, gi=gi: e.tensor_tensor(out=wmT[:, 512 + gi * 128:512 + (gi + 1) * 128], in0=sc[0][:, 0:128], in1=msb[:, :], op=ALU.mult), ["sc0", "msb"], ["wmT"])
        if kind == 0:
            DMA(lambda e: e.dma_start(out=st2[:, 4:8], in_=k.b_s[l].rearrange("g t -> t g"), allow_slow_non_contiguous=True), [], ["st2"])
        else:
            for s_ in range(16):
                DMA(lambda e, s_=s_: e.dma_start(out=st2[8 * s_:8 * s_ + 8, 4:8], in_=k.b_s[l, :, 0:8].rearrange("g t -> t g"), allow_slow_non_contiguous=True), [], ["st2"])
        DVE(lambda e, kind=kind: e.tensor_copy(out=V(bsE, kind * 256, [[64, 4], [1, 64]]), in_=V(st2, 4, [[1, 4], [0, 64]])), ["st2"], ["bsE"])
    for ch in range(2):
        DMA(lambda e, ch=ch: e.dma_start(out=cwT[:, ch * 31:(ch + 1) * 31], in_=k.conv_w[l, :, ch * 128:(ch + 1) * 128].rearrange("j c -> c j"), allow_slow_non_contiguous=True), [], ["cwT"])
    DVE(lambda e: e.memset(Sf[:, :], 0.0), [], ["Sf"])

    ck(f'params{l}')
    def layer_norm(x, n, gap, bap, xname, pnames, junk, jname):
        DVE(lambda e: e.memset(st2[:, 0:4], 0.0), [], ["st2"])
        DVE(lambda e: e.reduce_sum(out=st2[:, 0:1], in_=x[:, 0:n], axis=AX.X), [xname], ["st2"])
        ACT(lambda e: e.activation(out=junk[:, 0:n], in_=x[:, 0:n], func=AF.Square, accum_out=st2[:, 1:2]), [xname], [jname, "st2"])
        DVE(lambda e: e.tensor_scalar_mul(out=st2[:, 0:2], in0=st2[:, 0:2], scalar1=1.0 / n), ["st2"], ["st2"])
        DVE(lambda e: e.tensor_tensor(out=st2[:, 2:3], in0=st2[:, 0:1], in1=st2[:, 0:1], op=ALU.mult), ["st2"], ["st2"])
        DVE(lambda e: e.tensor_tensor(out=st2[:, 1:2], in0=st2[:, 1:2], in1=st2[:, 2:3], op=ALU.subtract), ["st2"], ["st2"])
        DVE(lambda e: e.tensor_scalar_add(out=st2[:, 1:2], in0=st2[:, 1:2], scalar1=EPS), ["st2"], ["st2"])
        ACT(lambda e: e.activation(out=st2[:, 1:2], in_=st2[:, 1:2], func=AF.Sqrt), ["st2"], ["st2"])
        DVE(lambda e: e.reciprocal(out=st2[:, 1:2], in_=st2[:, 1:2]), ["st2"], ["st2"])
        DVE(lambda e: e.tensor_scalar(out=x[:, 0:n], in0=x[:, 0:n], scalar1=st2[:, 0:1], scalar2=st2[:, 1:2], op0=ALU.subtract, op1=ALU.mult), [xname, "st2"], [xname])
        DVE(lambda e: e.tensor_tensor(out=x[:, 0:n], in0=x[:, 0:n], in1=gap, op=ALU.mult), [xname] + pnames, [xname])
        DVE(lambda e: e.tensor_tensor(out=x[:, 0:n], in0=x[:, 0:n], in1=bap, op=ALU.add), [xname] + pnames, [xname])

    def rms_heads(x, xname, gslot, gcol, out_c0):
        DVE(lambda e: e.tensor_tensor(out=sq[:, :], in0=x[:, 0:256], in1=x[:, 0:256], op=ALU.mult), [xname], ["sq"])
        DVE(lambda e: e.reduce_sum(out=ss4[:, 0:4], in_=V(sq, 0, [[64, 4], [1, 64]]), axis=AX.X), ["sq"], ["ss4"])
        DVE(lambda e: e.tensor_scalar(out=ss4[:, 0:4], in0=ss4[:, 0:4], scalar1=1.0 / 64, scalar2=EPS, op0=ALU.mult, op1=ALU.add), ["ss4"], ["ss4"])
        ACT(lambda e: e.activation(out=ss4[:, 0:4], in_=ss4[:, 0:4], func=AF.Sqrt), ["ss4"], ["ss4"])
        DVE(lambda e: e.reciprocal(out=ss4[:, 0:4], in_=ss4[:, 0:4]), ["ss4"], ["ss4"])
        DVE(lambda e: e.tensor_tensor(out=V(x, 0, [[64, 4], [1, 64]]), in0=V(x, 0, [[64, 4], [1, 64]]), in1=V(ss4, 0, [[1, 4], [0, 64]]), op=ALU.mult), [xname, "ss4"], [xname])
        DVE(lambda e: e.tensor_tensor(out=x[:, 0:256], in0=x[:, 0:256], in1=prm[:, gslot * 256:(gslot + 1) * 256], op=ALU.mult), [xname, "prm"], [xname])
        DVE(lambda e: e.tensor_tensor(out=cat[:, out_c0:out_c0 + 256], in0=x[:, 0:256], in1=gate[:, gcol:gcol + 256], op=ALU.mult), [xname, "gate"], ["cat"])

    def compute_mod(csrc):
        DMA(lambda e: e.dma_start(out=t1[:, :], in_=csrc[:, :]), [], ["t1"])
        ACT(lambda e: e.activation(out=hb[:, :], in_=t1[:, :], func=AF.Silu), ["t1"], ["hb"])
        for kc in range(8):
            PE(lambda e, kc=kc: e.transpose(out=tpb[:, kc * 128:(kc + 1) * 128], in_=hb[:, kc * 128:(kc + 1) * 128], identity=identb[:, :]), ["hb", "identb"], ["tpb"])
        DVE(lambda e: e.tensor_copy(out=hT[:, :], in_=tpb[:, :]), ["tpb"], ["hT"])
        for n in range(6):
            for kc in range(8):
                i = 0; cnt[0] += 1
                DMA(lambda e, i=i, kc=kc, n=n: e.dma_start(out=stg[i][:, :], in_=k.w_ada[l, kc * 128:(kc + 1) * 128, n * 512:(n + 1) * 512]), [], [f"stg{i}"])
                POOL(lambda e, i=i: e.tensor_copy(out=stgb[i][:, :], in_=stg[i][:, :]), [f"stg{i}"], [f"stgb{i}"])
                PE(lambda e, i=i, kc=kc, n=n: e.matmul(out=zz[n % 2][:, :], lhsT=hT[:, kc * 128:(kc + 1) * 128], rhs=stgb[i][:, :], start=(kc == 0), stop=(kc == 7)),
                   ["hT", f"stgb{i}"], [zn[n % 2]])
            DMA(lambda e, n=n: e.dma_start(out=t1[:, 0:512], in_=DV(k.b_ada, l * 3 * D + n * 512, [[0, 128], [1, 512]])), [], ["t1"])
            DVE(lambda e, n=n: e.tensor_tensor(out=mod[:, n * 512:(n + 1) * 512], in0=zz[n % 2][:, :], in1=t1[:, 0:512], op=ALU.add), [zn[n % 2], "t1"], ["mod"])
        DVE(lambda e: e.tensor_scalar_add(out=mod[:, D:2 * D], in0=mod[:, D:2 * D], scalar1=1.0), ["mod"], ["mod"])

    def tile(kind, i):
        kk = kind
        nseg = 8 if kind == 0 else 16
        nj = nseg // 2
        seglen = 128 // nseg
        if kind == 0:
            src = (k.xp if l == 0 else k.y1)[i * 128:(i + 1) * 128, :]
            srcn = [] if l == 0 else ["y1"]
        else:
            src = (k.xs if l == 0 else k.y1s)[:, :]
            srcn = [] if l == 0 else ["y1s"]
        DMA(lambda e: e.dma_start(out=xt[:, :], in_=src), srcn, ["xt"])
        DVE(lambda e: e.tensor_tensor(out=t1[:, :], in0=xt[:, :], in1=mod[:, D:2 * D], op=ALU.mult), ["xt", "mod"], ["t1"])
        DVE(lambda e: e.tensor_tensor(out=hb[:, :], in0=t1[:, :], in1=mod[:, 0:D], op=ALU.add), ["t1", "mod"], ["hb"])
        for kc in range(8):
            PE(lambda e, kc=kc: e.transpose(out=tpb[:, kc * 128:(kc + 1) * 128], in_=hb[:, kc * 128:(kc + 1) * 128], identity=identb[:, :]), ["hb", "identb"], ["tpb"])
        DVE(lambda e: e.tensor_copy(out=hT[:, :], in_=tpb[:, :]), ["tpb"], ["hT"])

        def inproj(n):
            z = zz[n % 2] if not (DBG_MISC and n == 2) else misc
            for kc in range(8):
                PE(lambda e, kc=kc: e.matmul(out=z[:, :], lhsT=hT[:, kc * 128:(kc + 1) * 128], rhs=win[:, kc * DIN + n * 512:kc * DIN + (n + 1) * 512], start=(kc == 0), stop=(kc == 7)),
                   ["hT", "win"] + (DBG_SER_LIST if DBG_SER else []), [zn[n % 2] if not (DBG_MISC and n == 2) else "misc"])
            zname = (zn[n % 2] if not (DBG_MISC and n == 2) else "misc")
            DVE(lambda e: e.tensor_copy(out=qk[:, :], in_=z[:, :]), [zname], ["qk"])
            return qk, "qk"

        ck(f'hT{l}{kind}{i}')
        z, zr = inproj(0)
        ck(f'z0{l}{kind}{i}')
        if kind == 0:
            cs_ = V(cosp, i * 16, [[0, 16], [1, 16]]); sn_ = V(sinp, i * 16, [[0, 16], [1, 16]]); csn = ["cosp", "sinp"]
        else:
            cs_ = V(coss, 0, [[0, 16], [1, 16]]); sn_ = V(sins, 0, [[0, 16], [1, 16]]); csn = ["coss", "sins"]
        x1 = V(qk, 0, [[32, 16], [1, 16]]); x2 = V(qk, 16, [[32, 16], [1, 16]])
        o1 = V(qkr, 0, [[32, 16], [1, 16]]); o2 = V(qkr, 16, [[32, 16], [1, 16]]); tm = V(e1, 0, [[16, 16], [1, 16]])
        DVE(lambda e: e.tensor_tensor(out=o1, in0=x1, in1=cs_, op=ALU.mult), ["qk"] + csn, ["qkr"])
        DVE(lambda e: e.tensor_tensor(out=tm, in0=x2, in1=sn_, op=ALU.mult), ["qk"] + csn, ["e1"])
        DVE(lambda e: e.tensor_tensor(out=o1, in0=o1, in1=tm, op=ALU.subtract), ["qkr", "e1"], ["qkr"])
        DVE(lambda e: e.tensor_tensor(out=o2, in0=x2, in1=cs_, op=ALU.mult), ["qk"] + csn, ["qkr"])
        DVE(lambda e: e.tensor_tensor(out=tm, in0=x1, in1=sn_, op=ALU.mult), ["qk"] + csn, ["e1"])
        DVE(lambda e: e.tensor_tensor(out=o2, in0=o2, in1=tm, op=ALU.add), ["qkr", "e1"], ["qkr"])
        ck(f'rope{l}{kind}{i}')
        if kind == 0:
            DMA(lambda e: e.dma_start(out=k.nk_p[l, i * 128:(i + 1) * 128, :], in_=qkr[:, 256:512]), ["qkr"], [])
        else:
            DMA(lambda e: e.dma_start(out=k.nk_s[l, :, :], in_=qkr[:, 256:512]), ["qkr"], [])
        ck(f'nk{l}{kind}{i}')
        DVE(lambda e: e.tensor_copy(out=qb[:, :], in_=qkr[:, 0:256]), ["qkr"], ["qb"])
        DVE(lambda e: e.tensor_copy(out=kb[:, :], in_=qkr[:, 256:512]), ["qkr"], ["kb"])
        for ch in range(2):
            PE(lambda e, ch=ch: e.transpose(out=tpb[:, ch * 128:(ch + 1) * 128], in_=qb[:, ch * 128:(ch + 1) * 128], identity=identb[:, :]), ["qb", "identb"], ["tpb"])
            PE(lambda e, ch=ch: e.transpose(out=tpb[:, 256 + ch * 128:256 + (ch + 1) * 128], in_=kb[:, ch * 128:(ch + 1) * 128], identity=identb[:, :]), ["kb", "identb"], ["tpb"])
        for ch in range(2):
            for m in range(4):
                DVE(lambda e, ch=ch, m=m: e.tensor_scalar(out=qblk[:, ch * 512 + m * 128:ch * 512 + (m + 1) * 128], in0=tpb[:, ch * 128:(ch + 1) * 128], scalar1=rowm[:, m:m + 1], scalar2=None, op0=ALU.mult),
                    ["tpb", "rowm"], ["qblk"])
        ck(f'qblk{l}{kind}{i}')
        DVE(lambda e: e.tensor_copy(out=kTt[:, :], in_=tpb[:, 256:512]), ["tpb"], ["kTt"])
        ck(f'ktt{l}{kind}{i}')
        if kind == 0:
            DMA(lambda e: e.dma_start(out=k.ktD.ap().rearrange("p (c n) -> p c n", c=2)[:, :, i * 128:(i + 1) * 128], in_=V(kTt, 0, [[128, 2], [1, 128]])), ["kTt"], ["ktD"])
        ck(f'c0{l}{kind}{i}')
        z, zr = inproj(1)
        DVE(lambda e: e.tensor_copy(out=sq[:, :], in_=z[:, 0:256]), [zr], ["sq"])
        DVE(lambda e: e.tensor_copy(out=V(vat, 0, [[65, 4], [1, 64]]), in_=V(z, 0, [[64, 4], [1, 64]])), [zr], ["vat"])
        ACT(lambda e: e.activation(out=gate[:, 0:256], in_=z[:, 256:512], func=AF.Silu), [zr], ["gate"])
        if kind == 0:
            DMA(lambda e: e.dma_start(out=k.nv_p[l, i * 128:(i + 1) * 128, :], in_=sq[:, :]), ["sq"], [])
            DMA(lambda e: e.dma_start(out=k.vD[i * 128:(i + 1) * 128, :], in_=vat[:, :]), ["vat"], ["vD"])
        else:
            DMA(lambda e: e.dma_start(out=k.nv_s[l, :, :], in_=sq[:, :]), ["sq"], [])
        ck(f'c1{l}{kind}{i}')
        z, zr = inproj(2)
        ck(f'z2{l}{kind}{i}')
        ACT(lambda e: e.activation(out=qkr[:, :], in_=z[:, :], func=AF.Erf, scale=0.7071067811865476), [zr], ["qkr"])
        ck(f'erf{l}{kind}{i}')
        DVE(lambda e: e.tensor_scalar(out=qkr[:, :], in0=qkr[:, :], scalar1=1.0, scalar2=0.5, op0=ALU.add, op1=ALU.mult), ["qkr"], ["qkr"])
        DVE(lambda e: e.tensor_tensor(out=e3[:, :], in0=qkr[:, 0:256], in1=z[:, 0:256], op=ALU.mult), ["qkr", zr], ["e3"])
        DVE(lambda e: e.tensor_tensor(out=dq[:, :], in0=qkr[:, 256:512], in1=z[:, 256:512], op=ALU.mult), ["qkr", zr], ["dq"])
        ck(f'gelu{l}{kind}{i}')
        layer_norm(dq, 256, prm[:, 0:256], prm[:, 256:512], "dq", ["prm"], sq, "sq")
        ck(f'ln{l}{kind}{i}')
        if kind == 1:
            DMA(lambda e: e.dma_start(out=k.nch_s[l, :, :], in_=dq[:, :]), ["dq"], [])
        DVE(lambda e: e.tensor_copy(out=vnb[:, :], in_=dq[:, :]), ["dq"], ["vnb"])
        ck(f'c2{l}{kind}{i}')
        z, zr = inproj(3)
        ACT(lambda e: e.activation(out=gate[:, 256:512], in_=z[:, 0:256], func=AF.Silu), [zr], ["gate"])
        DVE(lambda e: e.tensor_copy(out=oat[:, :], in_=z[:, 256:512]), [zr], ["oat"])
        for gi in range(4):
            PE(lambda e, gi=gi: e.matmul(out=misc[:, gi * 64:(gi + 1) * 64], lhsT=wmT[:, kk * 512 + gi * 128:kk * 512 + (gi + 1) * 128], rhs=vnb[:, gi * 64:(gi + 1) * 64], start=True, stop=True),
               ["wmT", "vnb"], ["misc"])
        DVE(lambda e: e.tensor_tensor(out=od[:, :], in0=misc[:, 0:256], in1=bsE[:, kk * 256:(kk + 1) * 256], op=ALU.add), ["misc", "bsE"], ["od"])
        DVE(lambda e: e.tensor_tensor(out=od[:, :], in0=od[:, :], in1=e3[:, :], op=ALU.mult), ["od", "e3"], ["od"])
        DVE(lambda e: e.tensor_tensor(out=cat[:, 256:512], in0=od[:, :], in1=gate[:, 256:512], op=ALU.mult), ["od", "gate"], ["cat"])
        ck(f'c3{l}{kind}{i}')
        z, zr = inproj(4)
        ACT(lambda e: e.activation(out=df[:, :], in_=z[:, 0:256], func=AF.Sigmoid), [zr], ["df"])
        ACT(lambda e: e.activation(out=gate[:, 512:768], in_=z[:, 256:512], func=AF.Silu), [zr], ["gate"])
        DVE(lambda e: e.tensor_tensor(out=df[:, :], in0=df[:, :], in1=oat[:, :], op=ALU.mult), ["df", "oat"], ["df"])
        if kind == 0:
            if i == NT - 1:
                DMA(lambda e: e.dma_start(out=k.ncv_p[l, :, :], in_=df[98:128, :]), ["df"], [])
        else:
            for s_ in range(16):
                DMA(lambda e, s_=s_: e.dma_start(out=k.ncv_s[l, s_, 22:30, :], in_=df[8 * s_:8 * s_ + 8, :]), ["df"], [])
            DMA(lambda e: e.dma_start(out=k.ncv_s[l, :, 0:22, :], in_=k.st_conv[l, :, 8:30, :]), [], [])
        for ch in range(2):
            PE(lambda e, ch=ch: e.transpose(out=sc[0][:, ch * 128:(ch + 1) * 128], in_=df[:, ch * 128:(ch + 1) * 128], identity=identf[:, :]), ["df", "identf"], ["sc0"])
        if kind == 0:
            if i == 0:
                DVE(lambda e: e.memset(hcT[:, :], 0.0), [], ["hcT"])
            else:
                DVE(lambda e: e.tensor_copy(out=V(e1, 0, [[30, 2], [1, 30]]), in_=V(hcT, 128, [[158, 2], [1, 30]])), ["hcT"], ["e1"])
                DVE(lambda e: e.tensor_copy(out=V(hcT, 0, [[158, 2], [1, 30]]), in_=V(e1, 0, [[30, 2], [1, 30]])), ["e1"], ["hcT"])
            DVE(lambda e: e.tensor_copy(out=V(hcT, 30, [[158, 2], [1, 128]]), in_=V(sc[0], 0, [[128, 2], [1, 128]])), ["sc0"], ["hcT"])
        else:
            DVE(lambda e: e.tensor_copy(out=V(hcT, 30, [[608, 2], [38, 16], [1, 8]]), in_=V(sc[0], 0, [[128, 2], [8, 16], [1, 8]])), ["sc0"], ["hcT"])
            for q4 in range(4):
                DMA(lambda e, q4=q4: e.dma_start(out=kpg[0:120, :], in_=k.st_conv[l, 4 * q4:4 * q4 + 4, :, :].rearrange("s r c -> (s r) c")), [], ["kpg"])
                for ch in range(2):
                    PE(lambda e, ch=ch: e.transpose(out=sc[1][:, ch * 128:ch * 128 + 120], in_=kpg[0:120, ch * 128:(ch + 1) * 128], identity=identf[0:120, 0:120]), ["kpg", "identf"], ["sc1"])
                DVE(lambda e, q4=q4: e.tensor_copy(out=V(hcT, q4 * 4 * 38, [[608, 2], [38, 4], [1, 30]]), in_=V(sc[1], 0, [[128, 2], [30, 4], [1, 30]])), ["sc1"], ["hcT"])
        for ch in range(2):
            E = DVE
            for j in range(31):
                if kind == 0:
                    win_ = V(hcT, ch * 158 + j, [[1, 128]]); oap = V(qk, ch * 128, [[1, 128]])
                else:
                    win_ = V(hcT, ch * 608 + j, [[38, 16], [1, 8]]); oap = V(qk, ch * 128, [[8, 16], [1, 8]])
                if j == 0:
                    E(lambda e, win_=win_, oap=oap, ch=ch, j=j: e.tensor_scalar(out=oap, in0=win_, scalar1=cwT[:, ch * 31 + j:ch * 31 + j + 1], scalar2=None, op0=ALU.mult), ["hcT", "cwT"], [f"qk{ch}"])
                else:
                    E(lambda e, win_=win_, oap=oap, ch=ch, j=j: e.scalar_tensor_tensor(out=oap, in0=win_, scalar=cwT[:, ch * 31 + j:ch * 31 + j + 1], in1=oap, op0=ALU.mult, op1=ALU.add), ["hcT", "cwT", f"qk{ch}"], [f"qk{ch}"])
        for ch in range(2):
            PE(lambda e, ch=ch: e.transpose(out=sc[1][:, ch * 128:(ch + 1) * 128], in_=qk[:, ch * 128:(ch + 1) * 128], identity=identf[:, :]), [f"qk{ch}", "identf"], ["sc1"])
        DVE(lambda e: e.tensor_tensor(out=e2[:, :], in0=sc[1][:, 0:256], in1=prm[:, 512:768], op=ALU.add), ["sc1", "prm"], ["e2"])
        layer_norm(e2, 256, prm[:, 768:1024], prm[:, 1024:1280], "e2", ["prm"], sq, "sq")
        ACT(lambda e: e.activation(out=cyb[:, :], in_=e2[:, :], func=AF.Silu), ["e2"], ["cyb"])
        for ch in range(2):
            PE(lambda e, ch=ch: e.transpose(out=tpb[:, ch * 128:(ch + 1) * 128], in_=cyb[:, ch * 128:(ch + 1) * 128], identity=identb[:, :]), ["cyb", "identb"], ["tpb"])
        DVE(lambda e: e.tensor_copy(out=cyT[:, :], in_=tpb[:, 0:256]), ["tpb"], ["cyT"])
        for ch in range(2):
            PE(lambda e, ch=ch: e.matmul(out=misc[:, 0:256], lhsT=cyT[:, ch * 128:(ch + 1) * 128], rhs=wpw[:, ch * 256:(ch + 1) * 256], start=(ch == 0), stop=(ch == 1)), ["cyT", "wpw"], ["misc"])
        DVE(lambda e: e.tensor_tensor(out=cat[:, 512:768], in0=misc[:, 0:256], in1=gate[:, 512:768], op=ALU.mult), ["misc", "gate"], ["cat"])
        ck(f'c4{l}{kind}{i}')
        z, zr = inproj(5)
        ACT(lambda e: e.activation(out=dq[:, :], in_=z[:, 0:256], func=AF.Silu), [zr], ["dq"])
        ACT(lambda e: e.activation(out=df[:, :], in_=z[:, 256:512], func=AF.Sigmoid), [zr], ["df"])
        DVE(lambda e: e.tensor_tensor(out=df[:, :], in0=df[:, :], in1=prm[:, 6 * 256:7 * 256], op=ALU.mult), ["df", "prm"], ["df"])
        DVE(lambda e: e.tensor_tensor(out=df[:, :], in0=df[:, :], in1=prm[:, 5 * 256:6 * 256], op=ALU.add), ["df", "prm"], ["df"])
        DVE(lambda e: e.tensor_scalar(out=kd[:, :], in0=df[:, :], scalar1=-1.0, scalar2=1.0, op0=ALU.mult, op1=ALU.add), ["df"], ["kd"])
        DVE(lambda e: e.tensor_scalar_max(out=lf[:, :], in0=df[:, :], scalar1=1e-30), ["df"], ["lf"])
        ACT(lambda e: e.activation(out=lf[:, :], in_=lf[:, :], func=AF.Ln), ["lf"], ["lf"])
        z, zr = inproj(6)
        DVE(lambda e: e.tensor_copy(out=vdb[:, :], in_=z[:, 0:256]), [zr], ["vdb"])
        ACT(lambda e: e.activation(out=gate[:, 768:1024], in_=z[:, 256:512], func=AF.Silu), [zr], ["gate"])
        ck(f'c6{l}{kind}{i}')
        PE(lambda e: e.matmul(out=sc[0][:, 0:256], lhsT=mcum[:, kk * 128:(kk + 1) * 128], rhs=lf[:, :], start=True, stop=True), ["mcum", "lf"], ["sc0"])
        PE(lambda e: e.matmul(out=sc[0][:, 256:512], lhsT=mlast[:, kk * 128:(kk + 1) * 128], rhs=lf[:, :], start=True, stop=True), ["mlast", "lf"], ["sc0"])
        DVE(lambda e: e.tensor_copy(out=cl[:, 0:512], in_=sc[0][:, 0:512]), ["sc0"], ["cl"])
        ACT(lambda e: e.activation(out=e1[:, :], in_=cl[:, 0:256], func=AF.Exp), ["cl"], ["e1"])
        ACT(lambda e: e.activation(out=e2[:, :], in_=cl[:, 0:256], func=AF.Exp, scale=-1.0), ["cl"], ["e2"])
        DVE(lambda e: e.tensor_tensor(out=cl[:, 0:256], in0=cl[:, 256:512], in1=cl[:, 0:256], op=ALU.subtract), ["cl"], ["cl"])
        ACT(lambda e: e.activation(out=e3[:, :], in_=cl[:, 0:256], func=AF.Exp), ["cl"], ["e3"])
        ACT(lambda e: e.activation(out=V(eldup, 0, [[128, 4], [64, 2], [1, 64]]), in_=V(cl, 256, [[64, 4], [0, 2], [1, 64]]), func=AF.Exp), ["cl"], ["eldup"])
        dup_o = lambda t_: V(t_, 0, [[128, 4], [64, 2], [1, 64]])
        dup_i = lambda t_: V(t_, 0, [[64, 4], [0, 2], [1, 64]])
        DVE(lambda e: e.tensor_tensor(out=dup_o(qdup), in0=dup_i(dq), in1=dup_i(e1), op=ALU.mult), ["dq", "e1"], ["qdup"])
        DVE(lambda e: e.tensor_tensor(out=kt_[:, :], in0=kd[:, :], in1=e2[:, :], op=ALU.mult), ["kd", "e2"], ["kt_"])
        DVE(lambda e: e.tensor_tensor(out=dup_o(kpdup), in0=dup_i(kd), in1=dup_i(e3), op=ALU.mult), ["kd", "e3"], ["kpdup"])
        for h in range(4):
            PE(lambda e, h=h: e.transpose(out=tpb[:, h * 128:(h + 1) * 128], in_=qdup[:, h * 128:(h + 1) * 128], identity=identb[:, :]), ["qdup", "identb"], ["tpb"])
            PE(lambda e, h=h: e.transpose(out=tpb[0:64, 512 + h * 128:512 + (h + 1) * 128], in_=kt_[:, h * 64:(h + 1) * 64], identity=identb[:, :]), ["kt_", "identb"], ["tpb"])
            PE(lambda e, h=h: e.transpose(out=sc[1][:, h * 128:(h + 1) * 128], in_=eldup[:, h * 128:(h + 1) * 128], identity=identf[:, :]), ["eldup", "identf"], ["sc1"])
        DVE(lambda e: e.tensor_copy(out=q2T[:, :], in_=tpb[:, 0:512]), ["tpb"], ["q2T"])
        DVE(lambda e: e.tensor_copy(out=khT[0:64, :], in_=tpb[0:64, 512:1024]), ["tpb"], ["khT"])
        DVE(lambda e: e.tensor_copy(out=a2T[:, :], in_=sc[1][:, :]), ["sc1"], ["a2T"])
        for h in range(4):
            PE(lambda e, h=h: e.matmul(out=misc[:, h * 128:(h + 1) * 128], lhsT=khT[0:64, h * 128:(h + 1) * 128], rhs=q2T[0:64, h * 128:(h + 1) * 128], start=True, stop=True), ["khT", "q2T"], ["misc"])
        DVE(lambda e: e.tensor_tensor(out=attm[:, :], in0=misc[:, :], in1=mattb[:, kk * 512:(kk + 1) * 512], op=ALU.mult), ["misc", "mattb"], ["attm"])
        for h in range(4):
            if kind == 1:
                for du in range(2):
                    DMA(lambda e, h=h, du=du: e.dma_start(out=V(sstf, 0, [[64, 8], [1, 64]], p0=du * 64, npart=64),
                                                          in_=DV(k.st_hgrn, (l * 16 + du) * 16384 + h * 4096, [[64, 64], [2 * 16384, 8], [1, 64]])), [], ["sstf"])
                ACT(lambda e: e.copy(out=sst[:, :], in_=sstf[:, :]), ["sstf"], ["sst"])
            DVE(lambda e, h=h: e.tensor_tensor(out=V(vexp, 0, [[64, nseg], [1, 64]]), in0=V(vdb, h * 64, [[0, nseg], [1, 64]]), in1=V(segb, kk * 1024, [[64, nseg], [1, 64]]), op=ALU.mult), ["vdb", "segb"], ["vexp"])
            PE(lambda e, h=h: e.matmul(out=ot[0][:, :], lhsT=kpdup[:, h * 128:(h + 1) * 128], rhs=vexp[:, 0:512], start=True, stop=True), ["kpdup", "vexp"], ["ot0"])
            if kind == 1:
                PE(lambda e, h=h: e.matmul(out=ot[1][:, :], lhsT=kpdup[:, h * 128:(h + 1) * 128], rhs=vexp[:, 512:1024], start=True, stop=True), ["kpdup", "vexp"], ["ot1"])
            for sg in range(nseg):
                hf = sg % 2; j = sg // 2
                acol = h * 128 + seglen * sg
                bsrc = ot[sg // 8]; bname = f"ot{sg // 8}"
                if kind == 0:
                    ACT(lambda e, h=h, hf=hf, j=j: e.copy(out=V(sst, j * 64, [[1, 64]], p0=hf * 64, npart=64), in_=V(Sf, h * 64, [[1, 64]], p0=hf * 64, npart=64)), ["Sf"], ["sst"])
                    DVE(lambda e, h=h, sg=sg, acol=acol, bsrc=bsrc: e.scalar_tensor_tensor(out=Sf[:, h * 64:(h + 1) * 64], in0=Sf[:, h * 64:(h + 1) * 64], scalar=a2T[:, acol:acol + 1], in1=bsrc[:, (sg % 8) * 64:(sg % 8 + 1) * 64], op0=ALU.mult, op1=ALU.add),
                        ["Sf", "a2T", bname], ["Sf"])
                else:
                    DVE(lambda e, h=h, sg=sg, hf=hf, j=j, acol=acol, bsrc=bsrc: e.scalar_tensor_tensor(
                        out=V(sstf, j * 64, [[1, 64]], p0=hf * 64, npart=64), in0=V(sstf, j * 64, [[1, 64]], p0=hf * 64, npart=64),
                        scalar=V(a2T, acol, [[1, 1]], p0=hf * 64, npart=64), in1=V(bsrc, (sg % 8) * 64, [[1, 64]], p0=hf * 64, npart=64), op0=ALU.mult, op1=ALU.add),
                        ["sstf", "a2T", bname], ["sstf"])
            PE(lambda e, h=h: e.matmul(out=zA[:, h * 64:(h + 1) * 64], lhsT=attm[:, h * 128:(h + 1) * 128], rhs=vdb[:, h * 64:(h + 1) * 64], start=True, stop=False), ["attm", "vdb"], ["zA"])
            DVE(lambda e, h=h: e.tensor_tensor(out=V(qexp, 0, [[128, nj], [1, 128]]), in0=V(q2T, h * 128, [[0, nj], [1, 128]]), in1=V(cmb, kk * 1024, [[128, nj], [1, 128]]), op=ALU.mult), ["q2T", "cmb"], ["qexp"])
            for j in range(nj):
                PE(lambda e, h=h, j=j: e.matmul(out=zA[:, h * 64:(h + 1) * 64], lhsT=qexp[:, j * 128:(j + 1) * 128], rhs=sst[:, j * 64:(j + 1) * 64], start=False, stop=(j == nj - 1)), ["qexp", "sst"], ["zA"])
            if kind == 1:
                for du in range(2):
                    DMA(lambda e, h=h, du=du: e.dma_start(out=DV(k.nh_s, (l * 16 + du) * 16384 + h * 4096, [[64, 64], [2 * 16384, 8], [1, 64]]),
                                                          in_=V(sstf, 0, [[64, 8], [1, 64]], p0=du * 64, npart=64)), ["sstf"], [])
        if kind == 0 and i == NT - 1:
            DMA(lambda e: e.dma_start(out=DV(k.nh_p, l * 16384, [[64, 64], [4096, 4], [1, 64]]), in_=V(Sf, 0, [[64, 4], [1, 64]], npart=64)), ["Sf"], [])
        DVE(lambda e: e.tensor_copy(out=od[:, :], in_=zA[:, 0:256]), ["zA"], ["od"])
        rms_heads(od, "od", 8, 768, 768)

        ck(f'hgrn{l}{kind}{i}')
        def kblock(ktile, kname, vtile, vname, ncol, qcol, first, last, mask, mname, s_=None):
            for ch in range(2):
                if s_ is None:
                    rhs = qblk[:, ch * 512:(ch + 1) * 512]
                else:
                    rhs = V(qblk, ch * 512 + 8 * s_, [[128, 4], [1, 8]])
                PE(lambda e, ch=ch, rhs=rhs: e.matmul(out=sc[ch][:, 0:4 * ncol], lhsT=ktile[:, ch * 128:(ch + 1) * 128], rhs=rhs, start=True, stop=True), [kname, "qblk"], [f"sc{ch}"])
                stage, sname = (qk, "qk") if ch == 0 else (qkr, "qkr")
                ptb, pname = (pt_, "pt_") if ch == 0 else (qexp, "qexp")
                if ACT_PSUM:
                    ACT(lambda e, ch=ch, ptb=ptb: e.activation(out=ptb[:, 0:4 * ncol], in_=sc[ch][:, 0:4 * ncol], func=AF.Exp), [f"sc{ch}"], [pname])
                else:
                    DVE(lambda e, ch=ch, stage=stage: e.tensor_copy(out=stage[:, 0:4 * ncol], in_=sc[ch][:, 0:4 * ncol]), [f"sc{ch}"], [sname])
                    ACT(lambda e, ch=ch, stage=stage, ptb=ptb: e.activation(out=ptb[:, 0:4 * ncol], in_=stage[:, 0:4 * ncol], func=AF.Exp), [sname], [pname])
                if mask is not None:
                    DVE(lambda e, ptb=ptb: e.tensor_tensor(out=V(ptb, 0, [[128, 4], [1, 128]]), in0=V(ptb, 0, [[128, 4], [1, 128]]), in1=V(mask, 0, [[0, 4], [1, 128]]), op=ALU.mult), [pname, mname], [pname])
                for m in range(4):
                    h = 2 * ch + m // 2
                    PE(lambda e, ch=ch, m=m, h=h, ptb=ptb: e.matmul(out=V(ot[ch], m * 128 + qcol, [[1, ncol]], npart=65), lhsT=vtile[:, h * 65:(h + 1) * 65], rhs=ptb[:, m * ncol:(m + 1) * ncol], start=(first and m == 0), stop=last, skip_group_check=True),
                       [vname, pname], [f"ot{ch}"])

        if kind == 0:
            kvsets = [(ktB[j][:, :], f"ktB{j}", vB[j][:, :], f"vB{j}") for j in range(4)]
            for kbi in range(i):
                kt_ap, ktn, v_ap, vnm = kvsets[kbi % 4]
                DMA(lambda e, kbi=kbi, kt_ap=kt_ap: e.dma_start(out=kt_ap.rearrange("p (c n) -> p c n", c=2), in_=k.ktD.ap().rearrange("p (c n) -> p c n", c=2)[:, :, kbi * 128:(kbi + 1) * 128]), ["ktD"], [ktn])
                DMA(lambda e, kbi=kbi, v_ap=v_ap: e.dma_start(out=v_ap, in_=k.vD[kbi * 128:(kbi + 1) * 128, :]), ["vD"], [vnm], q="pool")
                kblock(kt_ap, ktn, v_ap, vnm, 128, 0, kbi == 0, False, None, None)
            kblock(kTt, "kTt", vat, "vat", 128, 0, i == 0, True, trilb, "trilb")
        else:
            kblock(kTt, "kTt", vat, "vat", 128, 0, True, False, msb, "msb")
            for s_ in range(16):
                for j in range(16):
                    col = s_ * 16 + j
                    DMA(lambda e, col=col: e.indirect_dma_start(out=kpg[:, :], out_offset=None, in_=k.cache_kl[l][:, :], in_offset=bass.IndirectOffsetOnAxis(ap=idx[:, col:col + 1], axis=0)), ["idx"], ["kpg"], q="pool")
                    DMA(lambda e, col=col: e.indirect_dma_start(out=vpg[:, :], out_offset=None, in_=k.cache_vl[l][:, :], in_offset=bass.IndirectOffsetOnAxis(ap=idx[:, col:col + 1], axis=0)), ["idx"], ["vpg"], q="pool")
                    DVE(lambda e: e.tensor_copy(out=kpb[:, :], in_=kpg[:, :]), ["kpg"], ["kpb"])
                    DVE(lambda e: e.tensor_copy(out=V(vpa, 0, [[65, 4], [1, 64]]), in_=V(vpg, 0, [[64, 4], [1, 64]])), ["vpg"], ["vpa"])
                    for ch in range(2):
                        PE(lambda e, ch=ch: e.transpose(out=tpb[:, ch * 128:(ch + 1) * 128], in_=kpb[:, ch * 128:(ch + 1) * 128], identity=identb[:, :]), ["kpb", "identb"], ["tpb"])
                    DVE(lambda e: e.tensor_copy(out=kpT[:, :], in_=tpb[:, 0:256]), ["tpb"], ["kpT"])
                    kblock(kpT, "kpT", vpa, "vpa", 8, 8 * s_, False, j == 15, None, None, s_=s_)
        ck(f'attn{l}{kind}{i}')
        for ch in range(2):
            DVE(lambda e, ch=ch: e.tensor_copy(out=t1[0:65, ch * 512:(ch + 1) * 512], in_=ot[ch][0:65, :]), [f"ot{ch}"], ["t1"])
            for m in range(4):
                PE(lambda e, ch=ch, m=m: e.transpose(out=sc[ch][:, m * 65:(m + 1) * 65], in_=t1[0:65, ch * 512 + m * 128:ch * 512 + (m + 1) * 128], identity=identf[0:65, 0:65]), ["t1", "identf"], [f"sc{ch}"])
            DVE(lambda e, ch=ch: e.tensor_copy(out=oa[:, ch * 260:(ch + 1) * 260], in_=sc[ch][:, 0:260]), [f"sc{ch}"], ["oa"])
        DVE(lambda e: e.reciprocal(out=rl[:, 0:8], in_=V(oa, 64, [[65, 8]])), ["oa"], ["rl"])
        for h in range(4):
            b0 = h * 130
            DVE(lambda e, h=h, b0=b0: e.tensor_scalar(out=sq[:, 0:64], in0=oa[:, b0 + 65:b0 + 129], scalar1=rl[:, 2 * h + 1:2 * h + 2], scalar2=lamt[:, 0:1], op0=ALU.mult, op1=ALU.mult), ["oa", "rl", "lamt"], ["sq"])
            DVE(lambda e, h=h, b0=b0: e.scalar_tensor_tensor(out=oat[:, h * 64:(h + 1) * 64], in0=oa[:, b0:b0 + 64], scalar=rl[:, 2 * h:2 * h + 1], in1=sq[:, 0:64], op0=ALU.mult, op1=ALU.add), ["oa", "rl", "sq"], ["oat"])
        rms_heads(oat, "oat", 7, 0, 0)
        ck(f'epi{l}{kind}{i}')
        if kind == 0 and DBG_CAT:
            DMA(lambda e: e.dma_start(out=k.dbg_cat[l, i * 128:(i + 1) * 128, :], in_=cat[:, :]), ["cat"], [])
        for kc in range(8):
            PE(lambda e, kc=kc: e.transpose(out=tpb[:, kc * 128:(kc + 1) * 128], in_=cat[:, kc * 128:(kc + 1) * 128], identity=identb[:, :]), ["cat", "identb"], ["tpb"])
        DVE(lambda e: e.tensor_copy(out=catT[:, :], in_=tpb[:, :]), ["tpb"], ["catT"])
        for n in range(2):
            for kc in range(8):
                PE(lambda e, kc=kc, n=n: e.matmul(out=zz[n][:, :], lhsT=catT[:, kc * 128:(kc + 1) * 128], rhs=wout[:, kc * D + n * 512:kc * D + (n + 1) * 512], start=(kc == 0), stop=(kc == 7)), ["catT", "wout"], [zn[n]])
            DVE(lambda e, n=n: e.tensor_tensor(out=t1[:, n * 512:(n + 1) * 512], in0=zz[n][:, :], in1=mod[:, 2 * D + n * 512:2 * D + (n + 1) * 512], op=ALU.mult), [zn[n], "mod"], ["t1"])
        DVE(lambda e: e.scalar_tensor_tensor(out=t1[:, :], in0=xt[:, :], scalar=ALPHA, in1=t1[:, :], op0=ALU.mult, op1=ALU.add), ["xt", "t1"], ["t1"])
        layer_norm(t1, D, lnp[:, 0:D], lnp[:, D:2 * D], "t1", ["lnp"], xt, "xt")
        if kind == 0:
            dst = (k.y1 if l == 0 else k.y_p)[i * 128:(i + 1) * 128, :]
            DMA(lambda e: e.dma_start(out=dst, in_=t1[:, :]), ["t1"], ["y1"] if l == 0 else [])
        else:
            dst = (k.y1s if l == 0 else k.y_s)[:, :]
            DMA(lambda e: e.dma_start(out=dst, in_=t1[:, :]), ["t1"], ["y1s"] if l == 0 else [])

    compute_mod(k.cp)
    ck(f'mod{l}')
    for i in range(k.NTILES):
        tile(0, i)
    compute_mod(k.cs)
    tile(1, 0)


def _consts():
    c = {}
    a = np.arange(128)
    s_, t_ = a[:, None], a[None, :]
    c["c_ident"] = np.eye(128, dtype=np.float32)
    c["c_tril"] = (s_ <= t_).astype(np.float32)
    c["c_ms"] = ((s_ // 8 == t_ // 8) & (s_ <= t_)).astype(np.float32)
    mc, ml, cm, sg = [], [], [], []
    for L in (16, 8):
        same = (s_ // L == t_ // L)
        mc.append((same & (s_ <= t_)).astype(np.float32))
        ml.append(same.astype(np.float32))
        nseg = 128 // L
        m = np.zeros((128, 8, 128), np.float32)
        for pp in range(128):
            for j in range(nseg // 2):
                seg = 2 * j + pp // 64
                m[pp, j, seg * L:(seg + 1) * L] = 1.0
        cm.append(m.reshape(128, 1024))
        g = np.zeros((128, 16, 64), np.float32)
        for seg in range(nseg):
            g[seg * L:(seg + 1) * L, seg, :] = 1.0
        sg.append(g.reshape(128, 1024))
    c["c_mcum"] = np.stack(mc); c["c_mlast"] = np.stack(ml); c["c_matt"] = np.stack(mc)
    c["c_cm"] = np.stack(cm); c["c_seg"] = np.stack(sg)
    rm = np.zeros((128, 4), np.float32)
    for m in range(4):
        rm[32 * m:32 * m + 32, m] = SCALE
    c["c_rowm"] = rm
    c["pos_p"] = (128.0 * np.arange(NT)[None, :] + a[:, None]).astype(np.float32)
    c["pos_s"] = (2048.0 + (a % 8)).astype(np.float32).reshape(128, 1)
    inv = (np.float32(10000.0) ** (-np.arange(16, dtype=np.float32) * np.float32(2.0) / np.float32(32.0))).astype(np.float32)
    c["invf"] = np.tile(inv[None, :], (128, 1)).astype(np.float32)
    c["iota_p"] = a.astype(np.float32).reshape(128, 1)
    return c


_PROG = None


def kernel(x_prompt, x_sample, cache_k, cache_v, state_conv, state_hgrn, page_table, c_prompt, c_sample,
           w_ada, b_ada, w_in, lam_qk, attn_norm_g, sg_norm_g, sg_norm_b, w_s, b_s, conv_w, conv_b,
           conv_norm_g, conv_norm_b, w_pw, lower_bounds, hgrn_norm_g, w_out, ln_g, ln_b):
    global _PROG
    if _PROG is None:
        _PROG = build_program()
    k = _PROG
    f = lambda a: np.ascontiguousarray(np.asarray(a, dtype=np.float32))
    consts = _consts()
    ck = f(cache_k).reshape(2, NPOOL_ROWS, 256)[:, :POOL_ROWS_RUN]
    cv = f(cache_v).reshape(2, NPOOL_ROWS, 256)[:, :POOL_ROWS_RUN]
    shared = dict(cache_k0=ck[0], cache_k1=ck[1], cache_v0=cv[0], cache_v1=cv[1], w_ada=f(w_ada), b_ada=f(b_ada), w_in=f(w_in), lam_qk=f(lam_qk).reshape(2, 128),
                  attn_g=f(attn_norm_g), sg_g=f(sg_norm_g), sg_b=f(sg_norm_b), w_s=f(w_s), b_s=f(b_s), conv_w=f(conv_w),
                  conv_b=f(conv_b), cn_g=f(conv_norm_g), cn_b=f(conv_norm_b), w_pw=f(w_pw), lowb=f(lower_bounds),
                  hg_g=f(hgrn_norm_g), w_out=f(w_out), ln_g=f(ln_g), ln_b=f(ln_b))
    shared.update(consts)
    xp_, xs_ = f(x_prompt), f(x_sample)
    cp_, cs_ = f(c_prompt), f(c_sample)
    pt = np.asarray(page_table).astype(np.int32)
    sc_, sh_ = f(state_conv), f(state_hgrn)
    in_maps = []
    for r in range(NCORES_RUN):
        b = r % 2
        m = dict(shared)
        m["xp"] = xp_[b]
        m["xs"] = np.ascontiguousarray(xs_[16 * r:16 * r + 16].reshape(128, D))
        m["cp"] = np.ascontiguousarray(np.tile(cp_[b][None, :], (128, 1)))
        m["cs"] = np.ascontiguousarray(np.repeat(cs_[16 * r:16 * r + 16], 8, axis=0))
        m["ptab"] = np.ascontiguousarray(np.tile(pt[16 * r:16 * r + 16].reshape(1, 256), (128, 1)))
        m["st_conv"] = np.ascontiguousarray(sc_[:, 16 * r:16 * r + 16])
        m["st_hgrn"] = np.ascontiguousarray(sh_[:, 16 * r:16 * r + 16])
        in_maps.append(m)
    res = run_bass_kernel_spmd(k.nc, in_maps, core_ids=list(range(NCORES_RUN))).results
    res = list(res) + [res[0]] * (8 - len(res))
    global DBG_RES
    DBG_RES = res
    y_prompt = np.stack([res[b]["y_p"] for b in range(2)])
    y_sample = np.concatenate([res[r]["y_s"].reshape(16, 8, D) for r in range(8)], 0)
    nk_p = np.stack([res[b]["nk_p"] for b in range(2)], 1).reshape(2, 2, 8192, 4, 64)
    nv_p = np.stack([res[b]["nv_p"] for b in range(2)], 1).reshape(2, 2, 8192, 4, 64)
    nk_s = np.concatenate([res[r]["nk_s"].reshape(2, 16, 8, 4, 64) for r in range(8)], 1)
    nv_s = np.concatenate([res[r]["nv_s"].reshape(2, 16, 8, 4, 64) for r in range(8)], 1)
    nch = np.concatenate([res[r]["nch_s"].reshape(2, 16, 8, 256) for r in range(8)], 1)
    ncv_p = np.stack([res[b]["ncv_p"] for b in range(2)], 1)
    ncv_s = np.concatenate([res[r]["ncv_s"] for r in range(8)], 1)
    nh_p = np.stack([res[b]["nh_p"] for b in range(2)], 1)
    nh_s = np.concatenate([res[r]["nh_s"] for r in range(8)], 1)
    outs = (y_prompt, y_sample, nk_p, nv_p, nk_s, nv_s, nch, ncv_p, ncv_s, nh_p, nh_s)
    return tuple(np.ascontiguousarray(o, dtype=np.float32) for o in outs)
```

```python
import numpy as np
import concourse.bass as bass
import concourse.mybir as mybir
from concourse.bass_utils import run_bass_kernel_spmd

F32 = mybir.dt.float32
BF16 = mybir.dt.bfloat16
I32 = mybir.dt.int32
AF = mybir.ActivationFunctionType
ALU = mybir.AluOpType
AX = mybir.AxisListType


PSUM_NAMES = {"zA", "zB", "tpb", "sc0", "sc1", "ot0", "ot1", "misc"}


class Res:
    __slots__ = ("name", "w", "r")

    def __init__(self, name):
        self.name = name
        self.w = None
        self.r = []


class Op:
    __slots__ = ("eng", "fn", "deps", "idx", "dma", "lane", "lane_cnt", "marked", "val", "waits", "lane_wait")

    def __init__(self, eng, fn, dma):
        self.eng = eng
        self.fn = fn
        self.dma = dma
        self.deps = []
        self.marked = False
        self.val = 0
        self.waits = []
        self.lane = None
        self.lane_cnt = 0
        self.lane_wait = 0


class Prog:
    ENGS = ("pe", "act", "dve", "pool", "sp")
    NLANES = {"sp": 8, "pool": 6, "act": 2}

    def __init__(self, nc):
        self.nc = nc
        self.ops = {e: [] for e in self.ENGS}
        self.lane_rr = {e: 0 for e in self.NLANES}
        self.lane_count = {}
        self.nres = 0

    def res(self, name=None):
        self.nres += 1
        return Res(name or f"r{self.nres}")

    def add(self, eng, fn, reads=(), writes=(), dma=False):
        op = Op(eng, fn, dma)
        deps = []
        xr = [r for r in reads if r.name in PSUM_NAMES]
        if xr:
            reads = [r for r in reads if r.name not in PSUM_NAMES]
            writes = list(writes) + [r for r in xr if r not in writes]
        for r in reads:
            if r.w is not None:
                deps.append(r.w)
        for w in writes:
            if w.w is not None:
                deps.append(w.w)
            deps.extend(w.r)
        for r in reads:
            r.r.append(op)
        for w in writes:
            w.w = op
            w.r = []
        seen = set()
        for d in deps:
            if d is op or id(d) in seen:
                continue
            seen.add(id(d))
            if (not d.dma) and (not dma) and d.eng == eng and eng == "pe":
                continue
            op.deps.append(d)
        op.idx = len(self.ops[eng])
        if dma:
            n = self.NLANES[eng]
            k = self.lane_rr[eng]
            self.lane_rr[eng] = (k + 1) % n
            op.lane = (eng, k)
            c = self.lane_count.get(op.lane, 0)
            op.lane_wait = c
            op.lane_cnt = c + 1
            self.lane_count[op.lane] = c + 1
        self.ops[eng].append(op)
        return op

    def pe(self, fn, reads=(), writes=()):
        return self.add("pe", fn, reads, writes)

    def act(self, fn, reads=(), writes=()):
        return self.add("act", fn, reads, writes)

    def dve(self, fn, reads=(), writes=()):
        return self.add("dve", fn, reads, writes)

    def pool(self, fn, reads=(), writes=()):
        return self.add("pool", fn, reads, writes)

    def dma(self, fn, reads=(), writes=(), q="sp"):
        return self.add(q, fn, reads, writes, dma=True)

    def finalize_and_emit(self):
        nc = self.nc
        for e in self.ENGS:
            waited_idx = {}
            waited_lane = {}
            for op in self.ops[e]:
                need_idx = {}
                need_lane = {}
                for d in op.deps:
                    if d.dma:
                        if need_lane.get(d.lane, 0) < d.lane_cnt:
                            need_lane[d.lane] = d.lane_cnt
                    else:
                        if need_idx.get(d.eng, -1) < d.idx:
                            need_idx[d.eng] = d.idx
                if op.dma and op.lane_wait > 0:
                    if need_lane.get(op.lane, 0) < op.lane_wait:
                        need_lane[op.lane] = op.lane_wait
                op.waits = []
                for te, ix in need_idx.items():
                    if waited_idx.get(te, -1) >= ix:
                        continue
                    if te == e and not op.dma and ix < op.idx and False:
                        continue
                    waited_idx[te] = ix
                    tgt = self.ops[te][ix]
                    tgt.marked = True
                    op.waits.append(("op", tgt))
                for ln, cnt in need_lane.items():
                    if waited_lane.get(ln, 0) >= cnt:
                        continue
                    waited_lane[ln] = cnt
                    op.waits.append(("lane", ln, cnt))
        for e in self.ENGS:
            v = 0
            for op in self.ops[e]:
                if op.dma:
                    continue
                if op.marked:
                    v += 1
                    op.val = v
        self.stats = {e: len(self.ops[e]) for e in self.ENGS}
        sems = {}
        import contextlib
        with contextlib.ExitStack() as st:
            for e in self.ENGS:
                sems[e] = st.enter_context(nc.semaphore(f"sem_{e}"))
            lanes = {}
            for q, n in self.NLANES.items():
                for k in range(n):
                    lanes[(q, k)] = st.enter_context(nc.semaphore(f"lane_{q}{k}"))
            block = st.enter_context(nc.Block())

            def emit(e, eng):
                for op in self.ops[e]:
                    for w in op.waits:
                        if w[0] == "op":
                            eng.wait_ge(sems[w[1].eng], w[1].val)
                        else:
                            eng.wait_ge(lanes[w[1]], 16 * w[2])
                    ins = op.fn(eng)
                    if op.dma:
                        ins.then_inc(lanes[op.lane], 16)
                    elif op.marked:
                        ins.then_inc(sems[e], 1)
                if e in self.NLANES:
                    for k in range(self.NLANES[e]):
                        c = self.lane_count.get((e, k), 0)
                        if c:
                            eng.wait_ge(lanes[(e, k)], 16 * c)

            @block.tensor
            def _(eng):
                emit("pe", eng)

            @block.scalar
            def _(eng):
                emit("act", eng)

            @block.vector
            def _(eng):
                emit("dve", eng)

            @block.gpsimd
            def _(eng):
                emit("pool", eng)

            @block.sync
            def _(eng):
                emit("sp", eng)


D = 1024
DIN = 3584
NT = 64
NT_RUN = 64
NPOOL_ROWS = 2560 * 128
POOL_ROWS_RUN = NPOOL_ROWS
NCORES_RUN = 8
ALPHA = (2.0 * 2) ** 0.25
EPS = 1e-5
SCALE = 32 ** -0.5
import math


def V(t, off, dims, p0=0, npart=128):
    ps = t[:].ap[0][0]
    return bass.AP(t, p0 * ps + off, [[ps, npart]] + [list(d) for d in dims])


def DV(t, off, dims):
    return bass.AP(t, off, [list(d) for d in dims])


class K:
    pass


class _Stop(Exception):
    pass


STOP_AT = None
DBG_MISC = False
ACT_PSUM = True
DBG_CAT = False
DBG_RES = None
DBG_SER = False
DBG_SER_LIST = ["kTt", "gate"]


def ck(tag):
    if STOP_AT is not None and tag == STOP_AT:
        raise _Stop()


def build_program():
    nc = bass.Bass("TRN2", target_bir_lowering=False)
    p = Prog(nc)
    k = K()
    R = {}

    def res(n):
        if n not in R:
            R[n] = p.res(n)
        return R[n]

    def din(name, shape, dt=F32):
        return nc.dram_tensor(name, list(shape), dt, kind="ExternalInput")

    def dout(name, shape, dt=F32):
        return nc.dram_tensor(name, list(shape), dt, kind="ExternalOutput")

    xp = din("xp", [8192, D]); xs = din("xs", [128, D])
    cp = din("cp", [128, D]); cs = din("cs", [128, D])
    pos_p = din("pos_p", [128, NT]); pos_s = din("pos_s", [128, 1]); invf = din("invf", [128, 16])
    ptab = din("ptab", [128, 256], I32); iota_p = din("iota_p", [128, 1])
    cache_kl = [din("cache_k0", [POOL_ROWS_RUN, 256]), din("cache_k1", [POOL_ROWS_RUN, 256])]; cache_vl = [din("cache_v0", [POOL_ROWS_RUN, 256]), din("cache_v1", [POOL_ROWS_RUN, 256])]
    st_conv = din("st_conv", [2, 16, 30, 256]); st_hgrn = din("st_hgrn", [2, 16, 4, 64, 64])
    w_ada = din("w_ada", [2, D, 3 * D]); b_ada = din("b_ada", [2, 3 * D]); w_in = din("w_in", [2, D, DIN])
    lam_qk = din("lam_qk", [2, 128]); attn_g = din("attn_g", [2, 64])
    sg_g = din("sg_g", [2, 256]); sg_b = din("sg_b", [2, 256]); w_s = din("w_s", [2, 4, 128, 128]); b_s = din("b_s", [2, 4, 128])
    conv_w = din("conv_w", [2, 31, 256]); conv_b = din("conv_b", [2, 256]); cn_g = din("cn_g", [2, 256]); cn_b = din("cn_b", [2, 256])
    w_pw = din("w_pw", [2, 256, 256]); lowb = din("lowb", [2, 256]); hg_g = din("hg_g", [2, 64])
    w_out = din("w_out", [2, D, D]); ln_g = din("ln_g", [2, D]); ln_b = din("ln_b", [2, D])
    c_ident = din("c_ident", [128, 128]); c_tril = din("c_tril", [128, 128]); c_ms = din("c_ms", [128, 128])
    c_mcum = din("c_mcum", [2, 128, 128]); c_mlast = din("c_mlast", [2, 128, 128]); c_matt = din("c_matt", [2, 128, 128])
    c_cm = din("c_cm", [2, 128, 8 * 128]); c_seg = din("c_seg", [2, 128, 16 * 64]); c_rowm = din("c_rowm", [128, 4])

    y_p = dout("y_p", [8192, D]); y_s = dout("y_s", [128, D])
    nk_p = dout("nk_p", [2, 8192, 256]); nv_p = dout("nv_p", [2, 8192, 256])
    nk_s = dout("nk_s", [2, 128, 256]); nv_s = dout("nv_s", [2, 128, 256]); nch_s = dout("nch_s", [2, 128, 256])
    ncv_p = dout("ncv_p", [2, 30, 256]); ncv_s = dout("ncv_s", [2, 16, 30, 256])
    nh_p = dout("nh_p", [2, 4, 64, 64]); nh_s = dout("nh_s", [2, 16, 4, 64, 64])
    dbg_cat = dout("dbg_cat", [2, 8192, D], BF16) if DBG_CAT else None
    y1s = dout("y1s_scratch", [128, D])
    NTILES = NT_RUN
    y1 = dout("y1_scratch", [8192, D])
    ktD = dout("ktD", [128, 2 * 8192], BF16)
    vD = dout("vD", [8192, 260], BF16)

    def sb(name, free, dt=F32):
        return nc.alloc_sbuf_tensor(name, [128, free], dt)

    def ps(name, free, dt=F32):
        return nc.alloc_psum_tensor(name, [128, free], dt)

    zA = ps("zA", 512); zB = ps("zB", 512); tpb = ps("tpb", 1024, BF16)
    sc = [ps("sc0", 512), ps("sc1", 512)]
    ot = [ps("ot0", 512), ps("ot1", 512)]
    misc = ps("misc", 512)

    win = sb("win", 8 * DIN, BF16)
    wout = sb("wout", 8 * D, BF16)
    wpw = sb("wpw", 2 * 256, BF16)
    stg = [sb("stg0", 512)]
    ktB = [sb(f"ktB{j}", 256, BF16) for j in range(4)]
    vB = [sb(f"vB{j}", 260, BF16) for j in range(4)]
    identf = sb("identf", 128); identb = sb("identb", 128, BF16)
    trilb = sb("trilb", 128, BF16); msb = sb("msb", 128, BF16)
    mcum = sb("mcum", 2 * 128); mlast = sb("mlast", 2 * 128); mattb = sb("mattb", 2 * 512, BF16)
    cmb = sb("cmb", 2 * 1024, BF16); segb = sb("segb", 2 * 1024, BF16); rowm = sb("rowm", 4)
    mod = sb("mod", 3 * D)
    cosp = sb("cosp", NT * 16); sinp = sb("sinp", NT * 16); coss = sb("coss", 16); sins = sb("sins", 16)
    prm = sb("prm", 10 * 256)
    lnp = sb("lnp", 2 * D)
    lamt = sb("lamt", 8)
    wmT = sb("wmT", 2 * 4 * 128, BF16)
    bsE = sb("bsE", 2 * 256)
    cwT = sb("cwT", 2 * 31)
    idx = sb("idx", 256, I32); idxf = sb("idxf", 256)
    xt = sb("xt", D); hb = sb("hb", D, BF16); hT = sb("hT", D, BF16); t1 = sb("t1", D)
    qk = sb("qk", 512); qkr = sb("qkr", 512); qb = sb("qb", 256, BF16); kb = sb("kb", 256, BF16)
    qblk = sb("qblk", 2 * 4 * 128, BF16)
    gate = sb("gate", 4 * 256)
    vnb = sb("vnb", 256, BF16)
    hcT = sb("hcT", 2 * 640, BF16)
    cyb = sb("cyb", 256, BF16); cyT = sb("cyT", 256, BF16)
    dq = sb("dq", 256); df = sb("df", 256); lf = sb("lf", 256); kd = sb("kd", 256); vdb = sb("vdb", 256, BF16)
    cl = sb("cl", 512); e1 = sb("e1", 256); e2 = sb("e2", 256); e3 = sb("e3", 256)
    qdup = sb("qdup", 512, BF16); kt_ = sb("kt_", 256, BF16); kpdup = sb("kpdup", 512, BF16); eldup = sb("eldup", 512)
    q2T = sb("q2T", 512, BF16); khT = sb("khT", 512, BF16); a2T = sb("a2T", 512)
    qexp = sb("qexp", 1024, BF16)
    vexp = sb("vexp", 1024, BF16)
    attm = sb("attm", 512, BF16)
    sst = sb("sst", 8 * 64, BF16)
    sstf = sb("sstf", 8 * 64)
    Sf = sb("Sf", 4 * 64)
    od = sb("od", 256); sq = sb("sq", 256); ss4 = sb("ss4", 8)
    cat = sb("cat", D, BF16); catT = sb("catT", D, BF16)
    pt_ = sb("pt_", 512, BF16)
    oa = sb("oa", 2 * 260); rl = sb("rl", 8); oat = sb("oat", 256)
    st2 = sb("st2", 8)
    kpg = sb("kpg", 256); vpg = sb("vpg", 256); kpb = sb("kpb", 256, BF16); kpT = sb("kpT", 256, BF16); vpa = sb("vpa", 260, BF16)


    def rs(names):
        return [res(n) for n in names]

    def ACT(fn, r, w): p.act(fn, rs(r), rs(w))
    def DVE(fn, r, w): p.dve(fn, rs(r), rs(w))
    def POOL(fn, r, w): p.pool(fn, rs(r), rs(w))
    def PE(fn, r, w): p.pe(fn, rs(r), rs(w))
    def DMA(fn, r, w, q="sp"): p.dma(fn, rs(r), rs(w), q=q)

    def bcrow(dt_, off, n):
        return DV(dt_, off, [[0, 128], [1, n]])

    def load_const(dst, nfree, src_ap, name, cast_to=None, tmp=None):
        if cast_to is None:
            DMA(lambda e: e.dma_start(out=dst[:, 0:nfree], in_=src_ap), [], [name])
        else:
            DMA(lambda e: e.dma_start(out=tmp[:, 0:nfree], in_=src_ap), [], ["t1"])
            DVE(lambda e: e.tensor_copy(out=dst[:, 0:nfree], in_=tmp[:, 0:nfree]), ["t1"], [name])

    stgb = [sb("stgb0", 512, BF16), sb("stgb1", 512, BF16)]
    kTt = sb("kTt", 256, BF16); vat = sb("vat", 260, BF16)
    TWO_PI = 2.0 * math.pi
    load_const(identf, 128, c_ident[:, :], "identf")
    DVE(lambda e: e.tensor_copy(out=identb[:], in_=identf[:]), ["identf"], ["identb"])
    load_const(trilb, 128, c_tril[:, :], "trilb", BF16, t1)
    load_const(msb, 128, c_ms[:, :], "msb", BF16, t1)
    load_const(mcum, 256, c_mcum.ap().rearrange("k p t -> p k t"), "mcum")
    load_const(mlast, 256, c_mlast.ap().rearrange("k p t -> p k t"), "mlast")
    for kk in range(2):
        DMA(lambda e, kk=kk: e.dma_start(out=t1[:, 0:128], in_=c_matt[kk, :, :]), [], ["t1"])
        DVE(lambda e, kk=kk: e.tensor_copy(out=V(mattb, kk * 512, [[128, 4], [1, 128]]), in_=V(t1, 0, [[0, 4], [1, 128]])), ["t1"], ["mattb"])
        DMA(lambda e, kk=kk: e.dma_start(out=t1[:, 0:1024], in_=c_cm[kk, :, :]), [], ["t1"])
        DVE(lambda e, kk=kk: e.tensor_copy(out=cmb[:, kk * 1024:(kk + 1) * 1024], in_=t1[:, 0:1024]), ["t1"], ["cmb"])
        DMA(lambda e, kk=kk: e.dma_start(out=t1[:, 0:1024], in_=c_seg[kk, :, :]), [], ["t1"])
        DVE(lambda e, kk=kk: e.tensor_copy(out=segb[:, kk * 1024:(kk + 1) * 1024], in_=t1[:, 0:1024]), ["t1"], ["segb"])
    load_const(rowm, 4, c_rowm[:, :], "rowm")
    DMA(lambda e: e.dma_start(out=t1[:, 0:NT], in_=pos_p[:, :]), [], ["t1"])
    DMA(lambda e: e.dma_start(out=t1[:, 64:65], in_=pos_s[:, :]), [], ["t1"])
    DMA(lambda e: e.dma_start(out=t1[:, 128:144], in_=invf[:, :]), [], ["t1"])
    DVE(lambda e: e.tensor_tensor(out=V(xt, 0, [[16, NT], [1, 16]]), in0=V(t1, 0, [[1, NT], [0, 16]]), in1=V(t1, 128, [[0, NT], [1, 16]]), op=ALU.mult), ["t1"], ["xt"])
    DVE(lambda e: e.tensor_tensor(out=V(qk, 0, [[1, 16]]), in0=V(t1, 64, [[0, 16]]), in1=V(t1, 128, [[1, 16]]), op=ALU.mult), ["t1"], ["qk"])

    C1 = 6.28125
    C2 = TWO_PI - C1

    def sincos(dsin, dcos, sname, cname, src_ap, srcname, n):
        r = mod[:, 0:n]; ki = mod[:, 1024:1024 + n].bitcast(I32); kf = mod[:, 2048:2048 + n]; m = mod[:, 1024:1024 + n]
        DVE(lambda e: e.tensor_scalar(out=kf, in0=src_ap, scalar1=1.0 / TWO_PI, scalar2=None, op0=ALU.mult), [srcname], ["mod"])
        DVE(lambda e: e.tensor_copy(out=ki, in_=kf), ["mod"], ["mod"])
        DVE(lambda e: e.tensor_copy(out=kf, in_=ki), ["mod"], ["mod"])
        DVE(lambda e: e.scalar_tensor_tensor(out=r, in0=kf, scalar=-C1, in1=src_ap, op0=ALU.mult, op1=ALU.add), ["mod", srcname], ["mod"])
        DVE(lambda e: e.scalar_tensor_tensor(out=r, in0=kf, scalar=-C2, in1=r, op0=ALU.mult, op1=ALU.add), ["mod"], ["mod"])
        for which in range(2):
            if which == 1:
                DVE(lambda e: e.tensor_scalar_add(out=r, in0=r, scalar1=0.5 * math.pi), ["mod"], ["mod"])
            DVE(lambda e: e.tensor_scalar(out=m, in0=r, scalar1=math.pi, scalar2=None, op0=ALU.is_gt), ["mod"], ["mod"])
            DVE(lambda e: e.scalar_tensor_tensor(out=r, in0=m, scalar=-TWO_PI, in1=r, op0=ALU.mult, op1=ALU.add), ["mod"], ["mod"])
            DVE(lambda e: e.tensor_scalar(out=m, in0=r, scalar1=-math.pi, scalar2=None, op0=ALU.is_lt), ["mod"], ["mod"])
            DVE(lambda e: e.scalar_tensor_tensor(out=r, in0=m, scalar=TWO_PI, in1=r, op0=ALU.mult, op1=ALU.add), ["mod"], ["mod"])
            dst, dn = (dsin, sname) if which == 0 else (dcos, cname)
            ACT(lambda e, dst=dst: e.activation(out=dst[:, 0:n], in_=r, func=AF.Sin), ["mod"], [dn])
    sincos(sinp, cosp, "sinp", "cosp", xt[:, 0:NT * 16], "xt", NT * 16)
    sincos(sins, coss, "sins", "coss", qk[:, 0:16], "qk", 16)
    DMA(lambda e: e.dma_start(out=idx[:, :], in_=ptab[:, :]), [], ["idx"])
    DMA(lambda e: e.dma_start(out=st2[:, 0:1], in_=iota_p[:, :]), [], ["st2"])
    DVE(lambda e: e.tensor_copy(out=idxf[:, :], in_=idx[:, :]), ["idx"], ["idxf"])
    DVE(lambda e: e.tensor_scalar(out=idxf[:, :], in0=idxf[:, :], scalar1=128.0, scalar2=st2[:, 0:1], op0=ALU.mult, op1=ALU.add), ["idxf", "st2"], ["idxf"])
    DVE(lambda e: e.tensor_copy(out=idx[:, :], in_=idxf[:, :]), ["idxf"], ["idx"])
    for tv, nm in ((vB[0], "vB0"), (vB[1], "vB1"), (vpa, "vpa"), (vat, "vat")):
        DVE(lambda e, tv=tv: e.memset(tv[:, :], 1.0), [], [nm])

    k.__dict__.update(locals())
    try:
        ck('consts')
        for l in range(2):
            emit_layer(k, l)
    except _Stop:
        pass
    p.finalize_and_emit()
    return k


def emit_layer(k, l):
    g = k.__dict__
    p = k.p; nc = k.nc
    ACT, DVE, POOL, PE, DMA, res = k.ACT, k.DVE, k.POOL, k.PE, k.DMA, k.res
    (zA, zB, tpb, sc, ot, misc, win, wout, wpw, stg, stgb, identf, identb, trilb, msb, mcum, mlast, mattb, cmb, segb, rowm, mod,
     cosp, sinp, coss, sins, prm, lnp, lamt, wmT, bsE, cwT, idx, xt, hb, hT, t1, qk, qkr, qb, kb, qblk, gate, vnb,
     hcT, cyb, cyT, dq, df, lf, kd, vdb, cl, e1, e2, e3, qdup, kt_, kpdup, eldup, q2T, khT, a2T, qexp, vexp, attm,
     sst, sstf, Sf, od, sq, ss4, cat, catT, pt_, oa, rl, oat, st2, kpg, vpg, kpb, kpT, vpa, kTt, vat, ktB, vB) = [g[n] for n in (
        "zA zB tpb sc ot misc win wout wpw stg stgb identf identb trilb msb mcum mlast mattb cmb segb rowm mod "
        "cosp sinp coss sins prm lnp lamt wmT bsE cwT idx xt hb hT t1 qk qkr qb kb qblk gate vnb "
        "hcT cyb cyT dq df lf kd vdb cl e1 e2 e3 qdup kt_ kpdup eldup q2T khT a2T qexp vexp attm "
        "sst sstf Sf od sq ss4 cat catT pt_ oa rl oat st2 kpg vpg kpb kpT vpa kTt vat ktB vB").split()]
    lam_init = 0.8 - 0.6 * math.exp(-0.3 * l)
    zz = [zA, zB]; zn = ["zA", "zB"]

    cnt = [0]

    def load_w(dst, dst_off, dram, r0, c0, ncols, dname):
        i = 0; cnt[0] += 1
        DMA(lambda e: e.dma_start(out=stg[i][:, 0:ncols], in_=dram[l, r0:r0 + 128, c0:c0 + ncols]), [], [f"stg{i}"])
        POOL(lambda e: e.tensor_copy(out=dst[:, dst_off:dst_off + ncols], in_=stg[i][:, 0:ncols]), [f"stg{i}"], [dname])

    for kc in range(8):
        for n in range(7):
            load_w(win, kc * DIN + n * 512, k.w_in, kc * 128, n * 512, 512, "win")
        for n in range(2):
            load_w(wout, kc * D + n * 512, k.w_out, kc * 128, n * 512, 512, "wout")
    for ch in range(2):
        load_w(wpw, ch * 256, k.w_pw, ch * 128, 0, 256, "wpw")

    def bc(dram, n, slot, rep=1):
        if rep == 1:
            src = DV(dram, l * n, [[0, 128], [1, n]])
        else:
            src = DV(dram, l * n, [[0, 128], [0, rep], [1, n]])
        DMA(lambda e: e.dma_start(out=prm[:, slot * 256:(slot + 1) * 256] if rep == 1 else V(prm, slot * 256, [[n, rep], [1, n]]), in_=src), [], ["prm"])
    bc(k.sg_g, 256, 0); bc(k.sg_b, 256, 1); bc(k.conv_b, 256, 2); bc(k.cn_g, 256, 3); bc(k.cn_b, 256, 4)
    bc(k.attn_g, 64, 7, 4); bc(k.hg_g, 64, 8, 4)
    DVE(lambda e: e.tensor_scalar_mul(out=prm[:, 7 * 256:8 * 256], in0=prm[:, 7 * 256:8 * 256], scalar1=1.0 - lam_init), ["prm"], ["prm"])
    if l == 0:
        DVE(lambda e: e.memset(prm[:, 5 * 256:6 * 256], 0.0), [], ["prm"])
    else:
        DMA(lambda e: e.dma_start(out=prm[:, 5 * 256:6 * 256], in_=DV(k.lowb, 256, [[0, 128], [1, 256]])), [], ["prm"])
        DMA(lambda e: e.dma_start(out=prm[:, 9 * 256:10 * 256], in_=DV(k.lowb, 0, [[0, 128], [1, 256]])), [], ["prm"])
        DVE(lambda e: e.tensor_tensor(out=prm[:, 5 * 256:6 * 256], in0=prm[:, 5 * 256:6 * 256], in1=prm[:, 9 * 256:10 * 256], op=ALU.subtract), ["prm"], ["prm"])
        ACT(lambda e: e.activation(out=prm[:, 5 * 256:6 * 256], in_=prm[:, 5 * 256:6 * 256], func=AF.Sigmoid), ["prm"], ["prm"])
    DVE(lambda e: e.tensor_scalar(out=prm[:, 6 * 256:7 * 256], in0=prm[:, 5 * 256:6 * 256], scalar1=-1.0, scalar2=1.0, op0=ALU.mult, op1=ALU.add), ["prm"], ["prm"])
    DMA(lambda e: e.dma_start(out=lnp[:, 0:D], in_=DV(k.ln_g, l * D, [[0, 128], [1, D]])), [], ["lnp"])
    DMA(lambda e: e.dma_start(out=lnp[:, D:2 * D], in_=DV(k.ln_b, l * D, [[0, 128], [1, D]])), [], ["lnp"])
    DMA(lambda e: e.dma_start(out=e1[:, 0:128], in_=DV(k.lam_qk, l * 128, [[0, 128], [1, 128]])), [], ["e1"])
    DVE(lambda e: e.tensor_tensor(out=V(e2, 0, [[32, 2], [1, 32]]), in0=V(e1, 0, [[64, 2], [1, 32]]), in1=V(e1, 32, [[64, 2], [1, 32]]), op=ALU.mult), ["e1"], ["e2"])
    DVE(lambda e: e.reduce_sum(out=lamt[:, 1:3], in_=V(e2, 0, [[32, 2], [1, 32]]), axis=AX.X), ["e2"], ["lamt"])
    ACT(lambda e: e.activation(out=lamt[:, 1:3], in_=lamt[:, 1:3], func=AF.Exp), ["lamt"], ["lamt"])
    DVE(lambda e: e.tensor_tensor(out=lamt[:, 0:1], in0=lamt[:, 2:3], in1=lamt[:, 1:2], op=ALU.subtract), ["lamt"], ["lamt"])
    DVE(lambda e: e.tensor_scalar_add(out=lamt[:, 0:1], in0=lamt[:, 0:1], scalar1=-lam_init), ["lamt"], ["lamt"])
    for kind in range(2):
        for gi in range(4):
            if kind == 0:
                DMA(lambda e, gi=gi: e.dma_start(out=t1[:, 0:128], in_=k.w_s[l, gi, :, :]), [], ["t1"])
                PE(lambda e: e.transpose(out=sc[0][:, 0:128], in_=t1[:, 0:128], identity=identf[:, :]), ["t1", "identf"], ["sc0"])
                DVE(lambda e, gi=gi: e.tensor_tensor(out=wmT[:, gi * 128:(gi + 1) * 128], in0=sc[0][:, 0:128], in1=trilb[:, :], op=ALU.mult), ["sc0", "trilb"], ["wmT"])
            else:
                DVE(lambda e: e.memset(t1[:, 0:128], 0.0), [], ["t1"])
                for s_ in range(16):
                    DMA(lambda e, gi=gi, s_=s_: e.dma_start(out=t1[8 * s_:8 * s_ + 8, 8 * s_:8 * s_ + 8], in_=k.w_s[l, gi, 0:8, 0:8]), [], ["t1"])
                PE(lambda e: e.transpose(out=sc[0][:, 0:128], in_=t1[:, 0:128], identity=identf[:, :]), ["t1", "identf"], ["sc0"])
                DVE(lambda e, gi=gi: e.tensor_tensor(out=wmT[:, 512 + gi * 128:512 + (gi + 1) * 128], in0=sc[0][:, 0:128], in1=msb[:, :], op=ALU.mult), ["sc0", "msb"], ["wmT"])
        if kind == 0:
            DMA(lambda e: e.dma_start(out=st2[:, 4:8], in_=k.b_s[l].rearrange("g t -> t g"), allow_slow_non_contiguous=True), [], ["st2"])
        else:
            for s_ in range(16):
                DMA(lambda e, s_=s_: e.dma_start(out=st2[8 * s_:8 * s_ + 8, 4:8], in_=k.b_s[l, :, 0:8].rearrange("g t -> t g"), allow_slow_non_contiguous=True), [], ["st2"])
        DVE(lambda e, kind=kind: e.tensor_copy(out=V(bsE, kind * 256, [[64, 4], [1, 64]]), in_=V(st2, 4, [[1, 4], [0, 64]])), ["st2"], ["bsE"])
    for ch in range(2):
        DMA(lambda e, ch=ch: e.dma_start(out=cwT[:, ch * 31:(ch + 1) * 31], in_=k.conv_w[l, :, ch * 128:(ch + 1) * 128].rearrange("j c -> c j"), allow_slow_non_contiguous=True), [], ["cwT"])
    DVE(lambda e: e.memset(Sf[:, :], 0.0), [], ["Sf"])

    ck(f'params{l}')
    def layer_norm(x, n, gap, bap, xname, pnames, junk, jname):
        DVE(lambda e: e.memset(st2[:, 0:4], 0.0), [], ["st2"])
        DVE(lambda e: e.reduce_sum(out=st2[:, 0:1], in_=x[:, 0:n], axis=AX.X), [xname], ["st2"])
        ACT(lambda e: e.activation(out=junk[:, 0:n], in_=x[:, 0:n], func=AF.Square, accum_out=st2[:, 1:2]), [xname], [jname, "st2"])
        DVE(lambda e: e.tensor_scalar_mul(out=st2[:, 0:2], in0=st2[:, 0:2], scalar1=1.0 / n), ["st2"], ["st2"])
        DVE(lambda e: e.tensor_tensor(out=st2[:, 2:3], in0=st2[:, 0:1], in1=st2[:, 0:1], op=ALU.mult), ["st2"], ["st2"])
        DVE(lambda e: e.tensor_tensor(out=st2[:, 1:2], in0=st2[:, 1:2], in1=st2[:, 2:3], op=ALU.subtract), ["st2"], ["st2"])
        DVE(lambda e: e.tensor_scalar_add(out=st2[:, 1:2], in0=st2[:, 1:2], scalar1=EPS), ["st2"], ["st2"])
        ACT(lambda e: e.activation(out=st2[:, 1:2], in_=st2[:, 1:2], func=AF.Sqrt), ["st2"], ["st2"])
        DVE(lambda e: e.reciprocal(out=st2[:, 1:2], in_=st2[:, 1:2]), ["st2"], ["st2"])
        DVE(lambda e: e.tensor_scalar(out=x[:, 0:n], in0=x[:, 0:n], scalar1=st2[:, 0:1], scalar2=st2[:, 1:2], op0=ALU.subtract, op1=ALU.mult), [xname, "st2"], [xname])
        DVE(lambda e: e.tensor_tensor(out=x[:, 0:n], in0=x[:, 0:n], in1=gap, op=ALU.mult), [xname] + pnames, [xname])
        DVE(lambda e: e.tensor_tensor(out=x[:, 0:n], in0=x[:, 0:n], in1=bap, op=ALU.add), [xname] + pnames, [xname])

    def rms_heads(x, xname, gslot, gcol, out_c0):
        DVE(lambda e: e.tensor_tensor(out=sq[:, :], in0=x[:, 0:256], in1=x[:, 0:256], op=ALU.mult), [xname], ["sq"])
        DVE(lambda e: e.reduce_sum(out=ss4[:, 0:4], in_=V(sq, 0, [[64, 4], [1, 64]]), axis=AX.X), ["sq"], ["ss4"])
        DVE(lambda e: e.tensor_scalar(out=ss4[:, 0:4], in0=ss4[:, 0:4], scalar1=1.0 / 64, scalar2=EPS, op0=ALU.mult, op1=ALU.add), ["ss4"], ["ss4"])
        ACT(lambda e: e.activation(out=ss4[:, 0:4], in_=ss4[:, 0:4], func=AF.Sqrt), ["ss4"], ["ss4"])
        DVE(lambda e: e.reciprocal(out=ss4[:, 0:4], in_=ss4[:, 0:4]), ["ss4"], ["ss4"])
        DVE(lambda e: e.tensor_tensor(out=V(x, 0, [[64, 4], [1, 64]]), in0=V(x, 0, [[64, 4], [1, 64]]), in1=V(ss4, 0, [[1, 4], [0, 64]]), op=ALU.mult), [xname, "ss4"], [xname])
        DVE(lambda e: e.tensor_tensor(out=x[:, 0:256], in0=x[:, 0:256], in1=prm[:, gslot * 256:(gslot + 1) * 256], op=ALU.mult), [xname, "prm"], [xname])
        DVE(lambda e: e.tensor_tensor(out=cat[:, out_c0:out_c0 + 256], in0=x[:, 0:256], in1=gate[:, gcol:gcol + 256], op=ALU.mult), [xname, "gate"], ["cat"])

    def compute_mod(csrc):
        DMA(lambda e: e.dma_start(out=t1[:, :], in_=csrc[:, :]), [], ["t1"])
        ACT(lambda e: e.activation(out=hb[:, :], in_=t1[:, :], func=AF.Silu), ["t1"], ["hb"])
        for kc in range(8):
            PE(lambda e, kc=kc: e.transpose(out=tpb[:, kc * 128:(kc + 1) * 128], in_=hb[:, kc * 128:(kc + 1) * 128], identity=identb[:, :]), ["hb", "identb"], ["tpb"])
        DVE(lambda e: e.tensor_copy(out=hT[:, :], in_=tpb[:, :]), ["tpb"], ["hT"])
        for n in range(6):
            for kc in range(8):
                i = 0; cnt[0] += 1
                DMA(lambda e, i=i, kc=kc, n=n: e.dma_start(out=stg[i][:, :], in_=k.w_ada[l, kc * 128:(kc + 1) * 128, n * 512:(n + 1) * 512]), [], [f"stg{i}"])
                POOL(lambda e, i=i: e.tensor_copy(out=stgb[i][:, :], in_=stg[i][:, :]), [f"stg{i}"], [f"stgb{i}"])
                PE(lambda e, i=i, kc=kc, n=n: e.matmul(out=zz[n % 2][:, :], lhsT=hT[:, kc * 128:(kc + 1) * 128], rhs=stgb[i][:, :], start=(kc == 0), stop=(kc == 7)),
                   ["hT", f"stgb{i}"], [zn[n % 2]])
            DMA(lambda e, n=n: e.dma_start(out=t1[:, 0:512], in_=DV(k.b_ada, l * 3 * D + n * 512, [[0, 128], [1, 512]])), [], ["t1"])
            DVE(lambda e, n=n: e.tensor_tensor(out=mod[:, n * 512:(n + 1) * 512], in0=zz[n % 2][:, :], in1=t1[:, 0:512], op=ALU.add), [zn[n % 2], "t1"], ["mod"])
        DVE(lambda e: e.tensor_scalar_add(out=mod[:, D:2 * D], in0=mod[:, D:2 * D], scalar1=1.0), ["mod"], ["mod"])

    def tile(kind, i):
        kk = kind
        nseg = 8 if kind == 0 else 16
        nj = nseg // 2
        seglen = 128 // nseg
        if kind == 0:
            src = (k.xp if l == 0 else k.y1)[i * 128:(i + 1) * 128, :]
            srcn = [] if l == 0 else ["y1"]
        else:
            src = (k.xs if l == 0 else k.y1s)[:, :]
            srcn = [] if l == 0 else ["y1s"]
        DMA(lambda e: e.dma_start(out=xt[:, :], in_=src), srcn, ["xt"])
        DVE(lambda e: e.tensor_tensor(out=t1[:, :], in0=xt[:, :], in1=mod[:, D:2 * D], op=ALU.mult), ["xt", "mod"], ["t1"])
        DVE(lambda e: e.tensor_tensor(out=hb[:, :], in0=t1[:, :], in1=mod[:, 0:D], op=ALU.add), ["t1", "mod"], ["hb"])
        for kc in range(8):
            PE(lambda e, kc=kc: e.transpose(out=tpb[:, kc * 128:(kc + 1) * 128], in_=hb[:, kc * 128:(kc + 1) * 128], identity=identb[:, :]), ["hb", "identb"], ["tpb"])
        DVE(lambda e: e.tensor_copy(out=hT[:, :], in_=tpb[:, :]), ["tpb"], ["hT"])

        def inproj(n):
            z = zz[n % 2] if not (DBG_MISC and n == 2) else misc
            for kc in range(8):
                PE(lambda e, kc=kc: e.matmul(out=z[:, :], lhsT=hT[:, kc * 128:(kc + 1) * 128], rhs=win[:, kc * DIN + n * 512:kc * DIN + (n + 1) * 512], start=(kc == 0), stop=(kc == 7)),
                   ["hT", "win"] + (DBG_SER_LIST if DBG_SER else []), [zn[n % 2] if not (DBG_MISC and n == 2) else "misc"])
            zname = (zn[n % 2] if not (DBG_MISC and n == 2) else "misc")
            DVE(lambda e: e.tensor_copy(out=qk[:, :], in_=z[:, :]), [zname], ["qk"])
            return qk, "qk"

        ck(f'hT{l}{kind}{i}')
        z, zr = inproj(0)
        ck(f'z0{l}{kind}{i}')
        if kind == 0:
            cs_ = V(cosp, i * 16, [[0, 16], [1, 16]]); sn_ = V(sinp, i * 16, [[0, 16], [1, 16]]); csn = ["cosp", "sinp"]
        else:
            cs_ = V(coss, 0, [[0, 16], [1, 16]]); sn_ = V(sins, 0, [[0, 16], [1, 16]]); csn = ["coss", "sins"]
        x1 = V(qk, 0, [[32, 16], [1, 16]]); x2 = V(qk, 16, [[32, 16], [1, 16]])
        o1 = V(qkr, 0, [[32, 16], [1, 16]]); o2 = V(qkr, 16, [[32, 16], [1, 16]]); tm = V(e1, 0, [[16, 16], [1, 16]])
        DVE(lambda e: e.tensor_tensor(out=o1, in0=x1, in1=cs_, op=ALU.mult), ["qk"] + csn, ["qkr"])
        DVE(lambda e: e.tensor_tensor(out=tm, in0=x2, in1=sn_, op=ALU.mult), ["qk"] + csn, ["e1"])
        DVE(lambda e: e.tensor_tensor(out=o1, in0=o1, in1=tm, op=ALU.subtract), ["qkr", "e1"], ["qkr"])
        DVE(lambda e: e.tensor_tensor(out=o2, in0=x2, in1=cs_, op=ALU.mult), ["qk"] + csn, ["qkr"])
        DVE(lambda e: e.tensor_tensor(out=tm, in0=x1, in1=sn_, op=ALU.mult), ["qk"] + csn, ["e1"])
        DVE(lambda e: e.tensor_tensor(out=o2, in0=o2, in1=tm, op=ALU.add), ["qkr", "e1"], ["qkr"])
        ck(f'rope{l}{kind}{i}')
        if kind == 0:
            DMA(lambda e: e.dma_start(out=k.nk_p[l, i * 128:(i + 1) * 128, :], in_=qkr[:, 256:512]), ["qkr"], [])
        else:
            DMA(lambda e: e.dma_start(out=k.nk_s[l, :, :], in_=qkr[:, 256:512]), ["qkr"], [])
        ck(f'nk{l}{kind}{i}')
        DVE(lambda e: e.tensor_copy(out=qb[:, :], in_=qkr[:, 0:256]), ["qkr"], ["qb"])
        DVE(lambda e: e.tensor_copy(out=kb[:, :], in_=qkr[:, 256:512]), ["qkr"], ["kb"])
        for ch in range(2):
            PE(lambda e, ch=ch: e.transpose(out=tpb[:, ch * 128:(ch + 1) * 128], in_=qb[:, ch * 128:(ch + 1) * 128], identity=identb[:, :]), ["qb", "identb"], ["tpb"])
            PE(lambda e, ch=ch: e.transpose(out=tpb[:, 256 + ch * 128:256 + (ch + 1) * 128], in_=kb[:, ch * 128:(ch + 1) * 128], identity=identb[:, :]), ["kb", "identb"], ["tpb"])
        for ch in range(2):
            for m in range(4):
                DVE(lambda e, ch=ch, m=m: e.tensor_scalar(out=qblk[:, ch * 512 + m * 128:ch * 512 + (m + 1) * 128], in0=tpb[:, ch * 128:(ch + 1) * 128], scalar1=rowm[:, m:m + 1], scalar2=None, op0=ALU.mult),
                    ["tpb", "rowm"], ["qblk"])
        ck(f'qblk{l}{kind}{i}')
        DVE(lambda e: e.tensor_copy(out=kTt[:, :], in_=tpb[:, 256:512]), ["tpb"], ["kTt"])
        ck(f'ktt{l}{kind}{i}')
        if kind == 0:
            DMA(lambda e: e.dma_start(out=k.ktD.ap().rearrange("p (c n) -> p c n", c=2)[:, :, i * 128:(i + 1) * 128], in_=V(kTt, 0, [[128, 2], [1, 128]])), ["kTt"], ["ktD"])
        ck(f'c0{l}{kind}{i}')
        z, zr = inproj(1)
        DVE(lambda e: e.tensor_copy(out=sq[:, :], in_=z[:, 0:256]), [zr], ["sq"])
        DVE(lambda e: e.tensor_copy(out=V(vat, 0, [[65, 4], [1, 64]]), in_=V(z, 0, [[64, 4], [1, 64]])), [zr], ["vat"])
        ACT(lambda e: e.activation(out=gate[:, 0:256], in_=z[:, 256:512], func=AF.Silu), [zr], ["gate"])
        if kind == 0:
            DMA(lambda e: e.dma_start(out=k.nv_p[l, i * 128:(i + 1) * 128, :], in_=sq[:, :]), ["sq"], [])
            DMA(lambda e: e.dma_start(out=k.vD[i * 128:(i + 1) * 128, :], in_=vat[:, :]), ["vat"], ["vD"])
        else:
            DMA(lambda e: e.dma_start(out=k.nv_s[l, :, :], in_=sq[:, :]), ["sq"], [])
        ck(f'c1{l}{kind}{i}')
        z, zr = inproj(2)
        ck(f'z2{l}{kind}{i}')
        ACT(lambda e: e.activation(out=qkr[:, :], in_=z[:, :], func=AF.Erf, scale=0.7071067811865476), [zr], ["qkr"])
        ck(f'erf{l}{kind}{i}')
        DVE(lambda e: e.tensor_scalar(out=qkr[:, :], in0=qkr[:, :], scalar1=1.0, scalar2=0.5, op0=ALU.add, op1=ALU.mult), ["qkr"], ["qkr"])
        DVE(lambda e: e.tensor_tensor(out=e3[:, :], in0=qkr[:, 0:256], in1=z[:, 0:256], op=ALU.mult), ["qkr", zr], ["e3"])
        DVE(lambda e: e.tensor_tensor(out=dq[:, :], in0=qkr[:, 256:512], in1=z[:, 256:512], op=ALU.mult), ["qkr", zr], ["dq"])
        ck(f'gelu{l}{kind}{i}')
        layer_norm(dq, 256, prm[:, 0:256], prm[:, 256:512], "dq", ["prm"], sq, "sq")
        ck(f'ln{l}{kind}{i}')
        if kind == 1:
            DMA(lambda e: e.dma_start(out=k.nch_s[l, :, :], in_=dq[:, :]), ["dq"], [])
        DVE(lambda e: e.tensor_copy(out=vnb[:, :], in_=dq[:, :]), ["dq"], ["vnb"])
        ck(f'c2{l}{kind}{i}')
        z, zr = inproj(3)
        ACT(lambda e: e.activation(out=gate[:, 256:512], in_=z[:, 0:256], func=AF.Silu), [zr], ["gate"])
        DVE(lambda e: e.tensor_copy(out=oat[:, :], in_=z[:, 256:512]), [zr], ["oat"])
        for gi in range(4):
            PE(lambda e, gi=gi: e.matmul(out=misc[:, gi * 64:(gi + 1) * 64], lhsT=wmT[:, kk * 512 + gi * 128:kk * 512 + (gi + 1) * 128], rhs=vnb[:, gi * 64:(gi + 1) * 64], start=True, stop=True),
               ["wmT", "vnb"], ["misc"])
        DVE(lambda e: e.tensor_tensor(out=od[:, :], in0=misc[:, 0:256], in1=bsE[:, kk * 256:(kk + 1) * 256], op=ALU.add), ["misc", "bsE"], ["od"])
        DVE(lambda e: e.tensor_tensor(out=od[:, :], in0=od[:, :], in1=e3[:, :], op=ALU.mult), ["od", "e3"], ["od"])
        DVE(lambda e: e.tensor_tensor(out=cat[:, 256:512], in0=od[:, :], in1=gate[:, 256:512], op=ALU.mult), ["od", "gate"], ["cat"])
        ck(f'c3{l}{kind}{i}')
        z, zr = inproj(4)
        ACT(lambda e: e.activation(out=df[:, :], in_=z[:, 0:256], func=AF.Sigmoid), [zr], ["df"])
        ACT(lambda e: e.activation(out=gate[:, 512:768], in_=z[:, 256:512], func=AF.Silu), [zr], ["gate"])
        DVE(lambda e: e.tensor_tensor(out=df[:, :], in0=df[:, :], in1=oat[:, :], op=ALU.mult), ["df", "oat"], ["df"])
        if kind == 0:
            if i == NT - 1:
                DMA(lambda e: e.dma_start(out=k.ncv_p[l, :, :], in_=df[98:128, :]), ["df"], [])
        else:
            for s_ in range(16):
                DMA(lambda e, s_=s_: e.dma_start(out=k.ncv_s[l, s_, 22:30, :], in_=df[8 * s_:8 * s_ + 8, :]), ["df"], [])
            DMA(lambda e: e.dma_start(out=k.ncv_s[l, :, 0:22, :], in_=k.st_conv[l, :, 8:30, :]), [], [])
        for ch in range(2):
            PE(lambda e, ch=ch: e.transpose(out=sc[0][:, ch * 128:(ch + 1) * 128], in_=df[:, ch * 128:(ch + 1) * 128], identity=identf[:, :]), ["df", "identf"], ["sc0"])
        if kind == 0:
            if i == 0:
                DVE(lambda e: e.memset(hcT[:, :], 0.0), [], ["hcT"])
            else:
                DVE(lambda e: e.tensor_copy(out=V(e1, 0, [[30, 2], [1, 30]]), in_=V(hcT, 128, [[158, 2], [1, 30]])), ["hcT"], ["e1"])
                DVE(lambda e: e.tensor_copy(out=V(hcT, 0, [[158, 2], [1, 30]]), in_=V(e1, 0, [[30, 2], [1, 30]])), ["e1"], ["hcT"])
            DVE(lambda e: e.tensor_copy(out=V(hcT, 30, [[158, 2], [1, 128]]), in_=V(sc[0], 0, [[128, 2], [1, 128]])), ["sc0"], ["hcT"])
        else:
            DVE(lambda e: e.tensor_copy(out=V(hcT, 30, [[608, 2], [38, 16], [1, 8]]), in_=V(sc[0], 0, [[128, 2], [8, 16], [1, 8]])), ["sc0"], ["hcT"])
            for q4 in range(4):
                DMA(lambda e, q4=q4: e.dma_start(out=kpg[0:120, :], in_=k.st_conv[l, 4 * q4:4 * q4 + 4, :, :].rearrange("s r c -> (s r) c")), [], ["kpg"])
                for ch in range(2):
                    PE(lambda e, ch=ch: e.transpose(out=sc[1][:, ch * 128:ch * 128 + 120], in_=kpg[0:120, ch * 128:(ch + 1) * 128], identity=identf[0:120, 0:120]), ["kpg", "identf"], ["sc1"])
                DVE(lambda e, q4=q4: e.tensor_copy(out=V(hcT, q4 * 4 * 38, [[608, 2], [38, 4], [1, 30]]), in_=V(sc[1], 0, [[128, 2], [30, 4], [1, 30]])), ["sc1"], ["hcT"])
        for ch in range(2):
            E = DVE
            for j in range(31):
                if kind == 0:
                    win_ = V(hcT, ch * 158 + j, [[1, 128]]); oap = V(qk, ch * 128, [[1, 128]])
                else:
                    win_ = V(hcT, ch * 608 + j, [[38, 16], [1, 8]]); oap = V(qk, ch * 128, [[8, 16], [1, 8]])
                if j == 0:
                    E(lambda e, win_=win_, oap=oap, ch=ch, j=j: e.tensor_scalar(out=oap, in0=win_, scalar1=cwT[:, ch * 31 + j:ch * 31 + j + 1], scalar2=None, op0=ALU.mult), ["hcT", "cwT"], [f"qk{ch}"])
                else:
                    E(lambda e, win_=win_, oap=oap, ch=ch, j=j: e.scalar_tensor_tensor(out=oap, in0=win_, scalar=cwT[:, ch * 31 + j:ch * 31 + j + 1], in1=oap, op0=ALU.mult, op1=ALU.add), ["hcT", "cwT", f"qk{ch}"], [f"qk{ch}"])
        for ch in range(2):
            PE(lambda e, ch=ch: e.transpose(out=sc[1][:, ch * 128:(ch + 1) * 128], in_=qk[:, ch * 128:(ch + 1) * 128], identity=identf[:, :]), [f"qk{ch}", "identf"], ["sc1"])
        DVE(lambda e: e.tensor_tensor(out=e2[:, :], in0=sc[1][:, 0:256], in1=prm[:, 512:768], op=ALU.add), ["sc1", "prm"], ["e2"])
        layer_norm(e2, 256, prm[:, 768:1024], prm[:, 1024:1280], "e2", ["prm"], sq, "sq")
        ACT(lambda e: e.activation(out=cyb[:, :], in_=e2[:, :], func=AF.Silu), ["e2"], ["cyb"])
        for ch in range(2):
            PE(lambda e, ch=ch: e.transpose(out=tpb[:, ch * 128:(ch + 1) * 128], in_=cyb[:, ch * 128:(ch + 1) * 128], identity=identb[:, :]), ["cyb", "identb"], ["tpb"])
        DVE(lambda e: e.tensor_copy(out=cyT[:, :], in_=tpb[:, 0:256]), ["tpb"], ["cyT"])
        for ch in range(2):
            PE(lambda e, ch=ch: e.matmul(out=misc[:, 0:256], lhsT=cyT[:, ch * 128:(ch + 1) * 128], rhs=wpw[:, ch * 256:(ch + 1) * 256], start=(ch == 0), stop=(ch == 1)), ["cyT", "wpw"], ["misc"])
        DVE(lambda e: e.tensor_tensor(out=cat[:, 512:768], in0=misc[:, 0:256], in1=gate[:, 512:768], op=ALU.mult), ["misc", "gate"], ["cat"])
        ck(f'c4{l}{kind}{i}')
        z, zr = inproj(5)
        ACT(lambda e: e.activation(out=dq[:, :], in_=z[:, 0:256], func=AF.Silu), [zr], ["dq"])
        ACT(lambda e: e.activation(out=df[:, :], in_=z[:, 256:512], func=AF.Sigmoid), [zr], ["df"])
        DVE(lambda e: e.tensor_tensor(out=df[:, :], in0=df[:, :], in1=prm[:, 6 * 256:7 * 256], op=ALU.mult), ["df", "prm"], ["df"])
        DVE(lambda e: e.tensor_tensor(out=df[:, :], in0=df[:, :], in1=prm[:, 5 * 256:6 * 256], op=ALU.add), ["df", "prm"], ["df"])
        DVE(lambda e: e.tensor_scalar(out=kd[:, :], in0=df[:, :], scalar1=-1.0, scalar2=1.0, op0=ALU.mult, op1=ALU.add), ["df"], ["kd"])
        DVE(lambda e: e.tensor_scalar_max(out=lf[:, :], in0=df[:, :], scalar1=1e-30), ["df"], ["lf"])
        ACT(lambda e: e.activation(out=lf[:, :], in_=lf[:, :], func=AF.Ln), ["lf"], ["lf"])
        z, zr = inproj(6)
        DVE(lambda e: e.tensor_copy(out=vdb[:, :], in_=z[:, 0:256]), [zr], ["vdb"])
        ACT(lambda e: e.activation(out=gate[:, 768:1024], in_=z[:, 256:512], func=AF.Silu), [zr], ["gate"])
        ck(f'c6{l}{kind}{i}')
        PE(lambda e: e.matmul(out=sc[0][:, 0:256], lhsT=mcum[:, kk * 128:(kk + 1) * 128], rhs=lf[:, :], start=True, stop=True), ["mcum", "lf"], ["sc0"])
        PE(lambda e: e.matmul(out=sc[0][:, 256:512], lhsT=mlast[:, kk * 128:(kk + 1) * 128], rhs=lf[:, :], start=True, stop=True), ["mlast", "lf"], ["sc0"])
        DVE(lambda e: e.tensor_copy(out=cl[:, 0:512], in_=sc[0][:, 0:512]), ["sc0"], ["cl"])
        ACT(lambda e: e.activation(out=e1[:, :], in_=cl[:, 0:256], func=AF.Exp), ["cl"], ["e1"])
        ACT(lambda e: e.activation(out=e2[:, :], in_=cl[:, 0:256], func=AF.Exp, scale=-1.0), ["cl"], ["e2"])
        DVE(lambda e: e.tensor_tensor(out=cl[:, 0:256], in0=cl[:, 256:512], in1=cl[:, 0:256], op=ALU.subtract), ["cl"], ["cl"])
        ACT(lambda e: e.activation(out=e3[:, :], in_=cl[:, 0:256], func=AF.Exp), ["cl"], ["e3"])
        ACT(lambda e: e.activation(out=V(eldup, 0, [[128, 4], [64, 2], [1, 64]]), in_=V(cl, 256, [[64, 4], [0, 2], [1, 64]]), func=AF.Exp), ["cl"], ["eldup"])
        dup_o = lambda t_: V(t_, 0, [[128, 4], [64, 2], [1, 64]])
        dup_i = lambda t_: V(t_, 0, [[64, 4], [0, 2], [1, 64]])
        DVE(lambda e: e.tensor_tensor(out=dup_o(qdup), in0=dup_i(dq), in1=dup_i(e1), op=ALU.mult), ["dq", "e1"], ["qdup"])
        DVE(lambda e: e.tensor_tensor(out=kt_[:, :], in0=kd[:, :], in1=e2[:, :], op=ALU.mult), ["kd", "e2"], ["kt_"])
        DVE(lambda e: e.tensor_tensor(out=dup_o(kpdup), in0=dup_i(kd), in1=dup_i(e3), op=ALU.mult), ["kd", "e3"], ["kpdup"])
        for h in range(4):
            PE(lambda e, h=h: e.transpose(out=tpb[:, h * 128:(h + 1) * 128], in_=qdup[:, h * 128:(h + 1) * 128], identity=identb[:, :]), ["qdup", "identb"], ["tpb"])
            PE(lambda e, h=h: e.transpose(out=tpb[0:64, 512 + h * 128:512 + (h + 1) * 128], in_=kt_[:, h * 64:(h + 1) * 64], identity=identb[:, :]), ["kt_", "identb"], ["tpb"])
            PE(lambda e, h=h: e.transpose(out=sc[1][:, h * 128:(h + 1) * 128], in_=eldup[:, h * 128:(h + 1) * 128], identity=identf[:, :]), ["eldup", "identf"], ["sc1"])
        DVE(lambda e: e.tensor_copy(out=q2T[:, :], in_=tpb[:, 0:512]), ["tpb"], ["q2T"])
        DVE(lambda e: e.tensor_copy(out=khT[0:64, :], in_=tpb[0:64, 512:1024]), ["tpb"], ["khT"])
        DVE(lambda e: e.tensor_copy(out=a2T[:, :], in_=sc[1][:, :]), ["sc1"], ["a2T"])
        for h in range(4):
            PE(lambda e, h=h: e.matmul(out=misc[:, h * 128:(h + 1) * 128], lhsT=khT[0:64, h * 128:(h + 1) * 128], rhs=q2T[0:64, h * 128:(h + 1) * 128], start=True, stop=True), ["khT", "q2T"], ["misc"])
        DVE(lambda e: e.tensor_tensor(out=attm[:, :], in0=misc[:, :], in1=mattb[:, kk * 512:(kk + 1) * 512], op=ALU.mult), ["misc", "mattb"], ["attm"])
        for h in range(4):
            if kind == 1:
                for du in range(2):
                    DMA(lambda e, h=h, du=du: e.dma_start(out=V(sstf, 0, [[64, 8], [1, 64]], p0=du * 64, npart=64),
                                                          in_=DV(k.st_hgrn, (l * 16 + du) * 16384 + h * 4096, [[64, 64], [2 * 16384, 8], [1, 64]])), [], ["sstf"])
                ACT(lambda e: e.copy(out=sst[:, :], in_=sstf[:, :]), ["sstf"], ["sst"])
            DVE(lambda e, h=h: e.tensor_tensor(out=V(vexp, 0, [[64, nseg], [1, 64]]), in0=V(vdb, h * 64, [[0, nseg], [1, 64]]), in1=V(segb, kk * 1024, [[64, nseg], [1, 64]]), op=ALU.mult), ["vdb", "segb"], ["vexp"])
            PE(lambda e, h=h: e.matmul(out=ot[0][:, :], lhsT=kpdup[:, h * 128:(h + 1) * 128], rhs=vexp[:, 0:512], start=True, stop=True), ["kpdup", "vexp"], ["ot0"])
            if kind == 1:
                PE(lambda e, h=h: e.matmul(out=ot[1][:, :], lhsT=kpdup[:, h * 128:(h + 1) * 128], rhs=vexp[:, 512:1024], start=True, stop=True), ["kpdup", "vexp"], ["ot1"])
            for sg in range(nseg):
                hf = sg % 2; j = sg // 2
                acol = h * 128 + seglen * sg
                bsrc = ot[sg // 8]; bname = f"ot{sg // 8}"
                if kind == 0:
                    ACT(lambda e, h=h, hf=hf, j=j: e.copy(out=V(sst, j * 64, [[1, 64]], p0=hf * 64, npart=64), in_=V(Sf, h * 64, [[1, 64]], p0=hf * 64, npart=64)), ["Sf"], ["sst"])
                    DVE(lambda e, h=h, sg=sg, acol=acol, bsrc=bsrc: e.scalar_tensor_tensor(out=Sf[:, h * 64:(h + 1) * 64], in0=Sf[:, h * 64:(h + 1) * 64], scalar=a2T[:, acol:acol + 1], in1=bsrc[:, (sg % 8) * 64:(sg % 8 + 1) * 64], op0=ALU.mult, op1=ALU.add),
                        ["Sf", "a2T", bname], ["Sf"])
                else:
                    DVE(lambda e, h=h, sg=sg, hf=hf, j=j, acol=acol, bsrc=bsrc: e.scalar_tensor_tensor(
                        out=V(sstf, j * 64, [[1, 64]], p0=hf * 64, npart=64), in0=V(sstf, j * 64, [[1, 64]], p0=hf * 64, npart=64),
                        scalar=V(a2T, acol, [[1, 1]], p0=hf * 64, npart=64), in1=V(bsrc, (sg % 8) * 64, [[1, 64]], p0=hf * 64, npart=64), op0=ALU.mult, op1=ALU.add),
                        ["sstf", "a2T", bname], ["sstf"])
            PE(lambda e, h=h: e.matmul(out=zA[:, h * 64:(h + 1) * 64], lhsT=attm[:, h * 128:(h + 1) * 128], rhs=vdb[:, h * 64:(h + 1) * 64], start=True, stop=False), ["attm", "vdb"], ["zA"])
            DVE(lambda e, h=h: e.tensor_tensor(out=V(qexp, 0, [[128, nj], [1, 128]]), in0=V(q2T, h * 128, [[0, nj], [1, 128]]), in1=V(cmb, kk * 1024, [[128, nj], [1, 128]]), op=ALU.mult), ["q2T", "cmb"], ["qexp"])
            for j in range(nj):
                PE(lambda e, h=h, j=j: e.matmul(out=zA[:, h * 64:(h + 1) * 64], lhsT=qexp[:, j * 128:(j + 1) * 128], rhs=sst[:, j * 64:(j + 1) * 64], start=False, stop=(j == nj - 1)), ["qexp", "sst"], ["zA"])
            if kind == 1:
                for du in range(2):
                    DMA(lambda e, h=h, du=du: e.dma_start(out=DV(k.nh_s, (l * 16 + du) * 16384 + h * 4096, [[64, 64], [2 * 16384, 8], [1, 64]]),
                                                          in_=V(sstf, 0, [[64, 8], [1, 64]], p0=du * 64, npart=64)), ["sstf"], [])
        if kind == 0 and i == NT - 1:
            DMA(lambda e: e.dma_start(out=DV(k.nh_p, l * 16384, [[64, 64], [4096, 4], [1, 64]]), in_=V(Sf, 0, [[64, 4], [1, 64]], npart=64)), ["Sf"], [])
        DVE(lambda e: e.tensor_copy(out=od[:, :], in_=zA[:, 0:256]), ["zA"], ["od"])
        rms_heads(od, "od", 8, 768, 768)

        ck(f'hgrn{l}{kind}{i}')
        pending = []

        def kblock(ktile, kname, vtile, vname, ncol, qcol, first, last, mask, mname, s_=None):
            for ch in range(2):
                if s_ is None:
                    rhs = qblk[:, ch * 512:(ch + 1) * 512]
                else:
                    rhs = V(qblk, ch * 512 + 8 * s_, [[128, 4], [1, 8]])
                PE(lambda e, ch=ch, rhs=rhs: e.matmul(out=sc[ch][:, 0:4 * ncol], lhsT=ktile[:, ch * 128:(ch + 1) * 128], rhs=rhs, start=True, stop=True), [kname, "qblk"], [f"sc{ch}"])
                ptb, pname = (pt_, "pt_") if ch == 0 else (qexp, "qexp")
                ACT(lambda e, ch=ch, ptb=ptb: e.activation(out=ptb[:, 0:4 * ncol], in_=sc[ch][:, 0:4 * ncol], func=AF.Exp), [f"sc{ch}"], [pname])
                if mask is not None:
                    DVE(lambda e, ptb=ptb: e.tensor_tensor(out=V(ptb, 0, [[128, 4], [1, 128]]), in0=V(ptb, 0, [[128, 4], [1, 128]]), in1=V(mask, 0, [[0, 4], [1, 128]]), op=ALU.mult), [pname, mname], [pname])

                def pv(ch=ch, ptb=ptb, pname=pname):
                    for m in range(4):
                        h = 2 * ch + m // 2
                        PE(lambda e, m=m, h=h: e.matmul(out=V(ot[ch], m * 128 + qcol, [[1, ncol]], npart=65), lhsT=vtile[:, h * 65:(h + 1) * 65], rhs=ptb[:, m * ncol:(m + 1) * ncol],
                                                        start=(first and m == 0), stop=last, skip_group_check=True), [vname, pname], [f"ot{ch}"])
                if pending:
                    pending.pop(0)()
                pending.append(pv)

        if kind == 0:
            kvsets = [(ktB[j][:, :], f"ktB{j}", vB[j][:, :], f"vB{j}") for j in range(4)]
            for kbi in range(i):
                kt_ap, ktn, v_ap, vnm = kvsets[kbi % 4]
                DMA(lambda e, kbi=kbi, kt_ap=kt_ap: e.dma_start(out=kt_ap.rearrange("p (c n) -> p c n", c=2), in_=k.ktD.ap().rearrange("p (c n) -> p c n", c=2)[:, :, kbi * 128:(kbi + 1) * 128]), ["ktD"], [ktn])
                DMA(lambda e, kbi=kbi, v_ap=v_ap: e.dma_start(out=v_ap, in_=k.vD[kbi * 128:(kbi + 1) * 128, :]), ["vD"], [vnm], q="pool")
                kblock(kt_ap, ktn, v_ap, vnm, 128, 0, kbi == 0, False, None, None)
            kblock(kTt, "kTt", vat, "vat", 128, 0, i == 0, True, trilb, "trilb")
        else:
            kblock(kTt, "kTt", vat, "vat", 128, 0, True, False, msb, "msb")
            for s_ in range(16):
                for j in range(16):
                    col = s_ * 16 + j
                    while pending:
                        pending.pop(0)()
                    DMA(lambda e, col=col: e.indirect_dma_start(out=kpg[:, :], out_offset=None, in_=k.cache_kl[l][:, :], in_offset=bass.IndirectOffsetOnAxis(ap=idx[:, col:col + 1], axis=0)), ["idx"], ["kpg"], q="pool")
                    DMA(lambda e, col=col: e.indirect_dma_start(out=vpg[:, :], out_offset=None, in_=k.cache_vl[l][:, :], in_offset=bass.IndirectOffsetOnAxis(ap=idx[:, col:col + 1], axis=0)), ["idx"], ["vpg"], q="pool")
                    DVE(lambda e: e.tensor_copy(out=kpb[:, :], in_=kpg[:, :]), ["kpg"], ["kpb"])
                    DVE(lambda e: e.tensor_copy(out=V(vpa, 0, [[65, 4], [1, 64]]), in_=V(vpg, 0, [[64, 4], [1, 64]])), ["vpg"], ["vpa"])
                    for ch in range(2):
                        PE(lambda e, ch=ch: e.transpose(out=tpb[:, ch * 128:(ch + 1) * 128], in_=kpb[:, ch * 128:(ch + 1) * 128], identity=identb[:, :]), ["kpb", "identb"], ["tpb"])
                    DVE(lambda e: e.tensor_copy(out=kpT[:, :], in_=tpb[:, 0:256]), ["tpb"], ["kpT"])
                    kblock(kpT, "kpT", vpa, "vpa", 8, 8 * s_, False, j == 15, None, None, s_=s_)
        while pending:
            pending.pop(0)()
        ck(f'attn{l}{kind}{i}')
        for ch in range(2):
            DVE(lambda e, ch=ch: e.tensor_copy(out=t1[0:65, ch * 512:(ch + 1) * 512], in_=ot[ch][0:65, :]), [f"ot{ch}"], ["t1"])
            for m in range(4):
                PE(lambda e, ch=ch, m=m: e.transpose(out=sc[ch][:, m * 65:(m + 1) * 65], in_=t1[0:65, ch * 512 + m * 128:ch * 512 + (m + 1) * 128], identity=identf[0:65, 0:65]), ["t1", "identf"], [f"sc{ch}"])
            DVE(lambda e, ch=ch: e.tensor_copy(out=oa[:, ch * 260:(ch + 1) * 260], in_=sc[ch][:, 0:260]), [f"sc{ch}"], ["oa"])
        DVE(lambda e: e.reciprocal(out=rl[:, 0:8], in_=V(oa, 64, [[65, 8]])), ["oa"], ["rl"])
        for h in range(4):
            b0 = h * 130
            DVE(lambda e, h=h, b0=b0: e.tensor_scalar(out=sq[:, 0:64], in0=oa[:, b0 + 65:b0 + 129], scalar1=rl[:, 2 * h + 1:2 * h + 2], scalar2=lamt[:, 0:1], op0=ALU.mult, op1=ALU.mult), ["oa", "rl", "lamt"], ["sq"])
            DVE(lambda e, h=h, b0=b0: e.scalar_tensor_tensor(out=oat[:, h * 64:(h + 1) * 64], in0=oa[:, b0:b0 + 64], scalar=rl[:, 2 * h:2 * h + 1], in1=sq[:, 0:64], op0=ALU.mult, op1=ALU.add), ["oa", "rl", "sq"], ["oat"])
        rms_heads(oat, "oat", 7, 0, 0)
        ck(f'epi{l}{kind}{i}')
        if kind == 0 and DBG_CAT:
            DMA(lambda e: e.dma_start(out=k.dbg_cat[l, i * 128:(i + 1) * 128, :], in_=cat[:, :]), ["cat"], [])
        for kc in range(8):
            PE(lambda e, kc=kc: e.transpose(out=tpb[:, kc * 128:(kc + 1) * 128], in_=cat[:, kc * 128:(kc + 1) * 128], identity=identb[:, :]), ["cat", "identb"], ["tpb"])
        DVE(lambda e: e.tensor_copy(out=catT[:, :], in_=tpb[:, :]), ["tpb"], ["catT"])
        for n in range(2):
            for kc in range(8):
                PE(lambda e, kc=kc, n=n: e.matmul(out=zz[n][:, :], lhsT=catT[:, kc * 128:(kc + 1) * 128], rhs=wout[:, kc * D + n * 512:kc * D + (n + 1) * 512], start=(kc == 0), stop=(kc == 7)), ["catT", "wout"], [zn[n]])
            DVE(lambda e, n=n: e.tensor_tensor(out=t1[:, n * 512:(n + 1) * 512], in0=zz[n][:, :], in1=mod[:, 2 * D + n * 512:2 * D + (n + 1) * 512], op=ALU.mult), [zn[n], "mod"], ["t1"])
        DVE(lambda e: e.scalar_tensor_tensor(out=t1[:, :], in0=xt[:, :], scalar=ALPHA, in1=t1[:, :], op0=ALU.mult, op1=ALU.add), ["xt", "t1"], ["t1"])
        layer_norm(t1, D, lnp[:, 0:D], lnp[:, D:2 * D], "t1", ["lnp"], xt, "xt")
        if kind == 0:
            dst = (k.y1 if l == 0 else k.y_p)[i * 128:(i + 1) * 128, :]
            DMA(lambda e: e.dma_start(out=dst, in_=t1[:, :]), ["t1"], ["y1"] if l == 0 else [])
        else:
            dst = (k.y1s if l == 0 else k.y_s)[:, :]
            DMA(lambda e: e.dma_start(out=dst, in_=t1[:, :]), ["t1"], ["y1s"] if l == 0 else [])

    compute_mod(k.cp)
    ck(f'mod{l}')
    for i in range(k.NTILES):
        tile(0, i)
    compute_mod(k.cs)
    tile(1, 0)


def _consts():
    c = {}
    a = np.arange(128)
    s_, t_ = a[:, None], a[None, :]
    c["c_ident"] = np.eye(128, dtype=np.float32)
    c["c_tril"] = (s_ <= t_).astype(np.float32)
    c["c_ms"] = ((s_ // 8 == t_ // 8) & (s_ <= t_)).astype(np.float32)
    mc, ml, cm, sg = [], [], [], []
    for L in (16, 8):
        same = (s_ // L == t_ // L)
        mc.append((same & (s_ <= t_)).astype(np.float32))
        ml.append(same.astype(np.float32))
        nseg = 128 // L
        m = np.zeros((128, 8, 128), np.float32)
        for pp in range(128):
            for j in range(nseg // 2):
                seg = 2 * j + pp // 64
                m[pp, j, seg * L:(seg + 1) * L] = 1.0
        cm.append(m.reshape(128, 1024))
        g = np.zeros((128, 16, 64), np.float32)
        for seg in range(nseg):
            g[seg * L:(seg + 1) * L, seg, :] = 1.0
        sg.append(g.reshape(128, 1024))
    c["c_mcum"] = np.stack(mc); c["c_mlast"] = np.stack(ml); c["c_matt"] = np.stack(mc)
    c["c_cm"] = np.stack(cm); c["c_seg"] = np.stack(sg)
    rm = np.zeros((128, 4), np.float32)
    for m in range(4):
        rm[32 * m:32 * m + 32, m] = SCALE
    c["c_rowm"] = rm
    c["pos_p"] = (128.0 * np.arange(NT)[None, :] + a[:, None]).astype(np.float32)
    c["pos_s"] = (2048.0 + (a % 8)).astype(np.float32).reshape(128, 1)
    inv = (np.float32(10000.0) ** (-np.arange(16, dtype=np.float32) * np.float32(2.0) / np.float32(32.0))).astype(np.float32)
    c["invf"] = np.tile(inv[None, :], (128, 1)).astype(np.float32)
    c["iota_p"] = a.astype(np.float32).reshape(128, 1)
    return c


_PROG = None


def kernel(x_prompt, x_sample, cache_k, cache_v, state_conv, state_hgrn, page_table, c_prompt, c_sample,
           w_ada, b_ada, w_in, lam_qk, attn_norm_g, sg_norm_g, sg_norm_b, w_s, b_s, conv_w, conv_b,
           conv_norm_g, conv_norm_b, w_pw, lower_bounds, hgrn_norm_g, w_out, ln_g, ln_b):
    global _PROG
    if _PROG is None:
        _PROG = build_program()
    k = _PROG
    f = lambda a: np.ascontiguousarray(np.asarray(a, dtype=np.float32))
    consts = _consts()
    ck = f(cache_k).reshape(2, NPOOL_ROWS, 256)[:, :POOL_ROWS_RUN]
    cv = f(cache_v).reshape(2, NPOOL_ROWS, 256)[:, :POOL_ROWS_RUN]
    shared = dict(cache_k0=ck[0], cache_k1=ck[1], cache_v0=cv[0], cache_v1=cv[1], w_ada=f(w_ada), b_ada=f(b_ada), w_in=f(w_in), lam_qk=f(lam_qk).reshape(2, 128),
                  attn_g=f(attn_norm_g), sg_g=f(sg_norm_g), sg_b=f(sg_norm_b), w_s=f(w_s), b_s=f(b_s), conv_w=f(conv_w),
                  conv_b=f(conv_b), cn_g=f(conv_norm_g), cn_b=f(conv_norm_b), w_pw=f(w_pw), lowb=f(lower_bounds),
                  hg_g=f(hgrn_norm_g), w_out=f(w_out), ln_g=f(ln_g), ln_b=f(ln_b))
    shared.update(consts)
    xp_, xs_ = f(x_prompt), f(x_sample)
    cp_, cs_ = f(c_prompt), f(c_sample)
    pt = np.asarray(page_table).astype(np.int32)
    sc_, sh_ = f(state_conv), f(state_hgrn)
    in_maps = []
    for r in range(NCORES_RUN):
        b = r % 2
        m = dict(shared)
        m["xp"] = xp_[b]
        m["xs"] = np.ascontiguousarray(xs_[16 * r:16 * r + 16].reshape(128, D))
        m["cp"] = np.ascontiguousarray(np.tile(cp_[b][None, :], (128, 1)))
        m["cs"] = np.ascontiguousarray(np.repeat(cs_[16 * r:16 * r + 16], 8, axis=0))
        m["ptab"] = np.ascontiguousarray(np.tile(pt[16 * r:16 * r + 16].reshape(1, 256), (128, 1)))
        m["st_conv"] = np.ascontiguousarray(sc_[:, 16 * r:16 * r + 16])
        m["st_hgrn"] = np.ascontiguousarray(sh_[:, 16 * r:16 * r + 16])
        in_maps.append(m)
    res = run_bass_kernel_spmd(k.nc, in_maps, core_ids=list(range(NCORES_RUN))).results
    res = list(res) + [res[0]] * (8 - len(res))
    global DBG_RES
    DBG_RES = res
    y_prompt = np.stack([res[b]["y_p"] for b in range(2)])
    y_sample = np.concatenate([res[r]["y_s"].reshape(16, 8, D) for r in range(8)], 0)
    nk_p = np.stack([res[b]["nk_p"] for b in range(2)], 1).reshape(2, 2, 8192, 4, 64)
    nv_p = np.stack([res[b]["nv_p"] for b in range(2)], 1).reshape(2, 2, 8192, 4, 64)
    nk_s = np.concatenate([res[r]["nk_s"].reshape(2, 16, 8, 4, 64) for r in range(8)], 1)
    nv_s = np.concatenate([res[r]["nv_s"].reshape(2, 16, 8, 4, 64) for r in range(8)], 1)
    nch = np.concatenate([res[r]["nch_s"].reshape(2, 16, 8, 256) for r in range(8)], 1)
    ncv_p = np.stack([res[b]["ncv_p"] for b in range(2)], 1)
    ncv_s = np.concatenate([res[r]["ncv_s"] for r in range(8)], 1)
    nh_p = np.stack([res[b]["nh_p"] for b in range(2)], 1)
    nh_s = np.concatenate([res[r]["nh_s"] for r in range(8)], 1)
    outs = (y_prompt, y_sample, nk_p, nv_p, nk_s, nv_s, nch, ncv_p, ncv_s, nh_p, nh_s)
    return tuple(np.ascontiguousarray(o, dtype=np.float32) for o in outs)
```

```python
import numpy as np
import concourse.bass as bass
import concourse.mybir as mybir
from concourse.bass_utils import run_bass_kernel_spmd

F32 = mybir.dt.float32
BF16 = mybir.dt.bfloat16
I32 = mybir.dt.int32
AF = mybir.ActivationFunctionType
ALU = mybir.AluOpType
AX = mybir.AxisListType


PSUM_NAMES = {"zA", "zB", "tpb", "sc0", "sc1", "ot0", "ot1", "misc"}


class Res:
    __slots__ = ("name", "w", "r")

    def __init__(self, name):
        self.name = name
        self.w = None
        self.r = []


class Op:
    __slots__ = ("eng", "fn", "deps", "idx", "dma", "lane", "lane_cnt", "marked", "val", "waits", "lane_wait")

    def __init__(self, eng, fn, dma):
        self.eng = eng
        self.fn = fn
        self.dma = dma
        self.deps = []
        self.marked = False
        self.val = 0
        self.waits = []
        self.lane = None
        self.lane_cnt = 0
        self.lane_wait = 0


class Prog:
    ENGS = ("pe", "act", "dve", "pool", "sp")
    NLANES = {"sp": 8, "pool": 6, "act": 2}

    def __init__(self, nc):
        self.nc = nc
        self.ops = {e: [] for e in self.ENGS}
        self.lane_rr = {e: 0 for e in self.NLANES}
        self.lane_count = {}
        self.nres = 0

    def res(self, name=None):
        self.nres += 1
        return Res(name or f"r{self.nres}")

    def add(self, eng, fn, reads=(), writes=(), dma=False):
        op = Op(eng, fn, dma)
        deps = []
        xr = [r for r in reads if r.name in PSUM_NAMES]
        if xr:
            reads = [r for r in reads if r.name not in PSUM_NAMES]
            writes = list(writes) + [r for r in xr if r not in writes]
        for r in reads:
            if r.w is not None:
                deps.append(r.w)
        for w in writes:
            if w.w is not None:
                deps.append(w.w)
            deps.extend(w.r)
        for r in reads:
            r.r.append(op)
        for w in writes:
            w.w = op
            w.r = []
        seen = set()
        for d in deps:
            if d is op or id(d) in seen:
                continue
            seen.add(id(d))
            if (not d.dma) and (not dma) and d.eng == eng and eng == "pe":
                continue
            op.deps.append(d)
        op.idx = len(self.ops[eng])
        if dma:
            n = self.NLANES[eng]
            k = self.lane_rr[eng]
            self.lane_rr[eng] = (k + 1) % n
            op.lane = (eng, k)
            c = self.lane_count.get(op.lane, 0)
            op.lane_wait = c
            op.lane_cnt = c + 1
            self.lane_count[op.lane] = c + 1
        self.ops[eng].append(op)
        return op

    def pe(self, fn, reads=(), writes=()):
        return self.add("pe", fn, reads, writes)

    def act(self, fn, reads=(), writes=()):
        return self.add("act", fn, reads, writes)

    def dve(self, fn, reads=(), writes=()):
        return self.add("dve", fn, reads, writes)

    def pool(self, fn, reads=(), writes=()):
        return self.add("pool", fn, reads, writes)

    def dma(self, fn, reads=(), writes=(), q="sp"):
        return self.add(q, fn, reads, writes, dma=True)

    def finalize_and_emit(self):
        nc = self.nc
        for e in self.ENGS:
            waited_idx = {}
            waited_lane = {}
            for op in self.ops[e]:
                need_idx = {}
                need_lane = {}
                for d in op.deps:
                    if d.dma:
                        if need_lane.get(d.lane, 0) < d.lane_cnt:
                            need_lane[d.lane] = d.lane_cnt
                    else:
                        if need_idx.get(d.eng, -1) < d.idx:
                            need_idx[d.eng] = d.idx
                if op.dma and op.lane_wait > 0:
                    if need_lane.get(op.lane, 0) < op.lane_wait:
                        need_lane[op.lane] = op.lane_wait
                op.waits = []
                for te, ix in need_idx.items():
                    if waited_idx.get(te, -1) >= ix:
                        continue
                    if te == e and not op.dma and ix < op.idx and False:
                        continue
                    waited_idx[te] = ix
                    tgt = self.ops[te][ix]
                    tgt.marked = True
                    op.waits.append(("op", tgt))
                for ln, cnt in need_lane.items():
                    if waited_lane.get(ln, 0) >= cnt:
                        continue
                    waited_lane[ln] = cnt
                    op.waits.append(("lane", ln, cnt))
        for e in self.ENGS:
            v = 0
            for op in self.ops[e]:
                if op.dma:
                    continue
                if op.marked:
                    v += 1
                    op.val = v
        self.stats = {e: len(self.ops[e]) for e in self.ENGS}
        sems = {}
        import contextlib
        with contextlib.ExitStack() as st:
            for e in self.ENGS:
                sems[e] = st.enter_context(nc.semaphore(f"sem_{e}"))
            lanes = {}
            for q, n in self.NLANES.items():
                for k in range(n):
                    lanes[(q, k)] = st.enter_context(nc.semaphore(f"lane_{q}{k}"))
            block = st.enter_context(nc.Block())

            def emit(e, eng):
                for op in self.ops[e]:
                    for w in op.waits:
                        if w[0] == "op":
                            eng.wait_ge(sems[w[1].eng], w[1].val)
                        else:
                            eng.wait_ge(lanes[w[1]], 16 * w[2])
                    ins = op.fn(eng)
                    if op.dma:
                        ins.then_inc(lanes[op.lane], 16)
                    elif op.marked:
                        ins.then_inc(sems[e], 1)
                if e in self.NLANES:
                    for k in range(self.NLANES[e]):
                        c = self.lane_count.get((e, k), 0)
                        if c:
                            eng.wait_ge(lanes[(e, k)], 16 * c)

            @block.tensor
            def _(eng):
                emit("pe", eng)

            @block.scalar
            def _(eng):
                emit("act", eng)

            @block.vector
            def _(eng):
                emit("dve", eng)

            @block.gpsimd
            def _(eng):
                emit("pool", eng)

            @block.sync
            def _(eng):
                emit("sp", eng)


D = 1024
DIN = 3584
NT = 64
NT_RUN = 64
NPOOL_ROWS = 2560 * 128
POOL_ROWS_RUN = NPOOL_ROWS
NCORES_RUN = 8
ALPHA = (2.0 * 2) ** 0.25
EPS = 1e-5
SCALE = 32 ** -0.5
import math


def V(t, off, dims, p0=0, npart=128):
    ps = t[:].ap[0][0]
    return bass.AP(t, p0 * ps + off, [[ps, npart]] + [list(d) for d in dims])


def DV(t, off, dims):
    return bass.AP(t, off, [list(d) for d in dims])


class K:
    pass


class _Stop(Exception):
    pass


STOP_AT = None
DBG_MISC = False
ACT_PSUM = True
DBG_CAT = False
DBG_RES = None
DBG_SER = False
DBG_SER_LIST = ["kTt", "gate"]


def ck(tag):
    if STOP_AT is not None and tag == STOP_AT:
        raise _Stop()


def build_program():
    nc = bass.Bass("TRN2", target_bir_lowering=False)
    p = Prog(nc)
    k = K()
    R = {}

    def res(n):
        if n not in R:
            R[n] = p.res(n)
        return R[n]

    def din(name, shape, dt=F32):
        return nc.dram_tensor(name, list(shape), dt, kind="ExternalInput")

    def dout(name, shape, dt=F32):
        return nc.dram_tensor(name, list(shape), dt, kind="ExternalOutput")

    xp = din("xp", [8192, D]); xs = din("xs", [128, D])
    cp = din("cp", [128, D]); cs = din("cs", [128, D])
    pos_p = din("pos_p", [128, NT]); pos_s = din("pos_s", [128, 1]); invf = din("invf", [128, 16])
    ptab = din("ptab", [128, 256], I32); iota_p = din("iota_p", [128, 1])
    cache_kl = [din("cache_k0", [POOL_ROWS_RUN, 256]), din("cache_k1", [POOL_ROWS_RUN, 256])]; cache_vl = [din("cache_v0", [POOL_ROWS_RUN, 256]), din("cache_v1", [POOL_ROWS_RUN, 256])]
    st_conv = din("st_conv", [2, 16, 30, 256]); st_hgrn = din("st_hgrn", [2, 16, 4, 64, 64])
    w_ada = din("w_ada", [2, D, 3 * D]); b_ada = din("b_ada", [2, 3 * D]); w_in = din("w_in", [2, D, DIN])
    lam_qk = din("lam_qk", [2, 128]); attn_g = din("attn_g", [2, 64])
    sg_g = din("sg_g", [2, 256]); sg_b = din("sg_b", [2, 256]); w_s = din("w_s", [2, 4, 128, 128]); b_s = din("b_s", [2, 4, 128])
    conv_w = din("conv_w", [2, 31, 256]); conv_b = din("conv_b", [2, 256]); cn_g = din("cn_g", [2, 256]); cn_b = din("cn_b", [2, 256])
    w_pw = din("w_pw", [2, 256, 256]); lowb = din("lowb", [2, 256]); hg_g = din("hg_g", [2, 64])
    w_out = din("w_out", [2, D, D]); ln_g = din("ln_g", [2, D]); ln_b = din("ln_b", [2, D])
    c_ident = din("c_ident", [128, 128]); c_tril = din("c_tril", [128, 128]); c_ms = din("c_ms", [128, 128])
    c_mcum = din("c_mcum", [2, 128, 128]); c_mlast = din("c_mlast", [2, 128, 128]); c_matt = din("c_matt", [2, 128, 128])
    c_cm = din("c_cm", [2, 128, 8 * 128]); c_seg = din("c_seg", [2, 128, 16 * 64]); c_rowm = din("c_rowm", [128, 4])

    y_p = dout("y_p", [8192, D]); y_s = dout("y_s", [128, D])
    nk_p = dout("nk_p", [2, 8192, 256]); nv_p = dout("nv_p", [2, 8192, 256])
    nk_s = dout("nk_s", [2, 128, 256]); nv_s = dout("nv_s", [2, 128, 256]); nch_s = dout("nch_s", [2, 128, 256])
    ncv_p = dout("ncv_p", [2, 30, 256]); ncv_s = dout("ncv_s", [2, 16, 30, 256])
    nh_p = dout("nh_p", [2, 4, 64, 64]); nh_s = dout("nh_s", [2, 16, 4, 64, 64])
    dbg_cat = dout("dbg_cat", [2, 8192, D], BF16) if DBG_CAT else None
    y1s = dout("y1s_scratch", [128, D])
    NTILES = NT_RUN
    y1 = dout("y1_scratch", [8192, D])
    ktD = dout("ktD", [128, 2 * 8192], BF16)
    vD = dout("vD", [8192, 260], BF16)

    def sb(name, free, dt=F32):
        return nc.alloc_sbuf_tensor(name, [128, free], dt)

    def ps(name, free, dt=F32):
        return nc.alloc_psum_tensor(name, [128, free], dt)

    zA = ps("zA", 512); zB = ps("zB", 512); tpb = ps("tpb", 1024, BF16)
    sc = [ps("sc0", 512), ps("sc1", 512)]
    ot = [ps("ot0", 512), ps("ot1", 512)]
    misc = ps("misc", 512)

    win = sb("win", 8 * DIN, BF16)
    wout = sb("wout", 8 * D, BF16)
    wpw = sb("wpw", 2 * 256, BF16)
    stg = [sb("stg0", 512)]
    ktB = [sb(f"ktB{j}", 256, BF16) for j in range(4)]
    vB = [sb(f"vB{j}", 260, BF16) for j in range(4)]
    identf = sb("identf", 128); identb = sb("identb", 128, BF16)
    trilb = sb("trilb", 128, BF16); msb = sb("msb", 128, BF16)
    mcum = sb("mcum", 2 * 128); mlast = sb("mlast", 2 * 128); mattb = sb("mattb", 2 * 512, BF16)
    cmb = sb("cmb", 2 * 1024, BF16); segb = sb("segb", 2 * 1024, BF16); rowm = sb("rowm", 4)
    mod = sb("mod", 3 * D)
    cosp = sb("cosp", NT * 16); sinp = sb("sinp", NT * 16); coss = sb("coss", 16); sins = sb("sins", 16)
    prm = sb("prm", 10 * 256)
    lnp = sb("lnp", 2 * D)
    lamt = sb("lamt", 8)
    wmT = sb("wmT", 2 * 4 * 128, BF16)
    bsE = sb("bsE", 2 * 256)
    cwT = sb("cwT", 2 * 31)
    idx = sb("idx", 256, I32); idxf = sb("idxf", 256)
    xt = sb("xt", D); hb = sb("hb", D, BF16); hT = sb("hT", D, BF16); t1 = sb("t1", D)
    qk = sb("qk", 512); qkr = sb("qkr", 512); qb = sb("qb", 256, BF16); kb = sb("kb", 256, BF16)
    qblk = sb("qblk", 2 * 4 * 128, BF16)
    gate = sb("gate", 4 * 256)
    vnb = sb("vnb", 256, BF16)
    hcT = sb("hcT", 2 * 640, BF16)
    cyb = sb("cyb", 256, BF16); cyT = sb("cyT", 256, BF16)
    dq = sb("dq", 256); df = sb("df", 256); lf = sb("lf", 256); kd = sb("kd", 256); vdb = sb("vdb", 256, BF16)
    cl = sb("cl", 512); e1 = sb("e1", 256); e2 = sb("e2", 256); e3 = sb("e3", 256)
    qdup = sb("qdup", 512, BF16); kt_ = sb("kt_", 256, BF16); kpdup = sb("kpdup", 512, BF16); eldup = sb("eldup", 512)
    q2T = sb("q2T", 512, BF16); khT = sb("khT", 512, BF16); a2T = sb("a2T", 512)
    qexp = sb("qexp", 1024, BF16)
    vexp = sb("vexp", 1024, BF16)
    attm = sb("attm", 512, BF16)
    sst = sb("sst", 8 * 64, BF16)
    sstf = sb("sstf", 8 * 64)
    Sf = sb("Sf", 4 * 64)
    od = sb("od", 256); sq = sb("sq", 256); ss4 = sb("ss4", 8)
    cat = sb("cat", D, BF16); catT = sb("catT", D, BF16)
    pt_ = sb("pt_", 512, BF16)
    oa = sb("oa", 2 * 260); rl = sb("rl", 8); oat = sb("oat", 256)
    st2 = sb("st2", 8)
    kpg = sb("kpg", 256); vpg = sb("vpg", 256); kpb = sb("kpb", 256, BF16); kpT = sb("kpT", 256, BF16); vpa = sb("vpa", 260, BF16)


    def rs(names):
        return [res(n) for n in names]

    def ACT(fn, r, w): p.act(fn, rs(r), rs(w))
    def DVE(fn, r, w): p.dve(fn, rs(r), rs(w))
    def POOL(fn, r, w): p.pool(fn, rs(r), rs(w))
    def PE(fn, r, w): p.pe(fn, rs(r), rs(w))
    def DMA(fn, r, w, q="sp"): p.dma(fn, rs(r), rs(w), q=q)

    def bcrow(dt_, off, n):
        return DV(dt_, off, [[0, 128], [1, n]])

    def load_const(dst, nfree, src_ap, name, cast_to=None, tmp=None):
        if cast_to is None:
            DMA(lambda e: e.dma_start(out=dst[:, 0:nfree], in_=src_ap), [], [name])
        else:
            DMA(lambda e: e.dma_start(out=tmp[:, 0:nfree], in_=src_ap), [], ["t1"])
            DVE(lambda e: e.tensor_copy(out=dst[:, 0:nfree], in_=tmp[:, 0:nfree]), ["t1"], [name])

    stgb = [sb("stgb0", 512, BF16), sb("stgb1", 512, BF16)]
    kTt = sb("kTt", 256, BF16); vat = sb("vat", 260, BF16)
    TWO_PI = 2.0 * math.pi
    load_const(identf, 128, c_ident[:, :], "identf")
    DVE(lambda e: e.tensor_copy(out=identb[:], in_=identf[:]), ["identf"], ["identb"])
    load_const(trilb, 128, c_tril[:, :], "trilb", BF16, t1)
    load_const(msb, 128, c_ms[:, :], "msb", BF16, t1)
    load_const(mcum, 256, c_mcum.ap().rearrange("k p t -> p k t"), "mcum")
    load_const(mlast, 256, c_mlast.ap().rearrange("k p t -> p k t"), "mlast")
    for kk in range(2):
        DMA(lambda e, kk=kk: e.dma_start(out=t1[:, 0:128], in_=c_matt[kk, :, :]), [], ["t1"])
        DVE(lambda e, kk=kk: e.tensor_copy(out=V(mattb, kk * 512, [[128, 4], [1, 128]]), in_=V(t1, 0, [[0, 4], [1, 128]])), ["t1"], ["mattb"])
        DMA(lambda e, kk=kk: e.dma_start(out=t1[:, 0:1024], in_=c_cm[kk, :, :]), [], ["t1"])
        DVE(lambda e, kk=kk: e.tensor_copy(out=cmb[:, kk * 1024:(kk + 1) * 1024], in_=t1[:, 0:1024]), ["t1"], ["cmb"])
        DMA(lambda e, kk=kk: e.dma_start(out=t1[:, 0:1024], in_=c_seg[kk, :, :]), [], ["t1"])
        DVE(lambda e, kk=kk: e.tensor_copy(out=segb[:, kk * 1024:(kk + 1) * 1024], in_=t1[:, 0:1024]), ["t1"], ["segb"])
    load_const(rowm, 4, c_rowm[:, :], "rowm")
    DMA(lambda e: e.dma_start(out=t1[:, 0:NT], in_=pos_p[:, :]), [], ["t1"])
    DMA(lambda e: e.dma_start(out=t1[:, 64:65], in_=pos_s[:, :]), [], ["t1"])
    DMA(lambda e: e.dma_start(out=t1[:, 128:144], in_=invf[:, :]), [], ["t1"])
    DVE(lambda e: e.tensor_tensor(out=V(xt, 0, [[16, NT], [1, 16]]), in0=V(t1, 0, [[1, NT], [0, 16]]), in1=V(t1, 128, [[0, NT], [1, 16]]), op=ALU.mult), ["t1"], ["xt"])
    DVE(lambda e: e.tensor_tensor(out=V(qk, 0, [[1, 16]]), in0=V(t1, 64, [[0, 16]]), in1=V(t1, 128, [[1, 16]]), op=ALU.mult), ["t1"], ["qk"])

    C1 = 6.28125
    C2 = TWO_PI - C1

    def sincos(dsin, dcos, sname, cname, src_ap, srcname, n):
        r = mod[:, 0:n]; ki = mod[:, 1024:1024 + n].bitcast(I32); kf = mod[:, 2048:2048 + n]; m = mod[:, 1024:1024 + n]
        DVE(lambda e: e.tensor_scalar(out=kf, in0=src_ap, scalar1=1.0 / TWO_PI, scalar2=None, op0=ALU.mult), [srcname], ["mod"])
        DVE(lambda e: e.tensor_copy(out=ki, in_=kf), ["mod"], ["mod"])
        DVE(lambda e: e.tensor_copy(out=kf, in_=ki), ["mod"], ["mod"])
        DVE(lambda e: e.scalar_tensor_tensor(out=r, in0=kf, scalar=-C1, in1=src_ap, op0=ALU.mult, op1=ALU.add), ["mod", srcname], ["mod"])
        DVE(lambda e: e.scalar_tensor_tensor(out=r, in0=kf, scalar=-C2, in1=r, op0=ALU.mult, op1=ALU.add), ["mod"], ["mod"])
        for which in range(2):
            if which == 1:
                DVE(lambda e: e.tensor_scalar_add(out=r, in0=r, scalar1=0.5 * math.pi), ["mod"], ["mod"])
            DVE(lambda e: e.tensor_scalar(out=m, in0=r, scalar1=math.pi, scalar2=None, op0=ALU.is_gt), ["mod"], ["mod"])
            DVE(lambda e: e.scalar_tensor_tensor(out=r, in0=m, scalar=-TWO_PI, in1=r, op0=ALU.mult, op1=ALU.add), ["mod"], ["mod"])
            DVE(lambda e: e.tensor_scalar(out=m, in0=r, scalar1=-math.pi, scalar2=None, op0=ALU.is_lt), ["mod"], ["mod"])
            DVE(lambda e: e.scalar_tensor_tensor(out=r, in0=m, scalar=TWO_PI, in1=r, op0=ALU.mult, op1=ALU.add), ["mod"], ["mod"])
            dst, dn = (dsin, sname) if which == 0 else (dcos, cname)
            ACT(lambda e, dst=dst: e.activation(out=dst[:, 0:n], in_=r, func=AF.Sin), ["mod"], [dn])
    sincos(sinp, cosp, "sinp", "cosp", xt[:, 0:NT * 16], "xt", NT * 16)
    sincos(sins, coss, "sins", "coss", qk[:, 0:16], "qk", 16)
    DMA(lambda e: e.dma_start(out=idx[:, :], in_=ptab[:, :]), [], ["idx"])
    DMA(lambda e: e.dma_start(out=st2[:, 0:1], in_=iota_p[:, :]), [], ["st2"])
    DVE(lambda e: e.tensor_copy(out=idxf[:, :], in_=idx[:, :]), ["idx"], ["idxf"])
    DVE(lambda e: e.tensor_scalar(out=idxf[:, :], in0=idxf[:, :], scalar1=128.0, scalar2=st2[:, 0:1], op0=ALU.mult, op1=ALU.add), ["idxf", "st2"], ["idxf"])
    DVE(lambda e: e.tensor_copy(out=idx[:, :], in_=idxf[:, :]), ["idxf"], ["idx"])
    for tv, nm in ((vB[0], "vB0"), (vB[1], "vB1"), (vpa, "vpa"), (vat, "vat")):
        DVE(lambda e, tv=tv: e.memset(tv[:, :], 1.0), [], [nm])

    k.__dict__.update(locals())
    try:
        ck('consts')
        for l in range(2):
            emit_layer(k, l)
    except _Stop:
        pass
    p.finalize_and_emit()
    return k


def emit_layer(k, l):
    g = k.__dict__
    p = k.p; nc = k.nc
    ACT, DVE, POOL, PE, DMA, res = k.ACT, k.DVE, k.POOL, k.PE, k.DMA, k.res
    (zA, zB, tpb, sc, ot, misc, win, wout, wpw, stg, stgb, identf, identb, trilb, msb, mcum, mlast, mattb, cmb, segb, rowm, mod,
     cosp, sinp, coss, sins, prm, lnp, lamt, wmT, bsE, cwT, idx, xt, hb, hT, t1, qk, qkr, qb, kb, qblk, gate, vnb,
     hcT, cyb, cyT, dq, df, lf, kd, vdb, cl, e1, e2, e3, qdup, kt_, kpdup, eldup, q2T, khT, a2T, qexp, vexp, attm,
     sst, sstf, Sf, od, sq, ss4, cat, catT, pt_, oa, rl, oat, st2, kpg, vpg, kpb, kpT, vpa, kTt, vat, ktB, vB) = [g[n] for n in (
        "zA zB tpb sc ot misc win wout wpw stg stgb identf identb trilb msb mcum mlast mattb cmb segb rowm mod "
        "cosp sinp coss sins prm lnp lamt wmT bsE cwT idx xt hb hT t1 qk qkr qb kb qblk gate vnb "
        "hcT cyb cyT dq df lf kd vdb cl e1 e2 e3 qdup kt_ kpdup eldup q2T khT a2T qexp vexp attm "
        "sst sstf Sf od sq ss4 cat catT pt_ oa rl oat st2 kpg vpg kpb kpT vpa kTt vat ktB vB").split()]
    lam_init = 0.8 - 0.6 * math.exp(-0.3 * l)
    zz = [zA, zB]; zn = ["zA", "zB"]

    cnt = [0]

    def load_w(dst, dst_off, dram, r0, c0, ncols, dname):
        i = 0; cnt[0] += 1
        DMA(lambda e: e.dma_start(out=stg[i][:, 0:ncols], in_=dram[l, r0:r0 + 128, c0:c0 + ncols]), [], [f"stg{i}"])
        POOL(lambda e: e.tensor_copy(out=dst[:, dst_off:dst_off + ncols], in_=stg[i][:, 0:ncols]), [f"stg{i}"], [dname])

    for kc in range(8):
        for n in range(7):
            load_w(win, kc * DIN + n * 512, k.w_in, kc * 128, n * 512, 512, "win")
        for n in range(2):
            load_w(wout, kc * D + n * 512, k.w_out, kc * 128, n * 512, 512, "wout")
    for ch in range(2):
        load_w(wpw, ch * 256, k.w_pw, ch * 128, 0, 256, "wpw")

    def bc(dram, n, slot, rep=1):
        if rep == 1:
            src = DV(dram, l * n, [[0, 128], [1, n]])
        else:
            src = DV(dram, l * n, [[0, 128], [0, rep], [1, n]])
        DMA(lambda e: e.dma_start(out=prm[:, slot * 256:(slot + 1) * 256] if rep == 1 else V(prm, slot * 256, [[n, rep], [1, n]]), in_=src), [], ["prm"])
    bc(k.sg_g, 256, 0); bc(k.sg_b, 256, 1); bc(k.conv_b, 256, 2); bc(k.cn_g, 256, 3); bc(k.cn_b, 256, 4)
    bc(k.attn_g, 64, 7, 4); bc(k.hg_g, 64, 8, 4)
    DVE(lambda e: e.tensor_scalar_mul(out=prm[:, 7 * 256:8 * 256], in0=prm[:, 7 * 256:8 * 256], scalar1=1.0 - lam_init), ["prm"], ["prm"])
    if l == 0:
        DVE(lambda e: e.memset(prm[:, 5 * 256:6 * 256], 0.0), [], ["prm"])
    else:
        DMA(lambda e: e.dma_start(out=prm[:, 5 * 256:6 * 256], in_=DV(k.lowb, 256, [[0, 128], [1, 256]])), [], ["prm"])
        DMA(lambda e: e.dma_start(out=prm[:, 9 * 256:10 * 256], in_=DV(k.lowb, 0, [[0, 128], [1, 256]])), [], ["prm"])
        DVE(lambda e: e.tensor_tensor(out=prm[:, 5 * 256:6 * 256], in0=prm[:, 5 * 256:6 * 256], in1=prm[:, 9 * 256:10 * 256], op=ALU.subtract), ["prm"], ["prm"])
        ACT(lambda e: e.activation(out=prm[:, 5 * 256:6 * 256], in_=prm[:, 5 * 256:6 * 256], func=AF.Sigmoid), ["prm"], ["prm"])
    DVE(lambda e: e.tensor_scalar(out=prm[:, 6 * 256:7 * 256], in0=prm[:, 5 * 256:6 * 256], scalar1=-1.0, scalar2=1.0, op0=ALU.mult, op1=ALU.add), ["prm"], ["prm"])
    DMA(lambda e: e.dma_start(out=lnp[:, 0:D], in_=DV(k.ln_g, l * D, [[0, 128], [1, D]])), [], ["lnp"])
    DMA(lambda e: e.dma_start(out=lnp[:, D:2 * D], in_=DV(k.ln_b, l * D, [[0, 128], [1, D]])), [], ["lnp"])
    DMA(lambda e: e.dma_start(out=e1[:, 0:128], in_=DV(k.lam_qk, l * 128, [[0, 128], [1, 128]])), [], ["e1"])
    DVE(lambda e: e.tensor_tensor(out=V(e2, 0, [[32, 2], [1, 32]]), in0=V(e1, 0, [[64, 2], [1, 32]]), in1=V(e1, 32, [[64, 2], [1, 32]]), op=ALU.mult), ["e1"], ["e2"])
    DVE(lambda e: e.reduce_sum(out=lamt[:, 1:3], in_=V(e2, 0, [[32, 2], [1, 32]]), axis=AX.X), ["e2"], ["lamt"])
    ACT(lambda e: e.activation(out=lamt[:, 1:3], in_=lamt[:, 1:3], func=AF.Exp), ["lamt"], ["lamt"])
    DVE(lambda e: e.tensor_tensor(out=lamt[:, 0:1], in0=lamt[:, 2:3], in1=lamt[:, 1:2], op=ALU.subtract), ["lamt"], ["lamt"])
    DVE(lambda e: e.tensor_scalar_add(out=lamt[:, 0:1], in0=lamt[:, 0:1], scalar1=-lam_init), ["lamt"], ["lamt"])
    for kind in range(2):
        for gi in range(4):
            if kind == 0:
                DMA(lambda e, gi=gi: e.dma_start(out=t1[:, 0:128], in_=k.w_s[l, gi, :, :]), [], ["t1"])
                PE(lambda e: e.transpose(out=sc[0][:, 0:128], in_=t1[:, 0:128], identity=identf[:, :]), ["t1", "identf"], ["sc0"])
                DVE(lambda e, gi=gi: e.tensor_tensor(out=wmT[:, gi * 128:(gi + 1) * 128], in0=sc[0][:, 0:128], in1=trilb[:, :], op=ALU.mult), ["sc0", "trilb"], ["wmT"])
            else:
                DVE(lambda e: e.memset(t1[:, 0:128], 0.0), [], ["t1"])
                for s_ in range(16):
                    DMA(lambda e, gi=gi, s_=s_: e.dma_start(out=t1[8 * s_:8 * s_ + 8, 8 * s_:8 * s_ + 8], in_=k.w_s[l, gi, 0:8, 0:8]), [], ["t1"])
                PE(lambda e: e.transpose(out=sc[0][:, 0:128], in_=t1[:, 0:128], identity=identf[:, :]), ["t1", "identf"], ["sc0"])
                DVE(lambda e, gi=gi: e.tensor_tensor(out=wmT[:, 512 + gi * 128:512 + (gi + 1) * 128], in0=sc[0][:, 0:128], in1=msb[:, :], op=ALU.mult), ["sc0", "msb"], ["wmT"])
        if kind == 0:
            DMA(lambda e: e.dma_start(out=st2[:, 4:8], in_=k.b_s[l].rearrange("g t -> t g"), allow_slow_non_contiguous=True), [], ["st2"])
        else:
            for s_ in range(16):
                DMA(lambda e, s_=s_: e.dma_start(out=st2[8 * s_:8 * s_ + 8, 4:8], in_=k.b_s[l, :, 0:8].rearrange("g t -> t g"), allow_slow_non_contiguous=True), [], ["st2"])
        DVE(lambda e, kind=kind: e.tensor_copy(out=V(bsE, kind * 256, [[64, 4], [1, 64]]), in_=V(st2, 4, [[1, 4], [0, 64]])), ["st2"], ["bsE"])
    for ch in range(2):
        DMA(lambda e, ch=ch: e.dma_start(out=cwT[:, ch * 31:(ch + 1) * 31], in_=k.conv_w[l, :, ch * 128:(ch + 1) * 128].rearrange("j c -> c j"), allow_slow_non_contiguous=True), [], ["cwT"])
    DVE(lambda e: e.memset(Sf[:, :], 0.0), [], ["Sf"])

    ck(f'params{l}')
    def layer_norm(x, n, gap, bap, xname, pnames, junk, jname):
        DVE(lambda e: e.memset(st2[:, 0:4], 0.0), [], ["st2"])
        DVE(lambda e: e.reduce_sum(out=st2[:, 0:1], in_=x[:, 0:n], axis=AX.X), [xname], ["st2"])
        ACT(lambda e: e.activation(out=junk[:, 0:n], in_=x[:, 0:n], func=AF.Square, accum_out=st2[:, 1:2]), [xname], [jname, "st2"])
        DVE(lambda e: e.tensor_scalar_mul(out=st2[:, 0:2], in0=st2[:, 0:2], scalar1=1.0 / n), ["st2"], ["st2"])
        DVE(lambda e: e.tensor_tensor(out=st2[:, 2:3], in0=st2[:, 0:1], in1=st2[:, 0:1], op=ALU.mult), ["st2"], ["st2"])
        DVE(lambda e: e.tensor_tensor(out=st2[:, 1:2], in0=st2[:, 1:2], in1=st2[:, 2:3], op=ALU.subtract), ["st2"], ["st2"])
        DVE(lambda e: e.tensor_scalar_add(out=st2[:, 1:2], in0=st2[:, 1:2], scalar1=EPS), ["st2"], ["st2"])
        ACT(lambda e: e.activation(out=st2[:, 1:2], in_=st2[:, 1:2], func=AF.Sqrt), ["st2"], ["st2"])
        DVE(lambda e: e.reciprocal(out=st2[:, 1:2], in_=st2[:, 1:2]), ["st2"], ["st2"])
        DVE(lambda e: e.tensor_scalar(out=x[:, 0:n], in0=x[:, 0:n], scalar1=st2[:, 0:1], scalar2=st2[:, 1:2], op0=ALU.subtract, op1=ALU.mult), [xname, "st2"], [xname])
        DVE(lambda e: e.tensor_tensor(out=x[:, 0:n], in0=x[:, 0:n], in1=gap, op=ALU.mult), [xname] + pnames, [xname])
        DVE(lambda e: e.tensor_tensor(out=x[:, 0:n], in0=x[:, 0:n], in1=bap, op=ALU.add), [xname] + pnames, [xname])

    def rms_heads(x, xname, gslot, gcol, out_c0):
        DVE(lambda e: e.tensor_tensor(out=sq[:, :], in0=x[:, 0:256], in1=x[:, 0:256], op=ALU.mult), [xname], ["sq"])
        DVE(lambda e: e.reduce_sum(out=ss4[:, 0:4], in_=V(sq, 0, [[64, 4], [1, 64]]), axis=AX.X), ["sq"], ["ss4"])
        DVE(lambda e: e.tensor_scalar(out=ss4[:, 0:4], in0=ss4[:, 0:4], scalar1=1.0 / 64, scalar2=EPS, op0=ALU.mult, op1=ALU.add), ["ss4"], ["ss4"])
        ACT(lambda e: e.activation(out=ss4[:, 0:4], in_=ss4[:, 0:4], func=AF.Sqrt), ["ss4"], ["ss4"])
        DVE(lambda e: e.reciprocal(out=ss4[:, 0:4], in_=ss4[:, 0:4]), ["ss4"], ["ss4"])
        DVE(lambda e: e.tensor_tensor(out=V(x, 0, [[64, 4], [1, 64]]), in0=V(x, 0, [[64, 4], [1, 64]]), in1=V(ss4, 0, [[1, 4], [0, 64]]), op=ALU.mult), [xname, "ss4"], [xname])
        DVE(lambda e: e.tensor_tensor(out=x[:, 0:256], in0=x[:, 0:256], in1=prm[:, gslot * 256:(gslot + 1) * 256], op=ALU.mult), [xname, "prm"], [xname])
        DVE(lambda e: e.tensor_tensor(out=cat[:, out_c0:out_c0 + 256], in0=x[:, 0:256], in1=gate[:, gcol:gcol + 256], op=ALU.mult), [xname, "gate"], ["cat"])

    def compute_mod(csrc):
        DMA(lambda e: e.dma_start(out=t1[:, :], in_=csrc[:, :]), [], ["t1"])
        ACT(lambda e: e.activation(out=hb[:, :], in_=t1[:, :], func=AF.Silu), ["t1"], ["hb"])
        for kc in range(8):
            PE(lambda e, kc=kc: e.transpose(out=tpb[:, kc * 128:(kc + 1) * 128], in_=hb[:, kc * 128:(kc + 1) * 128], identity=identb[:, :]), ["hb", "identb"], ["tpb"])
        DVE(lambda e: e.tensor_copy(out=hT[:, :], in_=tpb[:, :]), ["tpb"], ["hT"])
        for n in range(6):
            for kc in range(8):
                i = 0; cnt[0] += 1
                DMA(lambda e, i=i, kc=kc, n=n: e.dma_start(out=stg[i][:, :], in_=k.w_ada[l, kc * 128:(kc + 1) * 128, n * 512:(n + 1) * 512]), [], [f"stg{i}"])
                POOL(lambda e, i=i: e.tensor_copy(out=stgb[i][:, :], in_=stg[i][:, :]), [f"stg{i}"], [f"stgb{i}"])
                PE(lambda e, i=i, kc=kc, n=n: e.matmul(out=zz[n % 2][:, :], lhsT=hT[:, kc * 128:(kc + 1) * 128], rhs=stgb[i][:, :], start=(kc == 0), stop=(kc == 7)),
                   ["hT", f"stgb{i}"], [zn[n % 2]])
            DMA(lambda e, n=n: e.dma_start(out=t1[:, 0:512], in_=DV(k.b_ada, l * 3 * D + n * 512, [[0, 128], [1, 512]])), [], ["t1"])
            DVE(lambda e, n=n: e.tensor_tensor(out=mod[:, n * 512:(n + 1) * 512], in0=zz[n % 2][:, :], in1=t1[:, 0:512], op=ALU.add), [zn[n % 2], "t1"], ["mod"])
        DVE(lambda e: e.tensor_scalar_add(out=mod[:, D:2 * D], in0=mod[:, D:2 * D], scalar1=1.0), ["mod"], ["mod"])

    def tile(kind, i):
        kk = kind
        nseg = 8 if kind == 0 else 16
        nj = nseg // 2
        seglen = 128 // nseg
        if kind == 0:
            src = (k.xp if l == 0 else k.y1)[i * 128:(i + 1) * 128, :]
            srcn = [] if l == 0 else ["y1"]
        else:
            src = (k.xs if l == 0 else k.y1s)[:, :]
            srcn = [] if l == 0 else ["y1s"]
        DMA(lambda e: e.dma_start(out=xt[:, :], in_=src), srcn, ["xt"])
        DVE(lambda e: e.tensor_tensor(out=t1[:, :], in0=xt[:, :], in1=mod[:, D:2 * D], op=ALU.mult), ["xt", "mod"], ["t1"])
        DVE(lambda e: e.tensor_tensor(out=hb[:, :], in0=t1[:, :], in1=mod[:, 0:D], op=ALU.add), ["t1", "mod"], ["hb"])
        for kc in range(8):
            PE(lambda e, kc=kc: e.transpose(out=tpb[:, kc * 128:(kc + 1) * 128], in_=hb[:, kc * 128:(kc + 1) * 128], identity=identb[:, :]), ["hb", "identb"], ["tpb"])
        DVE(lambda e: e.tensor_copy(out=hT[:, :], in_=tpb[:, :]), ["tpb"], ["hT"])

        def inproj(n):
            z = zz[n % 2] if not (DBG_MISC and n == 2) else misc
            for kc in range(8):
                PE(lambda e, kc=kc: e.matmul(out=z[:, :], lhsT=hT[:, kc * 128:(kc + 1) * 128], rhs=win[:, kc * DIN + n * 512:kc * DIN + (n + 1) * 512], start=(kc == 0), stop=(kc == 7)),
                   ["hT", "win"] + (DBG_SER_LIST if DBG_SER else []), [zn[n % 2] if not (DBG_MISC and n == 2) else "misc"])
            zname = (zn[n % 2] if not (DBG_MISC and n == 2) else "misc")
            DVE(lambda e: e.tensor_copy(out=qk[:, :], in_=z[:, :]), [zname], ["qk"])
            return qk, "qk"

        ck(f'hT{l}{kind}{i}')
        z, zr = inproj(0)
        ck(f'z0{l}{kind}{i}')
        if kind == 0:
            cs_ = V(cosp, i * 16, [[0, 16], [1, 16]]); sn_ = V(sinp, i * 16, [[0, 16], [1, 16]]); csn = ["cosp", "sinp"]
        else:
            cs_ = V(coss, 0, [[0, 16], [1, 16]]); sn_ = V(sins, 0, [[0, 16], [1, 16]]); csn = ["coss", "sins"]
        x1 = V(qk, 0, [[32, 16], [1, 16]]); x2 = V(qk, 16, [[32, 16], [1, 16]])
        o1 = V(qkr, 0, [[32, 16], [1, 16]]); o2 = V(qkr, 16, [[32, 16], [1, 16]]); tm = V(e1, 0, [[16, 16], [1, 16]])
        DVE(lambda e: e.tensor_tensor(out=o1, in0=x1, in1=cs_, op=ALU.mult), ["qk"] + csn, ["qkr"])
        DVE(lambda e: e.tensor_tensor(out=tm, in0=x2, in1=sn_, op=ALU.mult), ["qk"] + csn, ["e1"])
        DVE(lambda e: e.tensor_tensor(out=o1, in0=o1, in1=tm, op=ALU.subtract), ["qkr", "e1"], ["qkr"])
        DVE(lambda e: e.tensor_tensor(out=o2, in0=x2, in1=cs_, op=ALU.mult), ["qk"] + csn, ["qkr"])
        DVE(lambda e: e.tensor_tensor(out=tm, in0=x1, in1=sn_, op=ALU.mult), ["qk"] + csn, ["e1"])
        DVE(lambda e: e.tensor_tensor(out=o2, in0=o2, in1=tm, op=ALU.add), ["qkr", "e1"], ["qkr"])
        ck(f'rope{l}{kind}{i}')
        if kind == 0:
            DMA(lambda e: e.dma_start(out=k.nk_p[l, i * 128:(i + 1) * 128, :], in_=qkr[:, 256:512]), ["qkr"], [])
        else:
            DMA(lambda e: e.dma_start(out=k.nk_s[l, :, :], in_=qkr[:, 256:512]), ["qkr"], [])
        ck(f'nk{l}{kind}{i}')
        DVE(lambda e: e.tensor_copy(out=qb[:, :], in_=qkr[:, 0:256]), ["qkr"], ["qb"])
        DVE(lambda e: e.tensor_copy(out=kb[:, :], in_=qkr[:, 256:512]), ["qkr"], ["kb"])
        for ch in range(2):
            PE(lambda e, ch=ch: e.transpose(out=tpb[:, ch * 128:(ch + 1) * 128], in_=qb[:, ch * 128:(ch + 1) * 128], identity=identb[:, :]), ["qb", "identb"], ["tpb"])
            PE(lambda e, ch=ch: e.transpose(out=tpb[:, 256 + ch * 128:256 + (ch + 1) * 128], in_=kb[:, ch * 128:(ch + 1) * 128], identity=identb[:, :]), ["kb", "identb"], ["tpb"])
        for ch in range(2):
            for m in range(4):
                DVE(lambda e, ch=ch, m=m: e.tensor_scalar(out=qblk[:, ch * 512 + m * 128:ch * 512 + (m + 1) * 128], in0=tpb[:, ch * 128:(ch + 1) * 128], scalar1=rowm[:, m:m + 1], scalar2=None, op0=ALU.mult),
                    ["tpb", "rowm"], ["qblk"])
        ck(f'qblk{l}{kind}{i}')
        DVE(lambda e: e.tensor_copy(out=kTt[:, :], in_=tpb[:, 256:512]), ["tpb"], ["kTt"])
        ck(f'ktt{l}{kind}{i}')
        if kind == 0:
            DMA(lambda e: e.dma_start(out=k.ktD.ap().rearrange("p (c n) -> p c n", c=2)[:, :, i * 128:(i + 1) * 128], in_=V(kTt, 0, [[128, 2], [1, 128]])), ["kTt"], ["ktD"])
        ck(f'c0{l}{kind}{i}')
        z, zr = inproj(1)
        DVE(lambda e: e.tensor_copy(out=sq[:, :], in_=z[:, 0:256]), [zr], ["sq"])
        DVE(lambda e: e.tensor_copy(out=V(vat, 0, [[65, 4], [1, 64]]), in_=V(z, 0, [[64, 4], [1, 64]])), [zr], ["vat"])
        ACT(lambda e: e.activation(out=gate[:, 0:256], in_=z[:, 256:512], func=AF.Silu), [zr], ["gate"])
        if kind == 0:
            DMA(lambda e: e.dma_start(out=k.nv_p[l, i * 128:(i + 1) * 128, :], in_=sq[:, :]), ["sq"], [])
            DMA(lambda e: e.dma_start(out=k.vD[i * 128:(i + 1) * 128, :], in_=vat[:, :]), ["vat"], ["vD"])
        else:
            DMA(lambda e: e.dma_start(out=k.nv_s[l, :, :], in_=sq[:, :]), ["sq"], [])
        ck(f'c1{l}{kind}{i}')
        z, zr = inproj(2)
        ck(f'z2{l}{kind}{i}')
        ACT(lambda e: e.activation(out=qkr[:, :], in_=z[:, :], func=AF.Erf, scale=0.7071067811865476), [zr], ["qkr"])
        ck(f'erf{l}{kind}{i}')
        DVE(lambda e: e.tensor_scalar(out=qkr[:, :], in0=qkr[:, :], scalar1=1.0, scalar2=0.5, op0=ALU.add, op1=ALU.mult), ["qkr"], ["qkr"])
        DVE(lambda e: e.tensor_tensor(out=e3[:, :], in0=qkr[:, 0:256], in1=z[:, 0:256], op=ALU.mult), ["qkr", zr], ["e3"])
        DVE(lambda e: e.tensor_tensor(out=dq[:, :], in0=qkr[:, 256:512], in1=z[:, 256:512], op=ALU.mult), ["qkr", zr], ["dq"])
        ck(f'gelu{l}{kind}{i}')
        layer_norm(dq, 256, prm[:, 0:256], prm[:, 256:512], "dq", ["prm"], sq, "sq")
        ck(f'ln{l}{kind}{i}')
        if kind == 1:
            DMA(lambda e: e.dma_start(out=k.nch_s[l, :, :], in_=dq[:, :]), ["dq"], [])
        DVE(lambda e: e.tensor_copy(out=vnb[:, :], in_=dq[:, :]), ["dq"], ["vnb"])
        ck(f'c2{l}{kind}{i}')
        z, zr = inproj(3)
        ACT(lambda e: e.activation(out=gate[:, 256:512], in_=z[:, 0:256], func=AF.Silu), [zr], ["gate"])
        DVE(lambda e: e.tensor_copy(out=oat[:, :], in_=z[:, 256:512]), [zr], ["oat"])
        for gi in range(4):
            PE(lambda e, gi=gi: e.matmul(out=misc[:, gi * 64:(gi + 1) * 64], lhsT=wmT[:, kk * 512 + gi * 128:kk * 512 + (gi + 1) * 128], rhs=vnb[:, gi * 64:(gi + 1) * 64], start=True, stop=True),
               ["wmT", "vnb"], ["misc"])
        DVE(lambda e: e.tensor_tensor(out=od[:, :], in0=misc[:, 0:256], in1=bsE[:, kk * 256:(kk + 1) * 256], op=ALU.add), ["misc", "bsE"], ["od"])
        DVE(lambda e: e.tensor_tensor(out=od[:, :], in0=od[:, :], in1=e3[:, :], op=ALU.mult), ["od", "e3"], ["od"])
        DVE(lambda e: e.tensor_tensor(out=cat[:, 256:512], in0=od[:, :], in1=gate[:, 256:512], op=ALU.mult), ["od", "gate"], ["cat"])
        ck(f'c3{l}{kind}{i}')
        z, zr = inproj(4)
        ACT(lambda e: e.activation(out=df[:, :], in_=z[:, 0:256], func=AF.Sigmoid), [zr], ["df"])
        ACT(lambda e: e.activation(out=gate[:, 512:768], in_=z[:, 256:512], func=AF.Silu), [zr], ["gate"])
        DVE(lambda e: e.tensor_tensor(out=df[:, :], in0=df[:, :], in1=oat[:, :], op=ALU.mult), ["df", "oat"], ["df"])
        if kind == 0:
            if i == NT - 1:
                DMA(lambda e: e.dma_start(out=k.ncv_p[l, :, :], in_=df[98:128, :]), ["df"], [])
        else:
            for s_ in range(16):
                DMA(lambda e, s_=s_: e.dma_start(out=k.ncv_s[l, s_, 22:30, :], in_=df[8 * s_:8 * s_ + 8, :]), ["df"], [])
            DMA(lambda e: e.dma_start(out=k.ncv_s[l, :, 0:22, :], in_=k.st_conv[l, :, 8:30, :]), [], [])
        for ch in range(2):
            PE(lambda e, ch=ch: e.transpose(out=sc[0][:, ch * 128:(ch + 1) * 128], in_=df[:, ch * 128:(ch + 1) * 128], identity=identf[:, :]), ["df", "identf"], ["sc0"])
        if kind == 0:
            if i == 0:
                DVE(lambda e: e.memset(hcT[:, :], 0.0), [], ["hcT"])
            else:
                DVE(lambda e: e.tensor_copy(out=V(e1, 0, [[30, 2], [1, 30]]), in_=V(hcT, 128, [[158, 2], [1, 30]])), ["hcT"], ["e1"])
                DVE(lambda e: e.tensor_copy(out=V(hcT, 0, [[158, 2], [1, 30]]), in_=V(e1, 0, [[30, 2], [1, 30]])), ["e1"], ["hcT"])
            DVE(lambda e: e.tensor_copy(out=V(hcT, 30, [[158, 2], [1, 128]]), in_=V(sc[0], 0, [[128, 2], [1, 128]])), ["sc0"], ["hcT"])
        else:
            DVE(lambda e: e.tensor_copy(out=V(hcT, 30, [[608, 2], [38, 16], [1, 8]]), in_=V(sc[0], 0, [[128, 2], [8, 16], [1, 8]])), ["sc0"], ["hcT"])
            for q4 in range(4):
                DMA(lambda e, q4=q4: e.dma_start(out=kpg[0:120, :], in_=k.st_conv[l, 4 * q4:4 * q4 + 4, :, :].rearrange("s r c -> (s r) c")), [], ["kpg"])
                for ch in range(2):
                    PE(lambda e, ch=ch: e.transpose(out=sc[1][:, ch * 128:ch * 128 + 120], in_=kpg[0:120, ch * 128:(ch + 1) * 128], identity=identf[0:120, 0:120]), ["kpg", "identf"], ["sc1"])
                DVE(lambda e, q4=q4: e.tensor_copy(out=V(hcT, q4 * 4 * 38, [[608, 2], [38, 4], [1, 30]]), in_=V(sc[1], 0, [[128, 2], [30, 4], [1, 30]])), ["sc1"], ["hcT"])
        for j in range(31):
            for ch in range(2):
                cname = f"qk{ch}"
                if kind == 0:
                    win_ = V(hcT, ch * 158 + j, [[1, 128]]); oap = V(qk, ch * 128, [[1, 128]])
                else:
                    win_ = V(hcT, ch * 608 + j, [[38, 16], [1, 8]]); oap = V(qk, ch * 128, [[8, 16], [1, 8]])
                if j == 0:
                    DVE(lambda e, win_=win_, oap=oap, ch=ch, j=j: e.tensor_scalar(out=oap, in0=win_, scalar1=cwT[:, ch * 31 + j:ch * 31 + j + 1], scalar2=None, op0=ALU.mult), ["hcT", "cwT"], [cname])
                else:
                    DVE(lambda e, win_=win_, oap=oap, ch=ch, j=j: e.scalar_tensor_tensor(out=oap, in0=win_, scalar=cwT[:, ch * 31 + j:ch * 31 + j + 1], in1=oap, op0=ALU.mult, op1=ALU.add), ["hcT", "cwT", cname], [cname])
        for ch in range(2):
            PE(lambda e, ch=ch: e.transpose(out=sc[1][:, ch * 128:(ch + 1) * 128], in_=qk[:, ch * 128:(ch + 1) * 128], identity=identf[:, :]), [f"qk{ch}", "identf"], ["sc1"])
        DVE(lambda e: e.tensor_tensor(out=e2[:, :], in0=sc[1][:, 0:256], in1=prm[:, 512:768], op=ALU.add), ["sc1", "prm"], ["e2"])
        layer_norm(e2, 256, prm[:, 768:1024], prm[:, 1024:1280], "e2", ["prm"], sq, "sq")
        ACT(lambda e: e.activation(out=cyb[:, :], in_=e2[:, :], func=AF.Silu), ["e2"], ["cyb"])
        for ch in range(2):
            PE(lambda e, ch=ch: e.transpose(out=tpb[:, ch * 128:(ch + 1) * 128], in_=cyb[:, ch * 128:(ch + 1) * 128], identity=identb[:, :]), ["cyb", "identb"], ["tpb"])
        DVE(lambda e: e.tensor_copy(out=cyT[:, :], in_=tpb[:, 0:256]), ["tpb"], ["cyT"])
        for ch in range(2):
            PE(lambda e, ch=ch: e.matmul(out=misc[:, 0:256], lhsT=cyT[:, ch * 128:(ch + 1) * 128], rhs=wpw[:, ch * 256:(ch + 1) * 256], start=(ch == 0), stop=(ch == 1)), ["cyT", "wpw"], ["misc"])
        DVE(lambda e: e.tensor_tensor(out=cat[:, 512:768], in0=misc[:, 0:256], in1=gate[:, 512:768], op=ALU.mult), ["misc", "gate"], ["cat"])
        ck(f'c4{l}{kind}{i}')
        z, zr = inproj(5)
        ACT(lambda e: e.activation(out=dq[:, :], in_=z[:, 0:256], func=AF.Silu), [zr], ["dq"])
        ACT(lambda e: e.activation(out=df[:, :], in_=z[:, 256:512], func=AF.Sigmoid), [zr], ["df"])
        DVE(lambda e: e.tensor_tensor(out=df[:, :], in0=df[:, :], in1=prm[:, 6 * 256:7 * 256], op=ALU.mult), ["df", "prm"], ["df"])
        DVE(lambda e: e.tensor_tensor(out=df[:, :], in0=df[:, :], in1=prm[:, 5 * 256:6 * 256], op=ALU.add), ["df", "prm"], ["df"])
        DVE(lambda e: e.tensor_scalar(out=kd[:, :], in0=df[:, :], scalar1=-1.0, scalar2=1.0, op0=ALU.mult, op1=ALU.add), ["df"], ["kd"])
        DVE(lambda e: e.tensor_scalar_max(out=lf[:, :], in0=df[:, :], scalar1=1e-30), ["df"], ["lf"])
        ACT(lambda e: e.activation(out=lf[:, :], in_=lf[:, :], func=AF.Ln), ["lf"], ["lf"])
        z, zr = inproj(6)
        DVE(lambda e: e.tensor_copy(out=vdb[:, :], in_=z[:, 0:256]), [zr], ["vdb"])
        ACT(lambda e: e.activation(out=gate[:, 768:1024], in_=z[:, 256:512], func=AF.Silu), [zr], ["gate"])
        ck(f'c6{l}{kind}{i}')
        PE(lambda e: e.matmul(out=sc[0][:, 0:256], lhsT=mcum[:, kk * 128:(kk + 1) * 128], rhs=lf[:, :], start=True, stop=True), ["mcum", "lf"], ["sc0"])
        PE(lambda e: e.matmul(out=sc[0][:, 256:512], lhsT=mlast[:, kk * 128:(kk + 1) * 128], rhs=lf[:, :], start=True, stop=True), ["mlast", "lf"], ["sc0"])
        DVE(lambda e: e.tensor_copy(out=cl[:, 0:512], in_=sc[0][:, 0:512]), ["sc0"], ["cl"])
        ACT(lambda e: e.activation(out=e1[:, :], in_=cl[:, 0:256], func=AF.Exp), ["cl"], ["e1"])
        ACT(lambda e: e.activation(out=e2[:, :], in_=cl[:, 0:256], func=AF.Exp, scale=-1.0), ["cl"], ["e2"])
        DVE(lambda e: e.tensor_tensor(out=cl[:, 0:256], in0=cl[:, 256:512], in1=cl[:, 0:256], op=ALU.subtract), ["cl"], ["cl"])
        ACT(lambda e: e.activation(out=e3[:, :], in_=cl[:, 0:256], func=AF.Exp), ["cl"], ["e3"])
        ACT(lambda e: e.activation(out=V(eldup, 0, [[128, 4], [64, 2], [1, 64]]), in_=V(cl, 256, [[64, 4], [0, 2], [1, 64]]), func=AF.Exp), ["cl"], ["eldup"])
        dup_o = lambda t_: V(t_, 0, [[128, 4], [64, 2], [1, 64]])
        dup_i = lambda t_: V(t_, 0, [[64, 4], [0, 2], [1, 64]])
        DVE(lambda e: e.tensor_tensor(out=dup_o(qdup), in0=dup_i(dq), in1=dup_i(e1), op=ALU.mult), ["dq", "e1"], ["qdup"])
        DVE(lambda e: e.tensor_tensor(out=kt_[:, :], in0=kd[:, :], in1=e2[:, :], op=ALU.mult), ["kd", "e2"], ["kt_"])
        DVE(lambda e: e.tensor_tensor(out=dup_o(kpdup), in0=dup_i(kd), in1=dup_i(e3), op=ALU.mult), ["kd", "e3"], ["kpdup"])
        for h in range(4):
            PE(lambda e, h=h: e.transpose(out=tpb[:, h * 128:(h + 1) * 128], in_=qdup[:, h * 128:(h + 1) * 128], identity=identb[:, :]), ["qdup", "identb"], ["tpb"])
            PE(lambda e, h=h: e.transpose(out=tpb[0:64, 512 + h * 128:512 + (h + 1) * 128], in_=kt_[:, h * 64:(h + 1) * 64], identity=identb[:, :]), ["kt_", "identb"], ["tpb"])
            PE(lambda e, h=h: e.transpose(out=sc[1][:, h * 128:(h + 1) * 128], in_=eldup[:, h * 128:(h + 1) * 128], identity=identf[:, :]), ["eldup", "identf"], ["sc1"])
        DVE(lambda e: e.tensor_copy(out=q2T[:, :], in_=tpb[:, 0:512]), ["tpb"], ["q2T"])
        DVE(lambda e: e.tensor_copy(out=khT[0:64, :], in_=tpb[0:64, 512:1024]), ["tpb"], ["khT"])
        DVE(lambda e: e.tensor_copy(out=a2T[:, :], in_=sc[1][:, :]), ["sc1"], ["a2T"])
        for h in range(4):
            PE(lambda e, h=h: e.matmul(out=misc[:, h * 128:(h + 1) * 128], lhsT=khT[0:64, h * 128:(h + 1) * 128], rhs=q2T[0:64, h * 128:(h + 1) * 128], start=True, stop=True), ["khT", "q2T"], ["misc"])
        DVE(lambda e: e.tensor_tensor(out=attm[:, :], in0=misc[:, :], in1=mattb[:, kk * 512:(kk + 1) * 512], op=ALU.mult), ["misc", "mattb"], ["attm"])
        for h in range(4):
            if kind == 1:
                for du in range(2):
                    DMA(lambda e, h=h, du=du: e.dma_start(out=V(sstf, 0, [[64, 8], [1, 64]], p0=du * 64, npart=64),
                                                          in_=DV(k.st_hgrn, (l * 16 + du) * 16384 + h * 4096, [[64, 64], [2 * 16384, 8], [1, 64]])), [], ["sstf"])
                ACT(lambda e: e.copy(out=sst[:, :], in_=sstf[:, :]), ["sstf"], ["sst"])
            DVE(lambda e, h=h: e.tensor_tensor(out=V(vexp, 0, [[64, nseg], [1, 64]]), in0=V(vdb, h * 64, [[0, nseg], [1, 64]]), in1=V(segb, kk * 1024, [[64, nseg], [1, 64]]), op=ALU.mult), ["vdb", "segb"], ["vexp"])
            PE(lambda e, h=h: e.matmul(out=ot[0][:, :], lhsT=kpdup[:, h * 128:(h + 1) * 128], rhs=vexp[:, 0:512], start=True, stop=True), ["kpdup", "vexp"], ["ot0"])
            if kind == 1:
                PE(lambda e, h=h: e.matmul(out=ot[1][:, :], lhsT=kpdup[:, h * 128:(h + 1) * 128], rhs=vexp[:, 512:1024], start=True, stop=True), ["kpdup", "vexp"], ["ot1"])
            for sg in range(nseg):
                hf = sg % 2; j = sg // 2
                acol = h * 128 + seglen * sg
                bsrc = ot[sg // 8]; bname = f"ot{sg // 8}"
                if kind == 0:
                    ACT(lambda e, h=h, hf=hf, j=j: e.copy(out=V(sst, j * 64, [[1, 64]], p0=hf * 64, npart=64), in_=V(Sf, h * 64, [[1, 64]], p0=hf * 64, npart=64)), ["Sf"], ["sst"])
                    DVE(lambda e, h=h, sg=sg, acol=acol, bsrc=bsrc: e.scalar_tensor_tensor(out=Sf[:, h * 64:(h + 1) * 64], in0=Sf[:, h * 64:(h + 1) * 64], scalar=a2T[:, acol:acol + 1], in1=bsrc[:, (sg % 8) * 64:(sg % 8 + 1) * 64], op0=ALU.mult, op1=ALU.add),
                        ["Sf", "a2T", bname], ["Sf"])
                else:
                    DVE(lambda e, h=h, sg=sg, hf=hf, j=j, acol=acol, bsrc=bsrc: e.scalar_tensor_tensor(
                        out=V(sstf, j * 64, [[1, 64]], p0=hf * 64, npart=64), in0=V(sstf, j * 64, [[1, 64]], p0=hf * 64, npart=64),
                        scalar=V(a2T, acol, [[1, 1]], p0=hf * 64, npart=64), in1=V(bsrc, (sg % 8) * 64, [[1, 64]], p0=hf * 64, npart=64), op0=ALU.mult, op1=ALU.add),
                        ["sstf", "a2T", bname], ["sstf"])
            PE(lambda e, h=h: e.matmul(out=zA[:, h * 64:(h + 1) * 64], lhsT=attm[:, h * 128:(h + 1) * 128], rhs=vdb[:, h * 64:(h + 1) * 64], start=True, stop=False), ["attm", "vdb"], ["zA"])
            DVE(lambda e, h=h: e.tensor_tensor(out=V(qexp, 0, [[128, nj], [1, 128]]), in0=V(q2T, h * 128, [[0, nj], [1, 128]]), in1=V(cmb, kk * 1024, [[128, nj], [1, 128]]), op=ALU.mult), ["q2T", "cmb"], ["qexp"])
            for j in range(nj):
                PE(lambda e, h=h, j=j: e.matmul(out=zA[:, h * 64:(h + 1) * 64], lhsT=qexp[:, j * 128:(j + 1) * 128], rhs=sst[:, j * 64:(j + 1) * 64], start=False, stop=(j == nj - 1)), ["qexp", "sst"], ["zA"])
            if kind == 1:
                for du in range(2):
                    DMA(lambda e, h=h, du=du: e.dma_start(out=DV(k.nh_s, (l * 16 + du) * 16384 + h * 4096, [[64, 64], [2 * 16384, 8], [1, 64]]),
                                                          in_=V(sstf, 0, [[64, 8], [1, 64]], p0=du * 64, npart=64)), ["sstf"], [])
        if kind == 0 and i == NT - 1:
            DMA(lambda e: e.dma_start(out=DV(k.nh_p, l * 16384, [[64, 64], [4096, 4], [1, 64]]), in_=V(Sf, 0, [[64, 4], [1, 64]], npart=64)), ["Sf"], [])
        DVE(lambda e: e.tensor_copy(out=od[:, :], in_=zA[:, 0:256]), ["zA"], ["od"])
        rms_heads(od, "od", 8, 768, 768)

        ck(f'hgrn{l}{kind}{i}')
        pending = []

        def kblock(ktile, kname, vtile, vname, ncol, qcol, first, last, mask, mname, s_=None):
            for ch in range(2):
                if s_ is None:
                    rhs = qblk[:, ch * 512:(ch + 1) * 512]
                else:
                    rhs = V(qblk, ch * 512 + 8 * s_, [[128, 4], [1, 8]])
                PE(lambda e, ch=ch, rhs=rhs: e.matmul(out=sc[ch][:, 0:4 * ncol], lhsT=ktile[:, ch * 128:(ch + 1) * 128], rhs=rhs, start=True, stop=True), [kname, "qblk"], [f"sc{ch}"])
                ptb, pname = (pt_, "pt_") if ch == 0 else (qexp, "qexp")
                ACT(lambda e, ch=ch, ptb=ptb: e.activation(out=ptb[:, 0:4 * ncol], in_=sc[ch][:, 0:4 * ncol], func=AF.Exp), [f"sc{ch}"], [pname])
                if mask is not None:
                    DVE(lambda e, ptb=ptb: e.tensor_tensor(out=V(ptb, 0, [[128, 4], [1, 128]]), in0=V(ptb, 0, [[128, 4], [1, 128]]), in1=V(mask, 0, [[0, 4], [1, 128]]), op=ALU.mult), [pname, mname], [pname])

                def pv(ch=ch, ptb=ptb, pname=pname):
                    for m in range(4):
                        h = 2 * ch + m // 2
                        PE(lambda e, m=m, h=h: e.matmul(out=V(ot[ch], m * 128 + qcol, [[1, ncol]], npart=65), lhsT=vtile[:, h * 65:(h + 1) * 65], rhs=ptb[:, m * ncol:(m + 1) * ncol],
                                                        start=(first and m == 0), stop=last, skip_group_check=True), [vname, pname], [f"ot{ch}"])
                if pending:
                    pending.pop(0)()
                pending.append(pv)

        if kind == 0:
            kvsets = [(ktB[j][:, :], f"ktB{j}", vB[j][:, :], f"vB{j}") for j in range(4)]
            for kbi in range(i):
                kt_ap, ktn, v_ap, vnm = kvsets[kbi % 4]
                DMA(lambda e, kbi=kbi, kt_ap=kt_ap: e.dma_start(out=kt_ap.rearrange("p (c n) -> p c n", c=2), in_=k.ktD.ap().rearrange("p (c n) -> p c n", c=2)[:, :, kbi * 128:(kbi + 1) * 128]), ["ktD"], [ktn])
                DMA(lambda e, kbi=kbi, v_ap=v_ap: e.dma_start(out=v_ap, in_=k.vD[kbi * 128:(kbi + 1) * 128, :]), ["vD"], [vnm], q="pool")
                kblock(kt_ap, ktn, v_ap, vnm, 128, 0, kbi == 0, False, None, None)
            kblock(kTt, "kTt", vat, "vat", 128, 0, i == 0, True, trilb, "trilb")
        else:
            kblock(kTt, "kTt", vat, "vat", 128, 0, True, False, msb, "msb")
            for s_ in range(16):
                for j in range(16):
                    col = s_ * 16 + j
                    while pending:
                        pending.pop(0)()
                    DMA(lambda e, col=col: e.indirect_dma_start(out=kpg[:, :], out_offset=None, in_=k.cache_kl[l][:, :], in_offset=bass.IndirectOffsetOnAxis(ap=idx[:, col:col + 1], axis=0)), ["idx"], ["kpg"], q="pool")
                    DMA(lambda e, col=col: e.indirect_dma_start(out=vpg[:, :], out_offset=None, in_=k.cache_vl[l][:, :], in_offset=bass.IndirectOffsetOnAxis(ap=idx[:, col:col + 1], axis=0)), ["idx"], ["vpg"], q="pool")
                    DVE(lambda e: e.tensor_copy(out=kpb[:, :], in_=kpg[:, :]), ["kpg"], ["kpb"])
                    DVE(lambda e: e.tensor_copy(out=V(vpa, 0, [[65, 4], [1, 64]]), in_=V(vpg, 0, [[64, 4], [1, 64]])), ["vpg"], ["vpa"])
                    for ch in range(2):
                        PE(lambda e, ch=ch: e.transpose(out=tpb[:, ch * 128:(ch + 1) * 128], in_=kpb[:, ch * 128:(ch + 1) * 128], identity=identb[:, :]), ["kpb", "identb"], ["tpb"])
                    DVE(lambda e: e.tensor_copy(out=kpT[:, :], in_=tpb[:, 0:256]), ["tpb"], ["kpT"])
                    kblock(kpT, "kpT", vpa, "vpa", 8, 8 * s_, False, j == 15, None, None, s_=s_)
        while pending:
            pending.pop(0)()
        ck(f'attn{l}{kind}{i}')
        for ch in range(2):
            DVE(lambda e, ch=ch: e.tensor_copy(out=t1[0:65, ch * 512:(ch + 1) * 512], in_=ot[ch][0:65, :]), [f"ot{ch}"], ["t1"])
            for m in range(4):
                PE(lambda e, ch=ch, m=m: e.transpose(out=sc[ch][:, m * 65:(m + 1) * 65], in_=t1[0:65, ch * 512 + m * 128:ch * 512 + (m + 1) * 128], identity=identf[0:65, 0:65]), ["t1", "identf"], [f"sc{ch}"])
            DVE(lambda e, ch=ch: e.tensor_copy(out=oa[:, ch * 260:(ch + 1) * 260], in_=sc[ch][:, 0:260]), [f"sc{ch}"], ["oa"])
        DVE(lambda e: e.reciprocal(out=rl[:, 0:8], in_=V(oa, 64, [[65, 8]])), ["oa"], ["rl"])
        for h in range(4):
            b0 = h * 130
            DVE(lambda e, h=h, b0=b0: e.tensor_scalar(out=sq[:, 0:64], in0=oa[:, b0 + 65:b0 + 129], scalar1=rl[:, 2 * h + 1:2 * h + 2], scalar2=lamt[:, 0:1], op0=ALU.mult, op1=ALU.mult), ["oa", "rl", "lamt"], ["sq"])
            DVE(lambda e, h=h, b0=b0: e.scalar_tensor_tensor(out=oat[:, h * 64:(h + 1) * 64], in0=oa[:, b0:b0 + 64], scalar=rl[:, 2 * h:2 * h + 1], in1=sq[:, 0:64], op0=ALU.mult, op1=ALU.add), ["oa", "rl", "sq"], ["oat"])
        rms_heads(oat, "oat", 7, 0, 0)
        ck(f'epi{l}{kind}{i}')
        if kind == 0 and DBG_CAT:
            DMA(lambda e: e.dma_start(out=k.dbg_cat[l, i * 128:(i + 1) * 128, :], in_=cat[:, :]), ["cat"], [])
        for kc in range(8):
            PE(lambda e, kc=kc: e.transpose(out=tpb[:, kc * 128:(kc + 1) * 128], in_=cat[:, kc * 128:(kc + 1) * 128], identity=identb[:, :]), ["cat", "identb"], ["tpb"])
        DVE(lambda e: e.tensor_copy(out=catT[:, :], in_=tpb[:, :]), ["tpb"], ["catT"])
        for n in range(2):
            for kc in range(8):
                PE(lambda e, kc=kc, n=n: e.matmul(out=zz[n][:, :], lhsT=catT[:, kc * 128:(kc + 1) * 128], rhs=wout[:, kc * D + n * 512:kc * D + (n + 1) * 512], start=(kc == 0), stop=(kc == 7)), ["catT", "wout"], [zn[n]])
            DVE(lambda e, n=n: e.tensor_tensor(out=t1[:, n * 512:(n + 1) * 512], in0=zz[n][:, :], in1=mod[:, 2 * D + n * 512:2 * D + (n + 1) * 512], op=ALU.mult), [zn[n], "mod"], ["t1"])
        DVE(lambda e: e.scalar_tensor_tensor(out=t1[:, :], in0=xt[:, :], scalar=ALPHA, in1=t1[:, :], op0=ALU.mult, op1=ALU.add), ["xt", "t1"], ["t1"])
        layer_norm(t1, D, lnp[:, 0:D], lnp[:, D:2 * D], "t1", ["lnp"], xt, "xt")
        if kind == 0:
            dst = (k.y1 if l == 0 else k.y_p)[i * 128:(i + 1) * 128, :]
            DMA(lambda e: e.dma_start(out=dst, in_=t1[:, :]), ["t1"], ["y1"] if l == 0 else [])
        else:
            dst = (k.y1s if l == 0 else k.y_s)[:, :]
            DMA(lambda e: e.dma_start(out=dst, in_=t1[:, :]), ["t1"], ["y1s"] if l == 0 else [])

    compute_mod(k.cp)
    ck(f'mod{l}')
    for i in range(k.NTILES):
        tile(0, i)
    compute_mod(k.cs)
    tile(1, 0)


def _consts():
    c = {}
    a = np.arange(128)
    s_, t_ = a[:, None], a[None, :]
    c["c_ident"] = np.eye(128, dtype=np.float32)
    c["c_tril"] = (s_ <= t_).astype(np.float32)
    c["c_ms"] = ((s_ // 8 == t_ // 8) & (s_ <= t_)).astype(np.float32)
    mc, ml, cm, sg = [], [], [], []
    for L in (16, 8):
        same = (s_ // L == t_ // L)
        mc.append((same & (s_ <= t_)).astype(np.float32))
        ml.append(same.astype(np.float32))
        nseg = 128 // L
        m = np.zeros((128, 8, 128), np.float32)
        for pp in range(128):
            for j in range(nseg // 2):
                seg = 2 * j + pp // 64
                m[pp, j, seg * L:(seg + 1) * L] = 1.0
        cm.append(m.reshape(128, 1024))
        g = np.zeros((128, 16, 64), np.float32)
        for seg in range(nseg):
            g[seg * L:(seg + 1) * L, seg, :] = 1.0
        sg.append(g.reshape(128, 1024))
    c["c_mcum"] = np.stack(mc); c["c_mlast"] = np.stack(ml); c["c_matt"] = np.stack(mc)
    c["c_cm"] = np.stack(cm); c["c_seg"] = np.stack(sg)
    rm = np.zeros((128, 4), np.float32)
    for m in range(4):
        rm[32 * m:32 * m + 32, m] = SCALE
    c["c_rowm"] = rm
    c["pos_p"] = (128.0 * np.arange(NT)[None, :] + a[:, None]).astype(np.float32)
    c["pos_s"] = (2048.0 + (a % 8)).astype(np.float32).reshape(128, 1)
    inv = (np.float32(10000.0) ** (-np.arange(16, dtype=np.float32) * np.float32(2.0) / np.float32(32.0))).astype(np.float32)
    c["invf"] = np.tile(inv[None, :], (128, 1)).astype(np.float32)
    c["iota_p"] = a.astype(np.float32).reshape(128, 1)
    return c


_PROG = None


def kernel(x_prompt, x_sample, cache_k, cache_v, state_conv, state_hgrn, page_table, c_prompt, c_sample,
           w_ada, b_ada, w_in, lam_qk, attn_norm_g, sg_norm_g, sg_norm_b, w_s, b_s, conv_w, conv_b,
           conv_norm_g, conv_norm_b, w_pw, lower_bounds, hgrn_norm_g, w_out, ln_g, ln_b):
    global _PROG
    if _PROG is None:
        _PROG = build_program()
    k = _PROG
    f = lambda a: np.ascontiguousarray(np.asarray(a, dtype=np.float32))
    consts = _consts()
    ck = f(cache_k).reshape(2, NPOOL_ROWS, 256)[:, :POOL_ROWS_RUN]
    cv = f(cache_v).reshape(2, NPOOL_ROWS, 256)[:, :POOL_ROWS_RUN]
    shared = dict(cache_k0=ck[0], cache_k1=ck[1], cache_v0=cv[0], cache_v1=cv[1], w_ada=f(w_ada), b_ada=f(b_ada), w_in=f(w_in), lam_qk=f(lam_qk).reshape(2, 128),
                  attn_g=f(attn_norm_g), sg_g=f(sg_norm_g), sg_b=f(sg_norm_b), w_s=f(w_s), b_s=f(b_s), conv_w=f(conv_w),
                  conv_b=f(conv_b), cn_g=f(conv_norm_g), cn_b=f(conv_norm_b), w_pw=f(w_pw), lowb=f(lower_bounds),
                  hg_g=f(hgrn_norm_g), w_out=f(w_out), ln_g=f(ln_g), ln_b=f(ln_b))
    shared.update(consts)
    xp_, xs_ = f(x_prompt), f(x_sample)
    cp_, cs_ = f(c_prompt), f(c_sample)
    pt = np.asarray(page_table).astype(np.int32)
    sc_, sh_ = f(state_conv), f(state_hgrn)
    in_maps = []
    for r in range(NCORES_RUN):
        b = r % 2
        m = dict(shared)
        m["xp"] = xp_[b]
        m["xs"] = np.ascontiguousarray(xs_[16 * r:16 * r + 16].reshape(128, D))
        m["cp"] = np.ascontiguousarray(np.tile(cp_[b][None, :], (128, 1)))
        m["cs"] = np.ascontiguousarray(np.repeat(cs_[16 * r:16 * r + 16], 8, axis=0))
        m["ptab"] = np.ascontiguousarray(np.tile(pt[16 * r:16 * r + 16].reshape(1, 256), (128, 1)))
        m["st_conv"] = np.ascontiguousarray(sc_[:, 16 * r:16 * r + 16])
        m["st_hgrn"] = np.ascontiguousarray(sh_[:, 16 * r:16 * r + 16])
        in_maps.append(m)
    res = run_bass_kernel_spmd(k.nc, in_maps, core_ids=list(range(NCORES_RUN))).results
    res = list(res) + [res[0]] * (8 - len(res))
    global DBG_RES
    DBG_RES = res
    y_prompt = np.stack([res[b]["y_p"] for b in range(2)])
    y_sample = np.concatenate([res[r]["y_s"].reshape(16, 8, D) for r in range(8)], 0)
    nk_p = np.stack([res[b]["nk_p"] for b in range(2)], 1).reshape(2, 2, 8192, 4, 64)
    nv_p = np.stack([res[b]["nv_p"] for b in range(2)], 1).reshape(2, 2, 8192, 4, 64)
    nk_s = np.concatenate([res[r]["nk_s"].reshape(2, 16, 8, 4, 64) for r in range(8)], 1)
    nv_s = np.concatenate([res[r]["nv_s"].reshape(2, 16, 8, 4, 64) for r in range(8)], 1)
    nch = np.concatenate([res[r]["nch_s"].reshape(2, 16, 8, 256) for r in range(8)], 1)
    ncv_p = np.stack([res[b]["ncv_p"] for b in range(2)], 1)
    ncv_s = np.concatenate([res[r]["ncv_s"] for r in range(8)], 1)
    nh_p = np.stack([res[b]["nh_p"] for b in range(2)], 1)
    nh_s = np.concatenate([res[r]["nh_s"] for r in range(8)], 1)
    outs = (y_prompt, y_sample, nk_p, nv_p, nk_s, nv_s, nch, ncv_p, ncv_s, nh_p, nh_s)
    return tuple(np.ascontiguousarray(o, dtype=np.float32) for o in outs)
```

```python
import numpy as np
import concourse.bass as bass
import concourse.mybir as mybir
from concourse.bass_utils import run_bass_kernel_spmd

F32 = mybir.dt.float32
BF16 = mybir.dt.bfloat16
I32 = mybir.dt.int32
AF = mybir.ActivationFunctionType
ALU = mybir.AluOpType
AX = mybir.AxisListType


PSUM_NAMES = {"zA", "zB", "tpb", "sc0", "sc1", "ot0", "ot1", "misc"}


class Res:
    __slots__ = ("name", "w", "r")

    def __init__(self, name):
        self.name = name
        self.w = None
        self.r = []


class Op:
    __slots__ = ("eng", "fn", "deps", "idx", "dma", "lane", "lane_cnt", "marked", "val", "waits", "lane_wait")

    def __init__(self, eng, fn, dma):
        self.eng = eng
        self.fn = fn
        self.dma = dma
        self.deps = []
        self.marked = False
        self.val = 0
        self.waits = []
        self.lane = None
        self.lane_cnt = 0
        self.lane_wait = 0


class Prog:
    ENGS = ("pe", "act", "dve", "pool", "sp")
    NLANES = {"sp": 8, "pool": 6, "act": 2}

    def __init__(self, nc):
        self.nc = nc
        self.ops = {e: [] for e in self.ENGS}
        self.lane_rr = {e: 0 for e in self.NLANES}
        self.lane_count = {}
        self.nres = 0

    def res(self, name=None):
        self.nres += 1
        return Res(name or f"r{self.nres}")

    def add(self, eng, fn, reads=(), writes=(), dma=False):
        op = Op(eng, fn, dma)
        deps = []
        xr = [r for r in reads if r.name in PSUM_NAMES]
        if xr:
            reads = [r for r in reads if r.name not in PSUM_NAMES]
            writes = list(writes) + [r for r in xr if r not in writes]
        for r in reads:
            if r.w is not None:
                deps.append(r.w)
        for w in writes:
            if w.w is not None:
                deps.append(w.w)
            deps.extend(w.r)
        for r in reads:
            r.r.append(op)
        for w in writes:
            w.w = op
            w.r = []
        seen = set()
        for d in deps:
            if d is op or id(d) in seen:
                continue
            seen.add(id(d))
            if (not d.dma) and (not dma) and d.eng == eng and eng == "pe":
                continue
            op.deps.append(d)
        op.idx = len(self.ops[eng])
        if dma:
            n = self.NLANES[eng]
            k = self.lane_rr[eng]
            self.lane_rr[eng] = (k + 1) % n
            op.lane = (eng, k)
            c = self.lane_count.get(op.lane, 0)
            op.lane_wait = c
            op.lane_cnt = c + 1
            self.lane_count[op.lane] = c + 1
        self.ops[eng].append(op)
        return op

    def pe(self, fn, reads=(), writes=()):
        return self.add("pe", fn, reads, writes)

    def act(self, fn, reads=(), writes=()):
        return self.add("act", fn, reads, writes)

    def dve(self, fn, reads=(), writes=()):
        return self.add("dve", fn, reads, writes)

    def pool(self, fn, reads=(), writes=()):
        return self.add("pool", fn, reads, writes)

    def dma(self, fn, reads=(), writes=(), q="sp"):
        return self.add(q, fn, reads, writes, dma=True)

    def finalize_and_emit(self):
        nc = self.nc
        for e in self.ENGS:
            waited_idx = {}
            waited_lane = {}
            for op in self.ops[e]:
                need_idx = {}
                need_lane = {}
                for d in op.deps:
                    if d.dma:
                        if need_lane.get(d.lane, 0) < d.lane_cnt:
                            need_lane[d.lane] = d.lane_cnt
                    else:
                        if need_idx.get(d.eng, -1) < d.idx:
                            need_idx[d.eng] = d.idx
                if op.dma and op.lane_wait > 0:
                    if need_lane.get(op.lane, 0) < op.lane_wait:
                        need_lane[op.lane] = op.lane_wait
                op.waits = []
                for te, ix in need_idx.items():
                    if waited_idx.get(te, -1) >= ix:
                        continue
                    if te == e and not op.dma and ix < op.idx and False:
                        continue
                    waited_idx[te] = ix
                    tgt = self.ops[te][ix]
                    tgt.marked = True
                    op.waits.append(("op", tgt))
                for ln, cnt in need_lane.items():
                    if waited_lane.get(ln, 0) >= cnt:
                        continue
                    waited_lane[ln] = cnt
                    op.waits.append(("lane", ln, cnt))
        for e in self.ENGS:
            v = 0
            for op in self.ops[e]:
                if op.dma:
                    continue
                if op.marked:
                    v += 1
                    op.val = v
        self.stats = {e: len(self.ops[e]) for e in self.ENGS}
        sems = {}
        import contextlib
        with contextlib.ExitStack() as st:
            for e in self.ENGS:
                sems[e] = st.enter_context(nc.semaphore(f"sem_{e}"))
            lanes = {}
            for q, n in self.NLANES.items():
                for k in range(n):
                    lanes[(q, k)] = st.enter_context(nc.semaphore(f"lane_{q}{k}"))
            block = st.enter_context(nc.Block())

            def emit(e, eng):
                for op in self.ops[e]:
                    for w in op.waits:
                        if w[0] == "op":
                            eng.wait_ge(sems[w[1].eng], w[1].val)
                        else:
                            eng.wait_ge(lanes[w[1]], 16 * w[2])
                    ins = op.fn(eng)
                    if op.dma:
                        ins.then_inc(lanes[op.lane], 16)
                    elif op.marked:
                        ins.then_inc(sems[e], 1)
                if e in self.NLANES:
                    for k in range(self.NLANES[e]):
                        c = self.lane_count.get((e, k), 0)
                        if c:
                            eng.wait_ge(lanes[(e, k)], 16 * c)

            @block.tensor
            def _(eng):
                emit("pe", eng)

            @block.scalar
            def _(eng):
                emit("act", eng)

            @block.vector
            def _(eng):
                emit("dve", eng)

            @block.gpsimd
            def _(eng):
                emit("pool", eng)

            @block.sync
            def _(eng):
                emit("sp", eng)


D = 1024
DIN = 3584
NT = 64
NT_RUN = 64
NPOOL_ROWS = 2560 * 128
POOL_ROWS_RUN = NPOOL_ROWS
NCORES_RUN = 8
ALPHA = (2.0 * 2) ** 0.25
EPS = 1e-5
SCALE = 32 ** -0.5
import math


def V(t, off, dims, p0=0, npart=128):
    ps = t[:].ap[0][0]
    return bass.AP(t, p0 * ps + off, [[ps, npart]] + [list(d) for d in dims])


def DV(t, off, dims):
    return bass.AP(t, off, [list(d) for d in dims])


class K:
    pass


class _Stop(Exception):
    pass


STOP_AT = None
DBG_MISC = False
ACT_PSUM = True
DBG_CAT = False
DBG_RES = None
DBG_SER = False
DBG_SER_LIST = ["kTt", "gate"]


def ck(tag):
    if STOP_AT is not None and tag == STOP_AT:
        raise _Stop()


def build_program():
    nc = bass.Bass("TRN2", target_bir_lowering=False)
    p = Prog(nc)
    k = K()
    R = {}

    def res(n):
        if n not in R:
            R[n] = p.res(n)
        return R[n]

    def din(name, shape, dt=F32):
        return nc.dram_tensor(name, list(shape), dt, kind="ExternalInput")

    def dout(name, shape, dt=F32):
        return nc.dram_tensor(name, list(shape), dt, kind="ExternalOutput")

    xp = din("xp", [8192, D]); xs = din("xs", [128, D])
    cp = din("cp", [128, D]); cs = din("cs", [128, D])
    pos_p = din("pos_p", [128, NT]); pos_s = din("pos_s", [128, 1]); invf = din("invf", [128, 16])
    ptab = din("ptab", [128, 256], I32); iota_p = din("iota_p", [128, 1])
    cache_kl = [din("cache_k0", [POOL_ROWS_RUN, 256]), din("cache_k1", [POOL_ROWS_RUN, 256])]; cache_vl = [din("cache_v0", [POOL_ROWS_RUN, 256]), din("cache_v1", [POOL_ROWS_RUN, 256])]
    st_conv = din("st_conv", [2, 16, 30, 256]); st_hgrn = din("st_hgrn", [2, 16, 4, 64, 64])
    w_ada = din("w_ada", [2, D, 3 * D]); b_ada = din("b_ada", [2, 3 * D]); w_in = din("w_in", [2, D, DIN])
    lam_qk = din("lam_qk", [2, 128]); attn_g = din("attn_g", [2, 64])
    sg_g = din("sg_g", [2, 256]); sg_b = din("sg_b", [2, 256]); w_s = din("w_s", [2, 4, 128, 128]); b_s = din("b_s", [2, 4, 128])
    conv_w = din("conv_w", [2, 31, 256]); conv_b = din("conv_b", [2, 256]); cn_g = din("cn_g", [2, 256]); cn_b = din("cn_b", [2, 256])
    w_pw = din("w_pw", [2, 256, 256]); lowb = din("lowb", [2, 256]); hg_g = din("hg_g", [2, 64])
    w_out = din("w_out", [2, D, D]); ln_g = din("ln_g", [2, D]); ln_b = din("ln_b", [2, D])
    c_ident = din("c_ident", [128, 128]); c_tril = din("c_tril", [128, 128]); c_ms = din("c_ms", [128, 128])
    c_mcum = din("c_mcum", [2, 128, 128]); c_mlast = din("c_mlast", [2, 128, 128]); c_matt = din("c_matt", [2, 128, 128])
    c_cm = din("c_cm", [2, 128, 8 * 128]); c_seg = din("c_seg", [2, 128, 16 * 64]); c_rowm = din("c_rowm", [128, 4])

    y_p = dout("y_p", [8192, D]); y_s = dout("y_s", [128, D])
    nk_p = dout("nk_p", [2, 8192, 256]); nv_p = dout("nv_p", [2, 8192, 256])
    nk_s = dout("nk_s", [2, 128, 256]); nv_s = dout("nv_s", [2, 128, 256]); nch_s = dout("nch_s", [2, 128, 256])
    ncv_p = dout("ncv_p", [2, 30, 256]); ncv_s = dout("ncv_s", [2, 16, 30, 256])
    nh_p = dout("nh_p", [2, 4, 64, 64]); nh_s = dout("nh_s", [2, 16, 4, 64, 64])
    dbg_cat = dout("dbg_cat", [2, 8192, D], BF16) if DBG_CAT else None
    y1s = dout("y1s_scratch", [128, D])
    NTILES = NT_RUN
    y1 = dout("y1_scratch", [8192, D])
    ktD = dout("ktD", [128, 2 * 8192], BF16)
    vD = dout("vD", [8192, 260], BF16)

    def sb(name, free, dt=F32):
        return nc.alloc_sbuf_tensor(name, [128, free], dt)

    def ps(name, free, dt=F32):
        return nc.alloc_psum_tensor(name, [128, free], dt)

    zA = ps("zA", 512); zB = ps("zB", 512); tpb = ps("tpb", 1024, BF16)
    sc = [ps("sc0", 512), ps("sc1", 512)]
    ot = [ps("ot0", 512), ps("ot1", 512)]
    misc = ps("misc", 512)

    win = sb("win", 8 * DIN, BF16)
    wout = sb("wout", 8 * D, BF16)
    wpw = sb("wpw", 2 * 256, BF16)
    stg = [sb("stg0", 512)]
    ktB = [sb(f"ktB{j}", 256, BF16) for j in range(3)]
    vB = [sb(f"vB{j}", 260, BF16) for j in range(3)]
    identf = sb("identf", 128); identb = sb("identb", 128, BF16)
    trilb = sb("trilb", 128, BF16); msb = sb("msb", 128, BF16)
    mcum = sb("mcum", 2 * 128); mlast = sb("mlast", 2 * 128); mattb = sb("mattb", 2 * 512, BF16)
    cmb = sb("cmb", 2 * 1024, BF16); segb = sb("segb", 2 * 1024, BF16); rowm = sb("rowm", 4)
    mod = sb("mod", 3 * D)
    cosp = sb("cosp", NT * 16); sinp = sb("sinp", NT * 16); coss = sb("coss", 16); sins = sb("sins", 16)
    prm = sb("prm", 10 * 256)
    lnp = sb("lnp", 2 * D)
    lamt = sb("lamt", 8)
    wmT = sb("wmT", 2 * 4 * 128, BF16)
    bsE = sb("bsE", 2 * 256)
    cwT = sb("cwT", 2 * 31)
    idx = sb("idx", 256, I32); idxf = sb("idxf", 256)
    xt = sb("xt", D); hb = sb("hb", D, BF16); hT = sb("hT", D, BF16); t1 = sb("t1", D)
    qk = sb("qk", 512); qkr = sb("qkr", 512); qb = sb("qb", 256, BF16); kb = sb("kb", 256, BF16)
    qblk = sb("qblk", 2 * 4 * 128, BF16)
    gate = sb("gate", 4 * 256)
    vnb = sb("vnb", 256, BF16)
    hcT = sb("hcT", 2 * 640, BF16)
    cyb = sb("cyb", 256, BF16); cyT = sb("cyT", 256, BF16)
    dq = sb("dq", 256); df = sb("df", 256); lf = sb("lf", 256); kd = sb("kd", 256); vdb = sb("vdb", 256, BF16)
    cl = sb("cl", 512); e1 = sb("e1", 256); e2 = sb("e2", 256); e3 = sb("e3", 256)
    qdup = sb("qdup", 512, BF16); kt_ = sb("kt_", 256, BF16); kpdup = sb("kpdup", 512, BF16); eldup = sb("eldup", 512)
    q2T = sb("q2T", 512, BF16); khT = sb("khT", 512, BF16); a2T = sb("a2T", 512)
    qexp = sb("qexp", 1024, BF16)
    vexp = sb("vexp", 1024, BF16)
    attm = sb("attm", 512, BF16)
    sst = sb("sst", 8 * 64, BF16)
    sstf = sb("sstf", 8 * 64)
    Sf2 = sb("Sf2", 4 * 64)
    Sf = sb("Sf", 4 * 64)
    od = sb("od", 256); sq = sb("sq", 256); ss4 = sb("ss4", 8)
    cat = sb("cat", D, BF16); catT = sb("catT", D, BF16)
    pt_ = sb("pt_", 512, BF16)
    oa = sb("oa", 2 * 260); rl = sb("rl", 8); oat = sb("oat", 256)
    st2 = sb("st2", 8)
    kpg = sb("kpg", 256); vpg = sb("vpg", 256); kpb = sb("kpb", 256, BF16); kpT = sb("kpT", 256, BF16); vpa = sb("vpa", 260, BF16)


    def rs(names):
        return [res(n) for n in names]

    def ACT(fn, r, w): p.act(fn, rs(r), rs(w))
    def DVE(fn, r, w): p.dve(fn, rs(r), rs(w))
    def POOL(fn, r, w): p.pool(fn, rs(r), rs(w))
    def PE(fn, r, w): p.pe(fn, rs(r), rs(w))
    def DMA(fn, r, w, q="sp"): p.dma(fn, rs(r), rs(w), q=q)

    def bcrow(dt_, off, n):
        return DV(dt_, off, [[0, 128], [1, n]])

    def load_const(dst, nfree, src_ap, name, cast_to=None, tmp=None):
        if cast_to is None:
            DMA(lambda e: e.dma_start(out=dst[:, 0:nfree], in_=src_ap), [], [name])
        else:
            DMA(lambda e: e.dma_start(out=tmp[:, 0:nfree], in_=src_ap), [], ["t1"])
            DVE(lambda e: e.tensor_copy(out=dst[:, 0:nfree], in_=tmp[:, 0:nfree]), ["t1"], [name])

    stgb = [sb("stgb0", 512, BF16), sb("stgb1", 512, BF16)]
    kTt = sb("kTt", 256, BF16); vat = sb("vat", 260, BF16)
    TWO_PI = 2.0 * math.pi
    load_const(identf, 128, c_ident[:, :], "identf")
    DVE(lambda e: e.tensor_copy(out=identb[:], in_=identf[:]), ["identf"], ["identb"])
    load_const(trilb, 128, c_tril[:, :], "trilb", BF16, t1)
    load_const(msb, 128, c_ms[:, :], "msb", BF16, t1)
    load_const(mcum, 256, c_mcum.ap().rearrange("k p t -> p k t"), "mcum")
    load_const(mlast, 256, c_mlast.ap().rearrange("k p t -> p k t"), "mlast")
    for kk in range(2):
        DMA(lambda e, kk=kk: e.dma_start(out=t1[:, 0:128], in_=c_matt[kk, :, :]), [], ["t1"])
        DVE(lambda e, kk=kk: e.tensor_copy(out=V(mattb, kk * 512, [[128, 4], [1, 128]]), in_=V(t1, 0, [[0, 4], [1, 128]])), ["t1"], ["mattb"])
        DMA(lambda e, kk=kk: e.dma_start(out=t1[:, 0:1024], in_=c_cm[kk, :, :]), [], ["t1"])
        DVE(lambda e, kk=kk: e.tensor_copy(out=cmb[:, kk * 1024:(kk + 1) * 1024], in_=t1[:, 0:1024]), ["t1"], ["cmb"])
        DMA(lambda e, kk=kk: e.dma_start(out=t1[:, 0:1024], in_=c_seg[kk, :, :]), [], ["t1"])
        DVE(lambda e, kk=kk: e.tensor_copy(out=segb[:, kk * 1024:(kk + 1) * 1024], in_=t1[:, 0:1024]), ["t1"], ["segb"])
    load_const(rowm, 4, c_rowm[:, :], "rowm")
    DMA(lambda e: e.dma_start(out=t1[:, 0:NT], in_=pos_p[:, :]), [], ["t1"])
    DMA(lambda e: e.dma_start(out=t1[:, 64:65], in_=pos_s[:, :]), [], ["t1"])
    DMA(lambda e: e.dma_start(out=t1[:, 128:144], in_=invf[:, :]), [], ["t1"])
    DVE(lambda e: e.tensor_tensor(out=V(xt, 0, [[16, NT], [1, 16]]), in0=V(t1, 0, [[1, NT], [0, 16]]), in1=V(t1, 128, [[0, NT], [1, 16]]), op=ALU.mult), ["t1"], ["xt"])
    DVE(lambda e: e.tensor_tensor(out=V(qk, 0, [[1, 16]]), in0=V(t1, 64, [[0, 16]]), in1=V(t1, 128, [[1, 16]]), op=ALU.mult), ["t1"], ["qk"])

    C1 = 6.28125
    C2 = TWO_PI - C1

    def sincos(dsin, dcos, sname, cname, src_ap, srcname, n):
        r = mod[:, 0:n]; ki = mod[:, 1024:1024 + n].bitcast(I32); kf = mod[:, 2048:2048 + n]; m = mod[:, 1024:1024 + n]
        DVE(lambda e: e.tensor_scalar(out=kf, in0=src_ap, scalar1=1.0 / TWO_PI, scalar2=None, op0=ALU.mult), [srcname], ["mod"])
        DVE(lambda e: e.tensor_copy(out=ki, in_=kf), ["mod"], ["mod"])
        DVE(lambda e: e.tensor_copy(out=kf, in_=ki), ["mod"], ["mod"])
        DVE(lambda e: e.scalar_tensor_tensor(out=r, in0=kf, scalar=-C1, in1=src_ap, op0=ALU.mult, op1=ALU.add), ["mod", srcname], ["mod"])
        DVE(lambda e: e.scalar_tensor_tensor(out=r, in0=kf, scalar=-C2, in1=r, op0=ALU.mult, op1=ALU.add), ["mod"], ["mod"])
        for which in range(2):
            if which == 1:
                DVE(lambda e: e.tensor_scalar_add(out=r, in0=r, scalar1=0.5 * math.pi), ["mod"], ["mod"])
            DVE(lambda e: e.tensor_scalar(out=m, in0=r, scalar1=math.pi, scalar2=None, op0=ALU.is_gt), ["mod"], ["mod"])
            DVE(lambda e: e.scalar_tensor_tensor(out=r, in0=m, scalar=-TWO_PI, in1=r, op0=ALU.mult, op1=ALU.add), ["mod"], ["mod"])
            DVE(lambda e: e.tensor_scalar(out=m, in0=r, scalar1=-math.pi, scalar2=None, op0=ALU.is_lt), ["mod"], ["mod"])
            DVE(lambda e: e.scalar_tensor_tensor(out=r, in0=m, scalar=TWO_PI, in1=r, op0=ALU.mult, op1=ALU.add), ["mod"], ["mod"])
            dst, dn = (dsin, sname) if which == 0 else (dcos, cname)
            ACT(lambda e, dst=dst: e.activation(out=dst[:, 0:n], in_=r, func=AF.Sin), ["mod"], [dn])
    sincos(sinp, cosp, "sinp", "cosp", xt[:, 0:NT * 16], "xt", NT * 16)
    sincos(sins, coss, "sins", "coss", qk[:, 0:16], "qk", 16)
    DMA(lambda e: e.dma_start(out=idx[:, :], in_=ptab[:, :]), [], ["idx"])
    DMA(lambda e: e.dma_start(out=st2[:, 0:1], in_=iota_p[:, :]), [], ["st2"])
    DVE(lambda e: e.tensor_copy(out=idxf[:, :], in_=idx[:, :]), ["idx"], ["idxf"])
    DVE(lambda e: e.tensor_scalar(out=idxf[:, :], in0=idxf[:, :], scalar1=128.0, scalar2=st2[:, 0:1], op0=ALU.mult, op1=ALU.add), ["idxf", "st2"], ["idxf"])
    DVE(lambda e: e.tensor_copy(out=idx[:, :], in_=idxf[:, :]), ["idxf"], ["idx"])
    for tv, nm in ((vB[0], "vB0"), (vB[1], "vB1"), (vpa, "vpa"), (vat, "vat")):
        DVE(lambda e, tv=tv: e.memset(tv[:, :], 1.0), [], [nm])

    k.__dict__.update(locals())
    try:
        ck('consts')
        for l in range(2):
            emit_layer(k, l)
    except _Stop:
        pass
    p.finalize_and_emit()
    return k


def emit_layer(k, l):
    g = k.__dict__
    p = k.p; nc = k.nc
    ACT, DVE, POOL, PE, DMA, res = k.ACT, k.DVE, k.POOL, k.PE, k.DMA, k.res
    (zA, zB, tpb, sc, ot, misc, win, wout, wpw, stg, stgb, identf, identb, trilb, msb, mcum, mlast, mattb, cmb, segb, rowm, mod,
     cosp, sinp, coss, sins, prm, lnp, lamt, wmT, bsE, cwT, idx, xt, hb, hT, t1, qk, qkr, qb, kb, qblk, gate, vnb,
     hcT, cyb, cyT, dq, df, lf, kd, vdb, cl, e1, e2, e3, qdup, kt_, kpdup, eldup, q2T, khT, a2T, qexp, vexp, attm,
     sst, sstf, Sf, od, sq, ss4, cat, catT, pt_, oa, rl, oat, st2, kpg, vpg, kpb, kpT, vpa, kTt, vat, ktB, vB) = [g[n] for n in (
        "zA zB tpb sc ot misc win wout wpw stg stgb identf identb trilb msb mcum mlast mattb cmb segb rowm mod "
        "cosp sinp coss sins prm lnp lamt wmT bsE cwT idx xt hb hT t1 qk qkr qb kb qblk gate vnb "
        "hcT cyb cyT dq df lf kd vdb cl e1 e2 e3 qdup kt_ kpdup eldup q2T khT a2T qexp vexp attm "
        "sst sstf Sf od sq ss4 cat catT pt_ oa rl oat st2 kpg vpg kpb kpT vpa kTt vat ktB vB").split()]
    lam_init = 0.8 - 0.6 * math.exp(-0.3 * l)
    zz = [zA, zB]; zn = ["zA", "zB"]

    cnt = [0]

    def load_w(dst, dst_off, dram, r0, c0, ncols, dname):
        i = 0; cnt[0] += 1
        DMA(lambda e: e.dma_start(out=stg[i][:, 0:ncols], in_=dram[l, r0:r0 + 128, c0:c0 + ncols]), [], [f"stg{i}"])
        POOL(lambda e: e.tensor_copy(out=dst[:, dst_off:dst_off + ncols], in_=stg[i][:, 0:ncols]), [f"stg{i}"], [dname])

    for kc in range(8):
        for n in range(7):
            load_w(win, kc * DIN + n * 512, k.w_in, kc * 128, n * 512, 512, "win")
        for n in range(2):
            load_w(wout, kc * D + n * 512, k.w_out, kc * 128, n * 512, 512, "wout")
    for ch in range(2):
        load_w(wpw, ch * 256, k.w_pw, ch * 128, 0, 256, "wpw")

    def bc(dram, n, slot, rep=1):
        if rep == 1:
            src = DV(dram, l * n, [[0, 128], [1, n]])
        else:
            src = DV(dram, l * n, [[0, 128], [0, rep], [1, n]])
        DMA(lambda e: e.dma_start(out=prm[:, slot * 256:(slot + 1) * 256] if rep == 1 else V(prm, slot * 256, [[n, rep], [1, n]]), in_=src), [], ["prm"])
    bc(k.sg_g, 256, 0); bc(k.sg_b, 256, 1); bc(k.conv_b, 256, 2); bc(k.cn_g, 256, 3); bc(k.cn_b, 256, 4)
    bc(k.attn_g, 64, 7, 4); bc(k.hg_g, 64, 8, 4)
    DVE(lambda e: e.tensor_scalar_mul(out=prm[:, 7 * 256:8 * 256], in0=prm[:, 7 * 256:8 * 256], scalar1=1.0 - lam_init), ["prm"], ["prm"])
    if l == 0:
        DVE(lambda e: e.memset(prm[:, 5 * 256:6 * 256], 0.0), [], ["prm"])
    else:
        DMA(lambda e: e.dma_start(out=prm[:, 5 * 256:6 * 256], in_=DV(k.lowb, 256, [[0, 128], [1, 256]])), [], ["prm"])
        DMA(lambda e: e.dma_start(out=prm[:, 9 * 256:10 * 256], in_=DV(k.lowb, 0, [[0, 128], [1, 256]])), [], ["prm"])
        DVE(lambda e: e.tensor_tensor(out=prm[:, 5 * 256:6 * 256], in0=prm[:, 5 * 256:6 * 256], in1=prm[:, 9 * 256:10 * 256], op=ALU.subtract), ["prm"], ["prm"])
        ACT(lambda e: e.activation(out=prm[:, 5 * 256:6 * 256], in_=prm[:, 5 * 256:6 * 256], func=AF.Sigmoid), ["prm"], ["prm"])
    DVE(lambda e: e.tensor_scalar(out=prm[:, 6 * 256:7 * 256], in0=prm[:, 5 * 256:6 * 256], scalar1=-1.0, scalar2=1.0, op0=ALU.mult, op1=ALU.add), ["prm"], ["prm"])
    DMA(lambda e: e.dma_start(out=lnp[:, 0:D], in_=DV(k.ln_g, l * D, [[0, 128], [1, D]])), [], ["lnp"])
    DMA(lambda e: e.dma_start(out=lnp[:, D:2 * D], in_=DV(k.ln_b, l * D, [[0, 128], [1, D]])), [], ["lnp"])
    DMA(lambda e: e.dma_start(out=e1[:, 0:128], in_=DV(k.lam_qk, l * 128, [[0, 128], [1, 128]])), [], ["e1"])
    DVE(lambda e: e.tensor_tensor(out=V(e2, 0, [[32, 2], [1, 32]]), in0=V(e1, 0, [[64, 2], [1, 32]]), in1=V(e1, 32, [[64, 2], [1, 32]]), op=ALU.mult), ["e1"], ["e2"])
    DVE(lambda e: e.reduce_sum(out=lamt[:, 1:3], in_=V(e2, 0, [[32, 2], [1, 32]]), axis=AX.X), ["e2"], ["lamt"])
    ACT(lambda e: e.activation(out=lamt[:, 1:3], in_=lamt[:, 1:3], func=AF.Exp), ["lamt"], ["lamt"])
    DVE(lambda e: e.tensor_tensor(out=lamt[:, 0:1], in0=lamt[:, 2:3], in1=lamt[:, 1:2], op=ALU.subtract), ["lamt"], ["lamt"])
    DVE(lambda e: e.tensor_scalar_add(out=lamt[:, 0:1], in0=lamt[:, 0:1], scalar1=-lam_init), ["lamt"], ["lamt"])
    for kind in range(2):
        for gi in range(4):
            if kind == 0:
                DMA(lambda e, gi=gi: e.dma_start(out=t1[:, 0:128], in_=k.w_s[l, gi, :, :]), [], ["t1"])
                PE(lambda e: e.transpose(out=sc[0][:, 0:128], in_=t1[:, 0:128], identity=identf[:, :]), ["t1", "identf"], ["sc0"])
                DVE(lambda e, gi=gi: e.tensor_tensor(out=wmT[:, gi * 128:(gi + 1) * 128], in0=sc[0][:, 0:128], in1=trilb[:, :], op=ALU.mult), ["sc0", "trilb"], ["wmT"])
            else:
                DVE(lambda e: e.memset(t1[:, 0:128], 0.0), [], ["t1"])
                for s_ in range(16):
                    DMA(lambda e, gi=gi, s_=s_: e.dma_start(out=t1[8 * s_:8 * s_ + 8, 8 * s_:8 * s_ + 8], in_=k.w_s[l, gi, 0:8, 0:8]), [], ["t1"])
                PE(lambda e: e.transpose(out=sc[0][:, 0:128], in_=t1[:, 0:128], identity=identf[:, :]), ["t1", "identf"], ["sc0"])
                DVE(lambda e, gi=gi: e.tensor_tensor(out=wmT[:, 512 + gi * 128:512 + (gi + 1) * 128], in0=sc[0][:, 0:128], in1=msb[:, :], op=ALU.mult), ["sc0", "msb"], ["wmT"])
        if kind == 0:
            DMA(lambda e: e.dma_start(out=st2[:, 4:8], in_=k.b_s[l].rearrange("g t -> t g"), allow_slow_non_contiguous=True), [], ["st2"])
        else:
            for s_ in range(16):
                DMA(lambda e, s_=s_: e.dma_start(out=st2[8 * s_:8 * s_ + 8, 4:8], in_=k.b_s[l, :, 0:8].rearrange("g t -> t g"), allow_slow_non_contiguous=True), [], ["st2"])
        DVE(lambda e, kind=kind: e.tensor_copy(out=V(bsE, kind * 256, [[64, 4], [1, 64]]), in_=V(st2, 4, [[1, 4], [0, 64]])), ["st2"], ["bsE"])
    for ch in range(2):
        DMA(lambda e, ch=ch: e.dma_start(out=cwT[:, ch * 31:(ch + 1) * 31], in_=k.conv_w[l, :, ch * 128:(ch + 1) * 128].rearrange("j c -> c j"), allow_slow_non_contiguous=True), [], ["cwT"])
    DVE(lambda e: e.memset(Sf[:, :], 0.0), [], ["Sf"])

    ck(f'params{l}')
    def layer_norm(x, n, gap, bap, xname, pnames, junk, jname):
        DVE(lambda e: e.memset(st2[:, 0:4], 0.0), [], ["st2"])
        DVE(lambda e: e.reduce_sum(out=st2[:, 0:1], in_=x[:, 0:n], axis=AX.X), [xname], ["st2"])
        ACT(lambda e: e.activation(out=junk[:, 0:n], in_=x[:, 0:n], func=AF.Square, accum_out=st2[:, 1:2]), [xname], [jname, "st2"])
        DVE(lambda e: e.tensor_scalar_mul(out=st2[:, 0:2], in0=st2[:, 0:2], scalar1=1.0 / n), ["st2"], ["st2"])
        DVE(lambda e: e.tensor_tensor(out=st2[:, 2:3], in0=st2[:, 0:1], in1=st2[:, 0:1], op=ALU.mult), ["st2"], ["st2"])
        DVE(lambda e: e.tensor_tensor(out=st2[:, 1:2], in0=st2[:, 1:2], in1=st2[:, 2:3], op=ALU.subtract), ["st2"], ["st2"])
        DVE(lambda e: e.tensor_scalar_add(out=st2[:, 1:2], in0=st2[:, 1:2], scalar1=EPS), ["st2"], ["st2"])
        ACT(lambda e: e.activation(out=st2[:, 1:2], in_=st2[:, 1:2], func=AF.Sqrt), ["st2"], ["st2"])
        DVE(lambda e: e.reciprocal(out=st2[:, 1:2], in_=st2[:, 1:2]), ["st2"], ["st2"])
        DVE(lambda e: e.tensor_scalar(out=x[:, 0:n], in0=x[:, 0:n], scalar1=st2[:, 0:1], scalar2=st2[:, 1:2], op0=ALU.subtract, op1=ALU.mult), [xname, "st2"], [xname])
        DVE(lambda e: e.tensor_tensor(out=x[:, 0:n], in0=x[:, 0:n], in1=gap, op=ALU.mult), [xname] + pnames, [xname])
        DVE(lambda e: e.tensor_tensor(out=x[:, 0:n], in0=x[:, 0:n], in1=bap, op=ALU.add), [xname] + pnames, [xname])

    def rms_heads(x, xname, gslot, gcol, out_c0):
        DVE(lambda e: e.tensor_tensor(out=sq[:, :], in0=x[:, 0:256], in1=x[:, 0:256], op=ALU.mult), [xname], ["sq"])
        DVE(lambda e: e.reduce_sum(out=ss4[:, 0:4], in_=V(sq, 0, [[64, 4], [1, 64]]), axis=AX.X), ["sq"], ["ss4"])
        DVE(lambda e: e.tensor_scalar(out=ss4[:, 0:4], in0=ss4[:, 0:4], scalar1=1.0 / 64, scalar2=EPS, op0=ALU.mult, op1=ALU.add), ["ss4"], ["ss4"])
        ACT(lambda e: e.activation(out=ss4[:, 0:4], in_=ss4[:, 0:4], func=AF.Sqrt), ["ss4"], ["ss4"])
        DVE(lambda e: e.reciprocal(out=ss4[:, 0:4], in_=ss4[:, 0:4]), ["ss4"], ["ss4"])
        DVE(lambda e: e.tensor_tensor(out=V(x, 0, [[64, 4], [1, 64]]), in0=V(x, 0, [[64, 4], [1, 64]]), in1=V(ss4, 0, [[1, 4], [0, 64]]), op=ALU.mult), [xname, "ss4"], [xname])
        DVE(lambda e: e.tensor_tensor(out=x[:, 0:256], in0=x[:, 0:256], in1=prm[:, gslot * 256:(gslot + 1) * 256], op=ALU.mult), [xname, "prm"], [xname])
        DVE(lambda e: e.tensor_tensor(out=cat[:, out_c0:out_c0 + 256], in0=x[:, 0:256], in1=gate[:, gcol:gcol + 256], op=ALU.mult), [xname, "gate"], ["cat"])

    def compute_mod(csrc):
        DMA(lambda e: e.dma_start(out=t1[:, :], in_=csrc[:, :]), [], ["t1"])
        ACT(lambda e: e.activation(out=hb[:, :], in_=t1[:, :], func=AF.Silu), ["t1"], ["hb"])
        for kc in range(8):
            PE(lambda e, kc=kc: e.transpose(out=tpb[:, kc * 128:(kc + 1) * 128], in_=hb[:, kc * 128:(kc + 1) * 128], identity=identb[:, :]), ["hb", "identb"], ["tpb"])
        DVE(lambda e: e.tensor_copy(out=hT[:, :], in_=tpb[:, :]), ["tpb"], ["hT"])
        for n in range(6):
            for kc in range(8):
                i = 0; cnt[0] += 1
                DMA(lambda e, i=i, kc=kc, n=n: e.dma_start(out=stg[i][:, :], in_=k.w_ada[l, kc * 128:(kc + 1) * 128, n * 512:(n + 1) * 512]), [], [f"stg{i}"])
                POOL(lambda e, i=i: e.tensor_copy(out=stgb[i][:, :], in_=stg[i][:, :]), [f"stg{i}"], [f"stgb{i}"])
                PE(lambda e, i=i, kc=kc, n=n: e.matmul(out=zz[n % 2][:, :], lhsT=hT[:, kc * 128:(kc + 1) * 128], rhs=stgb[i][:, :], start=(kc == 0), stop=(kc == 7)),
                   ["hT", f"stgb{i}"], [zn[n % 2]])
            DMA(lambda e, n=n: e.dma_start(out=t1[:, 0:512], in_=DV(k.b_ada, l * 3 * D + n * 512, [[0, 128], [1, 512]])), [], ["t1"])
            DVE(lambda e, n=n: e.tensor_tensor(out=mod[:, n * 512:(n + 1) * 512], in0=zz[n % 2][:, :], in1=t1[:, 0:512], op=ALU.add), [zn[n % 2], "t1"], ["mod"])
        DVE(lambda e: e.tensor_scalar_add(out=mod[:, D:2 * D], in0=mod[:, D:2 * D], scalar1=1.0), ["mod"], ["mod"])

    def tile(kind, i):
        kk = kind
        nseg = 8 if kind == 0 else 16
        nj = nseg // 2
        seglen = 128 // nseg
        if kind == 0:
            src = (k.xp if l == 0 else k.y1)[i * 128:(i + 1) * 128, :]
            srcn = [] if l == 0 else ["y1"]
        else:
            src = (k.xs if l == 0 else k.y1s)[:, :]
            srcn = [] if l == 0 else ["y1s"]
        DMA(lambda e: e.dma_start(out=xt[:, :], in_=src), srcn, ["xt"])
        DVE(lambda e: e.tensor_tensor(out=t1[:, :], in0=xt[:, :], in1=mod[:, D:2 * D], op=ALU.mult), ["xt", "mod"], ["t1"])
        DVE(lambda e: e.tensor_tensor(out=hb[:, :], in0=t1[:, :], in1=mod[:, 0:D], op=ALU.add), ["t1", "mod"], ["hb"])
        for kc in range(8):
            PE(lambda e, kc=kc: e.transpose(out=tpb[:, kc * 128:(kc + 1) * 128], in_=hb[:, kc * 128:(kc + 1) * 128], identity=identb[:, :]), ["hb", "identb"], ["tpb"])
        DVE(lambda e: e.tensor_copy(out=hT[:, :], in_=tpb[:, :]), ["tpb"], ["hT"])

        def inproj(n):
            z = zz[n % 2] if not (DBG_MISC and n == 2) else misc
            for kc in range(8):
                PE(lambda e, kc=kc: e.matmul(out=z[:, :], lhsT=hT[:, kc * 128:(kc + 1) * 128], rhs=win[:, kc * DIN + n * 512:kc * DIN + (n + 1) * 512], start=(kc == 0), stop=(kc == 7)),
                   ["hT", "win"] + (DBG_SER_LIST if DBG_SER else []), [zn[n % 2] if not (DBG_MISC and n == 2) else "misc"])
            zname = (zn[n % 2] if not (DBG_MISC and n == 2) else "misc")
            DVE(lambda e: e.tensor_copy(out=qk[:, :], in_=z[:, :]), [zname], ["qk"])
            return qk, "qk"

        ck(f'hT{l}{kind}{i}')
        z, zr = inproj(0)
        ck(f'z0{l}{kind}{i}')
        if kind == 0:
            cs_ = V(cosp, i * 16, [[0, 16], [1, 16]]); sn_ = V(sinp, i * 16, [[0, 16], [1, 16]]); csn = ["cosp", "sinp"]
        else:
            cs_ = V(coss, 0, [[0, 16], [1, 16]]); sn_ = V(sins, 0, [[0, 16], [1, 16]]); csn = ["coss", "sins"]
        x1 = V(qk, 0, [[32, 16], [1, 16]]); x2 = V(qk, 16, [[32, 16], [1, 16]])
        o1 = V(qkr, 0, [[32, 16], [1, 16]]); o2 = V(qkr, 16, [[32, 16], [1, 16]]); tm = V(e1, 0, [[16, 16], [1, 16]])
        DVE(lambda e: e.tensor_tensor(out=o1, in0=x1, in1=cs_, op=ALU.mult), ["qk"] + csn, ["qkr"])
        DVE(lambda e: e.tensor_tensor(out=tm, in0=x2, in1=sn_, op=ALU.mult), ["qk"] + csn, ["e1"])
        DVE(lambda e: e.tensor_tensor(out=o1, in0=o1, in1=tm, op=ALU.subtract), ["qkr", "e1"], ["qkr"])
        DVE(lambda e: e.tensor_tensor(out=o2, in0=x2, in1=cs_, op=ALU.mult), ["qk"] + csn, ["qkr"])
        DVE(lambda e: e.tensor_tensor(out=tm, in0=x1, in1=sn_, op=ALU.mult), ["qk"] + csn, ["e1"])
        DVE(lambda e: e.tensor_tensor(out=o2, in0=o2, in1=tm, op=ALU.add), ["qkr", "e1"], ["qkr"])
        ck(f'rope{l}{kind}{i}')
        if kind == 0:
            DMA(lambda e: e.dma_start(out=k.nk_p[l, i * 128:(i + 1) * 128, :], in_=qkr[:, 256:512]), ["qkr"], [])
        else:
            DMA(lambda e: e.dma_start(out=k.nk_s[l, :, :], in_=qkr[:, 256:512]), ["qkr"], [])
        ck(f'nk{l}{kind}{i}')
        DVE(lambda e: e.tensor_copy(out=qb[:, :], in_=qkr[:, 0:256]), ["qkr"], ["qb"])
        DVE(lambda e: e.tensor_copy(out=kb[:, :], in_=qkr[:, 256:512]), ["qkr"], ["kb"])
        for ch in range(2):
            PE(lambda e, ch=ch: e.transpose(out=tpb[:, ch * 128:(ch + 1) * 128], in_=qb[:, ch * 128:(ch + 1) * 128], identity=identb[:, :]), ["qb", "identb"], ["tpb"])
            PE(lambda e, ch=ch: e.transpose(out=tpb[:, 256 + ch * 128:256 + (ch + 1) * 128], in_=kb[:, ch * 128:(ch + 1) * 128], identity=identb[:, :]), ["kb", "identb"], ["tpb"])
        for ch in range(2):
            for m in range(4):
                DVE(lambda e, ch=ch, m=m: e.tensor_scalar(out=qblk[:, ch * 512 + m * 128:ch * 512 + (m + 1) * 128], in0=tpb[:, ch * 128:(ch + 1) * 128], scalar1=rowm[:, m:m + 1], scalar2=None, op0=ALU.mult),
                    ["tpb", "rowm"], ["qblk"])
        ck(f'qblk{l}{kind}{i}')
        DVE(lambda e: e.tensor_copy(out=kTt[:, :], in_=tpb[:, 256:512]), ["tpb"], ["kTt"])
        ck(f'ktt{l}{kind}{i}')
        if kind == 0:
            DMA(lambda e: e.dma_start(out=k.ktD.ap().rearrange("p (c n) -> p c n", c=2)[:, :, i * 128:(i + 1) * 128], in_=V(kTt, 0, [[128, 2], [1, 128]])), ["kTt"], ["ktD"])
        ck(f'c0{l}{kind}{i}')
        z, zr = inproj(1)
        DVE(lambda e: e.tensor_copy(out=sq[:, :], in_=z[:, 0:256]), [zr], ["sq"])
        DVE(lambda e: e.tensor_copy(out=V(vat, 0, [[65, 4], [1, 64]]), in_=V(z, 0, [[64, 4], [1, 64]])), [zr], ["vat"])
        ACT(lambda e: e.activation(out=gate[:, 0:256], in_=z[:, 256:512], func=AF.Silu), [zr], ["gate"])
        if kind == 0:
            DMA(lambda e: e.dma_start(out=k.nv_p[l, i * 128:(i + 1) * 128, :], in_=sq[:, :]), ["sq"], [])
            DMA(lambda e: e.dma_start(out=k.vD[i * 128:(i + 1) * 128, :], in_=vat[:, :]), ["vat"], ["vD"])
        else:
            DMA(lambda e: e.dma_start(out=k.nv_s[l, :, :], in_=sq[:, :]), ["sq"], [])
        ck(f'c1{l}{kind}{i}')
        z, zr = inproj(2)
        ck(f'z2{l}{kind}{i}')
        ACT(lambda e: e.activation(out=qkr[:, :], in_=z[:, :], func=AF.Erf, scale=0.7071067811865476), [zr], ["qkr"])
        ck(f'erf{l}{kind}{i}')
        DVE(lambda e: e.tensor_scalar(out=qkr[:, :], in0=qkr[:, :], scalar1=1.0, scalar2=0.5, op0=ALU.add, op1=ALU.mult), ["qkr"], ["qkr"])
        DVE(lambda e: e.tensor_tensor(out=e3[:, :], in0=qkr[:, 0:256], in1=z[:, 0:256], op=ALU.mult), ["qkr", zr], ["e3"])
        DVE(lambda e: e.tensor_tensor(out=dq[:, :], in0=qkr[:, 256:512], in1=z[:, 256:512], op=ALU.mult), ["qkr", zr], ["dq"])
        ck(f'gelu{l}{kind}{i}')
        layer_norm(dq, 256, prm[:, 0:256], prm[:, 256:512], "dq", ["prm"], sq, "sq")
        ck(f'ln{l}{kind}{i}')
        if kind == 1:
            DMA(lambda e: e.dma_start(out=k.nch_s[l, :, :], in_=dq[:, :]), ["dq"], [])
        DVE(lambda e: e.tensor_copy(out=vnb[:, :], in_=dq[:, :]), ["dq"], ["vnb"])
        ck(f'c2{l}{kind}{i}')
        z, zr = inproj(3)
        ACT(lambda e: e.activation(out=gate[:, 256:512], in_=z[:, 0:256], func=AF.Silu), [zr], ["gate"])
        DVE(lambda e: e.tensor_copy(out=oat[:, :], in_=z[:, 256:512]), [zr], ["oat"])
        for gi in range(4):
            PE(lambda e, gi=gi: e.matmul(out=misc[:, gi * 64:(gi + 1) * 64], lhsT=wmT[:, kk * 512 + gi * 128:kk * 512 + (gi + 1) * 128], rhs=vnb[:, gi * 64:(gi + 1) * 64], start=True, stop=True),
               ["wmT", "vnb"], ["misc"])
        DVE(lambda e: e.tensor_tensor(out=od[:, :], in0=misc[:, 0:256], in1=bsE[:, kk * 256:(kk + 1) * 256], op=ALU.add), ["misc", "bsE"], ["od"])
        DVE(lambda e: e.tensor_tensor(out=od[:, :], in0=od[:, :], in1=e3[:, :], op=ALU.mult), ["od", "e3"], ["od"])
        DVE(lambda e: e.tensor_tensor(out=cat[:, 256:512], in0=od[:, :], in1=gate[:, 256:512], op=ALU.mult), ["od", "gate"], ["cat"])
        ck(f'c3{l}{kind}{i}')
        z, zr = inproj(4)
        ACT(lambda e: e.activation(out=df[:, :], in_=z[:, 0:256], func=AF.Sigmoid), [zr], ["df"])
        ACT(lambda e: e.activation(out=gate[:, 512:768], in_=z[:, 256:512], func=AF.Silu), [zr], ["gate"])
        DVE(lambda e: e.tensor_tensor(out=df[:, :], in0=df[:, :], in1=oat[:, :], op=ALU.mult), ["df", "oat"], ["df"])
        if kind == 0:
            if i == NT - 1:
                DMA(lambda e: e.dma_start(out=k.ncv_p[l, :, :], in_=df[98:128, :]), ["df"], [])
        else:
            for s_ in range(16):
                DMA(lambda e, s_=s_: e.dma_start(out=k.ncv_s[l, s_, 22:30, :], in_=df[8 * s_:8 * s_ + 8, :]), ["df"], [])
            DMA(lambda e: e.dma_start(out=k.ncv_s[l, :, 0:22, :], in_=k.st_conv[l, :, 8:30, :]), [], [])
        for ch in range(2):
            PE(lambda e, ch=ch: e.transpose(out=sc[0][:, ch * 128:(ch + 1) * 128], in_=df[:, ch * 128:(ch + 1) * 128], identity=identf[:, :]), ["df", "identf"], ["sc0"])
        if kind == 0:
            if i == 0:
                DVE(lambda e: e.memset(hcT[:, :], 0.0), [], ["hcT"])
            else:
                DVE(lambda e: e.tensor_copy(out=V(e1, 0, [[30, 2], [1, 30]]), in_=V(hcT, 128, [[158, 2], [1, 30]])), ["hcT"], ["e1"])
                DVE(lambda e: e.tensor_copy(out=V(hcT, 0, [[158, 2], [1, 30]]), in_=V(e1, 0, [[30, 2], [1, 30]])), ["e1"], ["hcT"])
            DVE(lambda e: e.tensor_copy(out=V(hcT, 30, [[158, 2], [1, 128]]), in_=V(sc[0], 0, [[128, 2], [1, 128]])), ["sc0"], ["hcT"])
        else:
            DVE(lambda e: e.tensor_copy(out=V(hcT, 30, [[608, 2], [38, 16], [1, 8]]), in_=V(sc[0], 0, [[128, 2], [8, 16], [1, 8]])), ["sc0"], ["hcT"])
            for q4 in range(4):
                DMA(lambda e, q4=q4: e.dma_start(out=kpg[0:120, :], in_=k.st_conv[l, 4 * q4:4 * q4 + 4, :, :].rearrange("s r c -> (s r) c")), [], ["kpg"])
                for ch in range(2):
                    PE(lambda e, ch=ch: e.transpose(out=sc[1][:, ch * 128:ch * 128 + 120], in_=kpg[0:120, ch * 128:(ch + 1) * 128], identity=identf[0:120, 0:120]), ["kpg", "identf"], ["sc1"])
                DVE(lambda e, q4=q4: e.tensor_copy(out=V(hcT, q4 * 4 * 38, [[608, 2], [38, 4], [1, 30]]), in_=V(sc[1], 0, [[128, 2], [30, 4], [1, 30]])), ["sc1"], ["hcT"])
        for j in range(31):
            for ch in range(2):
                cname = f"qk{ch}"
                if kind == 0:
                    win_ = V(hcT, ch * 158 + j, [[1, 128]]); oap = V(qk, ch * 128, [[1, 128]])
                else:
                    win_ = V(hcT, ch * 608 + j, [[38, 16], [1, 8]]); oap = V(qk, ch * 128, [[8, 16], [1, 8]])
                if j == 0:
                    DVE(lambda e, win_=win_, oap=oap, ch=ch, j=j: e.tensor_scalar(out=oap, in0=win_, scalar1=cwT[:, ch * 31 + j:ch * 31 + j + 1], scalar2=None, op0=ALU.mult), ["hcT", "cwT"], [cname])
                else:
                    DVE(lambda e, win_=win_, oap=oap, ch=ch, j=j: e.scalar_tensor_tensor(out=oap, in0=win_, scalar=cwT[:, ch * 31 + j:ch * 31 + j + 1], in1=oap, op0=ALU.mult, op1=ALU.add), ["hcT", "cwT", cname], [cname])
        for ch in range(2):
            PE(lambda e, ch=ch: e.transpose(out=sc[1][:, ch * 128:(ch + 1) * 128], in_=qk[:, ch * 128:(ch + 1) * 128], identity=identf[:, :]), [f"qk{ch}", "identf"], ["sc1"])
        DVE(lambda e: e.tensor_tensor(out=e2[:, :], in0=sc[1][:, 0:256], in1=prm[:, 512:768], op=ALU.add), ["sc1", "prm"], ["e2"])
        layer_norm(e2, 256, prm[:, 768:1024], prm[:, 1024:1280], "e2", ["prm"], sq, "sq")
        ACT(lambda e: e.activation(out=cyb[:, :], in_=e2[:, :], func=AF.Silu), ["e2"], ["cyb"])
        for ch in range(2):
            PE(lambda e, ch=ch: e.transpose(out=tpb[:, ch * 128:(ch + 1) * 128], in_=cyb[:, ch * 128:(ch + 1) * 128], identity=identb[:, :]), ["cyb", "identb"], ["tpb"])
        DVE(lambda e: e.tensor_copy(out=cyT[:, :], in_=tpb[:, 0:256]), ["tpb"], ["cyT"])
        for ch in range(2):
            PE(lambda e, ch=ch: e.matmul(out=misc[:, 0:256], lhsT=cyT[:, ch * 128:(ch + 1) * 128], rhs=wpw[:, ch * 256:(ch + 1) * 256], start=(ch == 0), stop=(ch == 1)), ["cyT", "wpw"], ["misc"])
        DVE(lambda e: e.tensor_tensor(out=cat[:, 512:768], in0=misc[:, 0:256], in1=gate[:, 512:768], op=ALU.mult), ["misc", "gate"], ["cat"])
        ck(f'c4{l}{kind}{i}')
        z, zr = inproj(5)
        ACT(lambda e: e.activation(out=dq[:, :], in_=z[:, 0:256], func=AF.Silu), [zr], ["dq"])
        ACT(lambda e: e.activation(out=df[:, :], in_=z[:, 256:512], func=AF.Sigmoid), [zr], ["df"])
        DVE(lambda e: e.tensor_tensor(out=df[:, :], in0=df[:, :], in1=prm[:, 6 * 256:7 * 256], op=ALU.mult), ["df", "prm"], ["df"])
        DVE(lambda e: e.tensor_tensor(out=df[:, :], in0=df[:, :], in1=prm[:, 5 * 256:6 * 256], op=ALU.add), ["df", "prm"], ["df"])
        DVE(lambda e: e.tensor_scalar(out=kd[:, :], in0=df[:, :], scalar1=-1.0, scalar2=1.0, op0=ALU.mult, op1=ALU.add), ["df"], ["kd"])
        DVE(lambda e: e.tensor_scalar_max(out=lf[:, :], in0=df[:, :], scalar1=1e-30), ["df"], ["lf"])
        ACT(lambda e: e.activation(out=lf[:, :], in_=lf[:, :], func=AF.Ln), ["lf"], ["lf"])
        z, zr = inproj(6)
        DVE(lambda e: e.tensor_copy(out=vdb[:, :], in_=z[:, 0:256]), [zr], ["vdb"])
        ACT(lambda e: e.activation(out=gate[:, 768:1024], in_=z[:, 256:512], func=AF.Silu), [zr], ["gate"])
        ck(f'c6{l}{kind}{i}')
        PE(lambda e: e.matmul(out=sc[0][:, 0:256], lhsT=mcum[:, kk * 128:(kk + 1) * 128], rhs=lf[:, :], start=True, stop=True), ["mcum", "lf"], ["sc0"])
        PE(lambda e: e.matmul(out=sc[0][:, 256:512], lhsT=mlast[:, kk * 128:(kk + 1) * 128], rhs=lf[:, :], start=True, stop=True), ["mlast", "lf"], ["sc0"])
        DVE(lambda e: e.tensor_copy(out=cl[:, 0:512], in_=sc[0][:, 0:512]), ["sc0"], ["cl"])
        ACT(lambda e: e.activation(out=e1[:, :], in_=cl[:, 0:256], func=AF.Exp), ["cl"], ["e1"])
        ACT(lambda e: e.activation(out=e2[:, :], in_=cl[:, 0:256], func=AF.Exp, scale=-1.0), ["cl"], ["e2"])
        DVE(lambda e: e.tensor_tensor(out=cl[:, 0:256], in0=cl[:, 256:512], in1=cl[:, 0:256], op=ALU.subtract), ["cl"], ["cl"])
        ACT(lambda e: e.activation(out=e3[:, :], in_=cl[:, 0:256], func=AF.Exp), ["cl"], ["e3"])
        ACT(lambda e: e.activation(out=V(eldup, 0, [[128, 4], [64, 2], [1, 64]]), in_=V(cl, 256, [[64, 4], [0, 2], [1, 64]]), func=AF.Exp), ["cl"], ["eldup"])
        dup_o = lambda t_: V(t_, 0, [[128, 4], [64, 2], [1, 64]])
        dup_i = lambda t_: V(t_, 0, [[64, 4], [0, 2], [1, 64]])
        DVE(lambda e: e.tensor_tensor(out=dup_o(qdup), in0=dup_i(dq), in1=dup_i(e1), op=ALU.mult), ["dq", "e1"], ["qdup"])
        DVE(lambda e: e.tensor_tensor(out=kt_[:, :], in0=kd[:, :], in1=e2[:, :], op=ALU.mult), ["kd", "e2"], ["kt_"])
        DVE(lambda e: e.tensor_tensor(out=dup_o(kpdup), in0=dup_i(kd), in1=dup_i(e3), op=ALU.mult), ["kd", "e3"], ["kpdup"])
        for h in range(4):
            PE(lambda e, h=h: e.transpose(out=tpb[:, h * 128:(h + 1) * 128], in_=qdup[:, h * 128:(h + 1) * 128], identity=identb[:, :]), ["qdup", "identb"], ["tpb"])
            PE(lambda e, h=h: e.transpose(out=tpb[0:64, 512 + h * 128:512 + (h + 1) * 128], in_=kt_[:, h * 64:(h + 1) * 64], identity=identb[:, :]), ["kt_", "identb"], ["tpb"])
            PE(lambda e, h=h: e.transpose(out=sc[1][:, h * 128:(h + 1) * 128], in_=eldup[:, h * 128:(h + 1) * 128], identity=identf[:, :]), ["eldup", "identf"], ["sc1"])
        DVE(lambda e: e.tensor_copy(out=q2T[:, :], in_=tpb[:, 0:512]), ["tpb"], ["q2T"])
        DVE(lambda e: e.tensor_copy(out=khT[0:64, :], in_=tpb[0:64, 512:1024]), ["tpb"], ["khT"])
        DVE(lambda e: e.tensor_copy(out=a2T[:, :], in_=sc[1][:, :]), ["sc1"], ["a2T"])
        for h in range(4):
            PE(lambda e, h=h: e.matmul(out=misc[:, h * 128:(h + 1) * 128], lhsT=khT[0:64, h * 128:(h + 1) * 128], rhs=q2T[0:64, h * 128:(h + 1) * 128], start=True, stop=True), ["khT", "q2T"], ["misc"])
        DVE(lambda e: e.tensor_tensor(out=attm[:, :], in0=misc[:, :], in1=mattb[:, kk * 512:(kk + 1) * 512], op=ALU.mult), ["misc", "mattb"], ["attm"])
        for h in range(4):
            if kind == 1:
                for du in range(2):
                    DMA(lambda e, h=h, du=du: e.dma_start(out=V(sstf, 0, [[64, 8], [1, 64]], p0=du * 64, npart=64),
                                                          in_=DV(k.st_hgrn, (l * 16 + du) * 16384 + h * 4096, [[64, 64], [2 * 16384, 8], [1, 64]])), [], ["sstf"])
                ACT(lambda e: e.copy(out=sst[:, :], in_=sstf[:, :]), ["sstf"], ["sst"])
            DVE(lambda e, h=h: e.tensor_tensor(out=V(vexp, 0, [[64, nseg], [1, 64]]), in0=V(vdb, h * 64, [[0, nseg], [1, 64]]), in1=V(segb, kk * 1024, [[64, nseg], [1, 64]]), op=ALU.mult), ["vdb", "segb"], ["vexp"])
            PE(lambda e, h=h: e.matmul(out=ot[0][:, :], lhsT=kpdup[:, h * 128:(h + 1) * 128], rhs=vexp[:, 0:512], start=True, stop=True), ["kpdup", "vexp"], ["ot0"])
            if kind == 1:
                PE(lambda e, h=h: e.matmul(out=ot[1][:, :], lhsT=kpdup[:, h * 128:(h + 1) * 128], rhs=vexp[:, 512:1024], start=True, stop=True), ["kpdup", "vexp"], ["ot1"])
            for sg in range(nseg):
                hf = sg % 2; j = sg // 2
                acol = h * 128 + seglen * sg
                bsrc = ot[sg // 8]; bname = f"ot{sg // 8}"
                if kind == 0:
                    (ssrc, sname_), (sdst, dname_) = ((k.Sf, "Sf"), (k.Sf2, "Sf2")) if sg % 2 == 0 else ((k.Sf2, "Sf2"), (k.Sf, "Sf"))
                    ACT(lambda e, h=h, hf=hf, j=j, ssrc=ssrc: e.copy(out=V(sst, j * 64, [[1, 64]], p0=hf * 64, npart=64), in_=V(ssrc, h * 64, [[1, 64]], p0=hf * 64, npart=64)), [sname_], ["sst"])
                    DVE(lambda e, h=h, sg=sg, acol=acol, bsrc=bsrc, ssrc=ssrc, sdst=sdst: e.scalar_tensor_tensor(out=sdst[:, h * 64:(h + 1) * 64], in0=ssrc[:, h * 64:(h + 1) * 64], scalar=a2T[:, acol:acol + 1], in1=bsrc[:, (sg % 8) * 64:(sg % 8 + 1) * 64], op0=ALU.mult, op1=ALU.add),
                        [sname_, "a2T", bname], [dname_])
                else:
                    DVE(lambda e, h=h, sg=sg, hf=hf, j=j, acol=acol, bsrc=bsrc: e.scalar_tensor_tensor(
                        out=V(sstf, j * 64, [[1, 64]], p0=hf * 64, npart=64), in0=V(sstf, j * 64, [[1, 64]], p0=hf * 64, npart=64),
                        scalar=V(a2T, acol, [[1, 1]], p0=hf * 64, npart=64), in1=V(bsrc, (sg % 8) * 64, [[1, 64]], p0=hf * 64, npart=64), op0=ALU.mult, op1=ALU.add),
                        ["sstf", "a2T", bname], ["sstf"])
            PE(lambda e, h=h: e.matmul(out=zA[:, h * 64:(h + 1) * 64], lhsT=attm[:, h * 128:(h + 1) * 128], rhs=vdb[:, h * 64:(h + 1) * 64], start=True, stop=False), ["attm", "vdb"], ["zA"])
            DVE(lambda e, h=h: e.tensor_tensor(out=V(qexp, 0, [[128, nj], [1, 128]]), in0=V(q2T, h * 128, [[0, nj], [1, 128]]), in1=V(cmb, kk * 1024, [[128, nj], [1, 128]]), op=ALU.mult), ["q2T", "cmb"], ["qexp"])
            for j in range(nj):
                PE(lambda e, h=h, j=j: e.matmul(out=zA[:, h * 64:(h + 1) * 64], lhsT=qexp[:, j * 128:(j + 1) * 128], rhs=sst[:, j * 64:(j + 1) * 64], start=False, stop=(j == nj - 1)), ["qexp", "sst"], ["zA"])
            if kind == 1:
                for du in range(2):
                    DMA(lambda e, h=h, du=du: e.dma_start(out=DV(k.nh_s, (l * 16 + du) * 16384 + h * 4096, [[64, 64], [2 * 16384, 8], [1, 64]]),
                                                          in_=V(sstf, 0, [[64, 8], [1, 64]], p0=du * 64, npart=64)), ["sstf"], [])
        if kind == 0 and i == NT - 1:
            DMA(lambda e: e.dma_start(out=DV(k.nh_p, l * 16384, [[64, 64], [4096, 4], [1, 64]]), in_=V(Sf, 0, [[64, 4], [1, 64]], npart=64)), ["Sf"], [])
        DVE(lambda e: e.tensor_copy(out=od[:, :], in_=zA[:, 0:256]), ["zA"], ["od"])
        rms_heads(od, "od", 8, 768, 768)

        ck(f'hgrn{l}{kind}{i}')
        pending = []

        def kblock(ktile, kname, vtile, vname, ncol, qcol, first, last, mask, mname, s_=None):
            for ch in range(2):
                if s_ is None:
                    rhs = qblk[:, ch * 512:(ch + 1) * 512]
                else:
                    rhs = V(qblk, ch * 512 + 8 * s_, [[128, 4], [1, 8]])
                PE(lambda e, ch=ch, rhs=rhs: e.matmul(out=sc[ch][:, 0:4 * ncol], lhsT=ktile[:, ch * 128:(ch + 1) * 128], rhs=rhs, start=True, stop=True), [kname, "qblk"], [f"sc{ch}"])
                ptb, pname = (pt_, "pt_") if ch == 0 else (qexp, "qexp")
                ACT(lambda e, ch=ch, ptb=ptb: e.activation(out=ptb[:, 0:4 * ncol], in_=sc[ch][:, 0:4 * ncol], func=AF.Exp), [f"sc{ch}"], [pname])
                if mask is not None:
                    DVE(lambda e, ptb=ptb: e.tensor_tensor(out=V(ptb, 0, [[128, 4], [1, 128]]), in0=V(ptb, 0, [[128, 4], [1, 128]]), in1=V(mask, 0, [[0, 4], [1, 128]]), op=ALU.mult), [pname, mname], [pname])

                def pv(ch=ch, ptb=ptb, pname=pname):
                    for m in range(4):
                        h = 2 * ch + m // 2
                        PE(lambda e, m=m, h=h: e.matmul(out=V(ot[ch], m * 128 + qcol, [[1, ncol]], npart=65), lhsT=vtile[:, h * 65:(h + 1) * 65], rhs=ptb[:, m * ncol:(m + 1) * ncol],
                                                        start=(first and m == 0), stop=last, skip_group_check=True), [vname, pname], [f"ot{ch}"])
                if pending:
                    pending.pop(0)()
                pending.append(pv)

        if kind == 0:
            kvsets = [(ktB[j][:, :], f"ktB{j}", vB[j][:, :], f"vB{j}") for j in range(3)]
            for kbi in range(i):
                kt_ap, ktn, v_ap, vnm = kvsets[kbi % 3]
                DMA(lambda e, kbi=kbi, kt_ap=kt_ap: e.dma_start(out=kt_ap.rearrange("p (c n) -> p c n", c=2), in_=k.ktD.ap().rearrange("p (c n) -> p c n", c=2)[:, :, kbi * 128:(kbi + 1) * 128]), ["ktD"], [ktn])
                DMA(lambda e, kbi=kbi, v_ap=v_ap: e.dma_start(out=v_ap, in_=k.vD[kbi * 128:(kbi + 1) * 128, :]), ["vD"], [vnm], q="pool")
                kblock(kt_ap, ktn, v_ap, vnm, 128, 0, kbi == 0, False, None, None)
            kblock(kTt, "kTt", vat, "vat", 128, 0, i == 0, True, trilb, "trilb")
        else:
            kblock(kTt, "kTt", vat, "vat", 128, 0, True, False, msb, "msb")
            for s_ in range(16):
                for j in range(16):
                    col = s_ * 16 + j
                    while pending:
                        pending.pop(0)()
                    DMA(lambda e, col=col: e.indirect_dma_start(out=kpg[:, :], out_offset=None, in_=k.cache_kl[l][:, :], in_offset=bass.IndirectOffsetOnAxis(ap=idx[:, col:col + 1], axis=0)), ["idx"], ["kpg"], q="pool")
                    DMA(lambda e, col=col: e.indirect_dma_start(out=vpg[:, :], out_offset=None, in_=k.cache_vl[l][:, :], in_offset=bass.IndirectOffsetOnAxis(ap=idx[:, col:col + 1], axis=0)), ["idx"], ["vpg"], q="pool")
                    DVE(lambda e: e.tensor_copy(out=kpb[:, :], in_=kpg[:, :]), ["kpg"], ["kpb"])
                    DVE(lambda e: e.tensor_copy(out=V(vpa, 0, [[65, 4], [1, 64]]), in_=V(vpg, 0, [[64, 4], [1, 64]])), ["vpg"], ["vpa"])
                    for ch in range(2):
                        PE(lambda e, ch=ch: e.transpose(out=tpb[:, ch * 128:(ch + 1) * 128], in_=kpb[:, ch * 128:(ch + 1) * 128], identity=identb[:, :]), ["kpb", "identb"], ["tpb"])
                    DVE(lambda e: e.tensor_copy(out=kpT[:, :], in_=tpb[:, 0:256]), ["tpb"], ["kpT"])
                    kblock(kpT, "kpT", vpa, "vpa", 8, 8 * s_, False, j == 15, None, None, s_=s_)
        while pending:
            pending.pop(0)()
        ck(f'attn{l}{kind}{i}')
        for ch in range(2):
            DVE(lambda e, ch=ch: e.tensor_copy(out=t1[0:65, ch * 512:(ch + 1) * 512], in_=ot[ch][0:65, :]), [f"ot{ch}"], ["t1"])
            for m in range(4):
                PE(lambda e, ch=ch, m=m: e.transpose(out=sc[ch][:, m * 65:(m + 1) * 65], in_=t1[0:65, ch * 512 + m * 128:ch * 512 + (m + 1) * 128], identity=identf[0:65, 0:65]), ["t1", "identf"], [f"sc{ch}"])
            DVE(lambda e, ch=ch: e.tensor_copy(out=oa[:, ch * 260:(ch + 1) * 260], in_=sc[ch][:, 0:260]), [f"sc{ch}"], ["oa"])
        DVE(lambda e: e.reciprocal(out=rl[:, 0:8], in_=V(oa, 64, [[65, 8]])), ["oa"], ["rl"])
        for h in range(4):
            b0 = h * 130
            DVE(lambda e, h=h, b0=b0: e.tensor_scalar(out=sq[:, 0:64], in0=oa[:, b0 + 65:b0 + 129], scalar1=rl[:, 2 * h + 1:2 * h + 2], scalar2=lamt[:, 0:1], op0=ALU.mult, op1=ALU.mult), ["oa", "rl", "lamt"], ["sq"])
            DVE(lambda e, h=h, b0=b0: e.scalar_tensor_tensor(out=oat[:, h * 64:(h + 1) * 64], in0=oa[:, b0:b0 + 64], scalar=rl[:, 2 * h:2 * h + 1], in1=sq[:, 0:64], op0=ALU.mult, op1=ALU.add), ["oa", "rl", "sq"], ["oat"])
        rms_heads(oat, "oat", 7, 0, 0)
        ck(f'epi{l}{kind}{i}')
        if kind == 0 and DBG_CAT:
            DMA(lambda e: e.dma_start(out=k.dbg_cat[l, i * 128:(i + 1) * 128, :], in_=cat[:, :]), ["cat"], [])
        for kc in range(8):
            PE(lambda e, kc=kc: e.transpose(out=tpb[:, kc * 128:(kc + 1) * 128], in_=cat[:, kc * 128:(kc + 1) * 128], identity=identb[:, :]), ["cat", "identb"], ["tpb"])
        DVE(lambda e: e.tensor_copy(out=catT[:, :], in_=tpb[:, :]), ["tpb"], ["catT"])
        for n in range(2):
            for kc in range(8):
                PE(lambda e, kc=kc, n=n: e.matmul(out=zz[n][:, :], lhsT=catT[:, kc * 128:(kc + 1) * 128], rhs=wout[:, kc * D + n * 512:kc * D + (n + 1) * 512], start=(kc == 0), stop=(kc == 7)), ["catT", "wout"], [zn[n]])
            DVE(lambda e, n=n: e.tensor_tensor(out=t1[:, n * 512:(n + 1) * 512], in0=zz[n][:, :], in1=mod[:, 2 * D + n * 512:2 * D + (n + 1) * 512], op=ALU.mult), [zn[n], "mod"], ["t1"])
        DVE(lambda e: e.scalar_tensor_tensor(out=t1[:, :], in0=xt[:, :], scalar=ALPHA, in1=t1[:, :], op0=ALU.mult, op1=ALU.add), ["xt", "t1"], ["t1"])
        layer_norm(t1, D, lnp[:, 0:D], lnp[:, D:2 * D], "t1", ["lnp"], xt, "xt")
        if kind == 0:
            dst = (k.y1 if l == 0 else k.y_p)[i * 128:(i + 1) * 128, :]
            DMA(lambda e: e.dma_start(out=dst, in_=t1[:, :]), ["t1"], ["y1"] if l == 0 else [])
        else:
            dst = (k.y1s if l == 0 else k.y_s)[:, :]
            DMA(lambda e: e.dma_start(out=dst, in_=t1[:, :]), ["t1"], ["y1s"] if l == 0 else [])

    compute_mod(k.cp)
    ck(f'mod{l}')
    for i in range(k.NTILES):
        tile(0, i)
    compute_mod(k.cs)
    tile(1, 0)


def _consts():
    c = {}
    a = np.arange(128)
    s_, t_ = a[:, None], a[None, :]
    c["c_ident"] = np.eye(128, dtype=np.float32)
    c["c_tril"] = (s_ <= t_).astype(np.float32)
    c["c_ms"] = ((s_ // 8 == t_ // 8) & (s_ <= t_)).astype(np.float32)
    mc, ml, cm, sg = [], [], [], []
    for L in (16, 8):
        same = (s_ // L == t_ // L)
        mc.append((same & (s_ <= t_)).astype(np.float32))
        ml.append(same.astype(np.float32))
        nseg = 128 // L
        m = np.zeros((128, 8, 128), np.float32)
        for pp in range(128):
            for j in range(nseg // 2):
                seg = 2 * j + pp // 64
                m[pp, j, seg * L:(seg + 1) * L] = 1.0
        cm.append(m.reshape(128, 1024))
        g = np.zeros((128, 16, 64), np.float32)
        for seg in range(nseg):
            g[seg * L:(seg + 1) * L, seg, :] = 1.0
        sg.append(g.reshape(128, 1024))
    c["c_mcum"] = np.stack(mc); c["c_mlast"] = np.stack(ml); c["c_matt"] = np.stack(mc)
    c["c_cm"] = np.stack(cm); c["c_seg"] = np.stack(sg)
    rm = np.zeros((128, 4), np.float32)
    for m in range(4):
        rm[32 * m:32 * m + 32, m] = SCALE
    c["c_rowm"] = rm
    c["pos_p"] = (128.0 * np.arange(NT)[None, :] + a[:, None]).astype(np.float32)
    c["pos_s"] = (2048.0 + (a % 8)).astype(np.float32).reshape(128, 1)
    inv = (np.float32(10000.0) ** (-np.arange(16, dtype=np.float32) * np.float32(2.0) / np.float32(32.0))).astype(np.float32)
    c["invf"] = np.tile(inv[None, :], (128, 1)).astype(np.float32)
    c["iota_p"] = a.astype(np.float32).reshape(128, 1)
    return c


_PROG = None


def kernel(x_prompt, x_sample, cache_k, cache_v, state_conv, state_hgrn, page_table, c_prompt, c_sample,
           w_ada, b_ada, w_in, lam_qk, attn_norm_g, sg_norm_g, sg_norm_b, w_s, b_s, conv_w, conv_b,
           conv_norm_g, conv_norm_b, w_pw, lower_bounds, hgrn_norm_g, w_out, ln_g, ln_b):
    global _PROG
    if _PROG is None:
        _PROG = build_program()
    k = _PROG
    f = lambda a: np.ascontiguousarray(np.asarray(a, dtype=np.float32))
    consts = _consts()
    ck = f(cache_k).reshape(2, NPOOL_ROWS, 256)[:, :POOL_ROWS_RUN]
    cv = f(cache_v).reshape(2, NPOOL_ROWS, 256)[:, :POOL_ROWS_RUN]
    shared = dict(cache_k0=ck[0], cache_k1=ck[1], cache_v0=cv[0], cache_v1=cv[1], w_ada=f(w_ada), b_ada=f(b_ada), w_in=f(w_in), lam_qk=f(lam_qk).reshape(2, 128),
                  attn_g=f(attn_norm_g), sg_g=f(sg_norm_g), sg_b=f(sg_norm_b), w_s=f(w_s), b_s=f(b_s), conv_w=f(conv_w),
                  conv_b=f(conv_b), cn_g=f(conv_norm_g), cn_b=f(conv_norm_b), w_pw=f(w_pw), lowb=f(lower_bounds),
                  hg_g=f(hgrn_norm_g), w_out=f(w_out), ln_g=f(ln_g), ln_b=f(ln_b))
    shared.update(consts)
    xp_, xs_ = f(x_prompt), f(x_sample)
    cp_, cs_ = f(c_prompt), f(c_sample)
    pt = np.asarray(page_table).astype(np.int32)
    sc_, sh_ = f(state_conv), f(state_hgrn)
    in_maps = []
    for r in range(NCORES_RUN):
        b = r % 2
        m = dict(shared)
        m["xp"] = xp_[b]
        m["xs"] = np.ascontiguousarray(xs_[16 * r:16 * r + 16].reshape(128, D))
        m["cp"] = np.ascontiguousarray(np.tile(cp_[b][None, :], (128, 1)))
        m["cs"] = np.ascontiguousarray(np.repeat(cs_[16 * r:16 * r + 16], 8, axis=0))
        m["ptab"] = np.ascontiguousarray(np.tile(pt[16 * r:16 * r + 16].reshape(1, 256), (128, 1)))
        m["st_conv"] = np.ascontiguousarray(sc_[:, 16 * r:16 * r + 16])
        m["st_hgrn"] = np.ascontiguousarray(sh_[:, 16 * r:16 * r + 16])
        in_maps.append(m)
    res = run_bass_kernel_spmd(k.nc, in_maps, core_ids=list(range(NCORES_RUN))).results
    res = list(res) + [res[0]] * (8 - len(res))
    global DBG_RES
    DBG_RES = res
    y_prompt = np.stack([res[b]["y_p"] for b in range(2)])
    y_sample = np.concatenate([res[r]["y_s"].reshape(16, 8, D) for r in range(8)], 0)
    nk_p = np.stack([res[b]["nk_p"] for b in range(2)], 1).reshape(2, 2, 8192, 4, 64)
    nv_p = np.stack([res[b]["nv_p"] for b in range(2)], 1).reshape(2, 2, 8192, 4, 64)
    nk_s = np.concatenate([res[r]["nk_s"].reshape(2, 16, 8, 4, 64) for r in range(8)], 1)
    nv_s = np.concatenate([res[r]["nv_s"].reshape(2, 16, 8, 4, 64) for r in range(8)], 1)
    nch = np.concatenate([res[r]["nch_s"].reshape(2, 16, 8, 256) for r in range(8)], 1)
    ncv_p = np.stack([res[b]["ncv_p"] for b in range(2)], 1)
    ncv_s = np.concatenate([res[r]["ncv_s"] for r in range(8)], 1)
    nh_p = np.stack([res[b]["nh_p"] for b in range(2)], 1)
    nh_s = np.concatenate([res[r]["nh_s"] for r in range(8)], 1)
    outs = (y_prompt, y_sample, nk_p, nv_p, nk_s, nv_s, nch, ncv_p, ncv_s, nh_p, nh_s)
    return tuple(np.ascontiguousarray(o, dtype=np.float32) for o in outs)
```
